# Optimizing a Trainium2 kernel written in Bass

```python
import jax, jax.numpy as jnp
from jax import lax
import numpy as np

D_MODEL = 1024
BATCH = 16
SEQ = 2048
DEPTH = 2

CHUNK = 64
Q_BLOCK = 128
NORM_EPS = 1e-6
MASK_VALUE = -1e30
TINY = 1e-30

MIX_WIDTH = D_MODEL
HGRN_HEADS = 4
HGRN_WIDTH = MIX_WIDTH // 4
HGRN_KEY_DIM = HGRN_WIDTH // HGRN_HEADS
HGRN_VAL_DIM = HGRN_WIDTH // HGRN_HEADS
POOL_WINDOWS = (2, 4, 8, 16)
POOL_GROUPS = len(POOL_WINDOWS)
POOL_WIDTH = MIX_WIDTH // 4
POOL_GROUP_DIM = POOL_WIDTH // POOL_GROUPS
FOX_HEADS = 8
FOX_WIDTH = MIX_WIDTH // 2
FOX_HEAD_DIM = FOX_WIDTH // FOX_HEADS

IN_WIDTHS = (HGRN_WIDTH, HGRN_WIDTH, HGRN_WIDTH, HGRN_WIDTH,
             POOL_WIDTH, POOL_WIDTH,
             FOX_WIDTH, FOX_WIDTH, FOX_WIDTH, FOX_WIDTH, FOX_HEADS)
IN_WIDTH = int(sum(IN_WIDTHS))
IN_SPLIT_POINTS = tuple(int(v) for v in np.cumsum(IN_WIDTHS)[:-1])

kernel_name = "hybrid_hgrn2_pool_fox_stream_encoder"


def _rmsnorm(x, g):
    xf = x.astype(jnp.float32)
    y = xf * lax.rsqrt(jnp.mean(xf * xf, axis=-1, keepdims=True) + NORM_EPS)
    return (y * g.astype(jnp.float32)).astype(x.dtype)


def _hgrn2_chunkwise(q, k, v, log_f):
    B, T, H, dk = q.shape
    dv = v.shape[-1]
    n = T // CHUNK

    def to_chunks(a):
        return a.astype(jnp.float32).reshape(B, n, CHUNK, H, a.shape[-1]).transpose(1, 0, 3, 2, 4)

    qc, kc, vc, gc = to_chunks(q), to_chunks(k), to_chunks(v), to_chunks(log_f)
    pos = jnp.arange(CHUNK)
    causal = (pos[:, None] >= pos[None, :])[:, :, None]
    causal_f = causal.astype(jnp.float32)

    def step(S, inp):
        qi, ki, vi, gi = inp
        b = jnp.cumsum(gi, axis=2)
        diff = b[:, :, :, None, :] - b[:, :, None, :, :]
        decay = jnp.exp(jnp.where(causal, diff, 0.0)) * causal_f
        attn = jnp.einsum('bhtd,bhsd,bhtsd->bhts', qi, ki, decay)
        o = (jnp.einsum('bhts,bhsv->bhtv', attn, vi)
             + jnp.einsum('bhtd,bhdv->bhtv', qi * jnp.exp(b), S))
        b_last = b[:, :, -1, :]
        S = (jnp.exp(b_last)[..., None] * S
             + jnp.einsum('bhsd,bhsv->bhdv', ki * jnp.exp(b_last[:, :, None, :] - b), vi))
        return S, o

    S0 = jnp.zeros((B, H, dk, dv), jnp.float32)
    _, o = lax.scan(step, S0, (qc, kc, vc, gc))
    return o.transpose(1, 0, 3, 2, 4).reshape(B, T, H, dv)


def _multiscale_pool(u):
    B, T, _ = u.shape
    uf = u.astype(jnp.float32).reshape(B, T, POOL_GROUPS, POOL_GROUP_DIM)
    cs = jnp.pad(jnp.cumsum(uf, axis=1), ((0, 0), (1, 0), (0, 0), (0, 0)))
    t = jnp.arange(T, dtype=jnp.float32)
    outs = []
    for g, w in enumerate(POOL_WINDOWS):
        c = cs[:, :, g]
        upper = c[:, 1:]
        lower = jnp.pad(c[:, :T + 1 - w], ((0, 0), (w - 1, 0), (0, 0)))
        count = jnp.minimum(t + 1.0, float(w))[None, :, None]
        outs.append((upper - lower) / count - uf[:, :, g])
    return jnp.stack(outs, axis=2)


def _forgetting_attention(q, k, v, log_f):
    B, T, H, D = q.shape
    c = jnp.cumsum(log_f, axis=1).transpose(0, 2, 1)
    scale = D ** -0.5
    outs = []
    for i in range(T // Q_BLOCK):
        q0, q1 = i * Q_BLOCK, (i + 1) * Q_BLOCK
        s = jnp.einsum('bqhd,bkhd->bhqk', q[:, q0:q1], k[:, :q1]).astype(jnp.float32) * scale
        mask = (q0 + jnp.arange(Q_BLOCK))[:, None] >= jnp.arange(q1)[None, :]
        bias = jnp.where(mask, c[:, :, q0:q1, None] - c[:, :, None, :q1], 0.0)
        p = jax.nn.softmax(jnp.where(mask, s + bias, MASK_VALUE), axis=-1)
        outs.append(jnp.einsum('bhqk,bkhd->bqhd', p.astype(v.dtype), v[:, :q1]))
    return jnp.concatenate(outs, axis=1)


def setup_inputs(seed: int = 0) -> dict:
    key = jax.random.key(seed)
    ks = jax.random.split(key, 11)
    f32 = jnp.float32
    x = jax.random.normal(ks[0], (BATCH, SEQ, D_MODEL), f32)
    lower_bounds = jax.random.normal(ks[1], (DEPTH, HGRN_WIDTH), f32)
    pre_norm_g = 1.0 + 0.05 * jax.random.normal(ks[2], (DEPTH, D_MODEL), f32)
    w_in = jax.random.normal(ks[3], (DEPTH, D_MODEL, IN_WIDTH), f32) * D_MODEL ** -0.5
    hgrn_norm_g = 1.0 + 0.05 * jax.random.normal(ks[4], (DEPTH, HGRN_WIDTH), f32)
    fox_f_bias = jax.random.uniform(ks[5], (DEPTH, FOX_HEADS), f32, minval=1.0, maxval=4.0)
    pool_w = jax.random.normal(ks[6], (DEPTH, POOL_GROUPS, POOL_GROUP_DIM, POOL_GROUP_DIM), f32) * POOL_GROUP_DIM ** -0.5
    pool_scale = jax.random.uniform(ks[7], (DEPTH, POOL_WIDTH), f32, minval=0.5, maxval=1.5)
    w_out = jax.random.normal(ks[8], (DEPTH, MIX_WIDTH, D_MODEL), f32) * MIX_WIDTH ** -0.5
    post_norm_g = 1.0 + 0.05 * jax.random.normal(ks[9], (DEPTH, D_MODEL), f32)
    return {"x": x, "lower_bounds": lower_bounds, "pre_norm_g": pre_norm_g, "w_in": w_in,
            "hgrn_norm_g": hgrn_norm_g, "fox_f_bias": fox_f_bias, "pool_w": pool_w,
            "pool_scale": pool_scale, "w_out": w_out, "post_norm_g": post_norm_g}


def reference(x, lower_bounds, pre_norm_g, w_in, hgrn_norm_g, fox_f_bias, pool_w, pool_scale, w_out, post_norm_g):
    B, T, _ = x.shape
    p = jax.nn.softmax(lower_bounds.astype(jnp.float32), axis=0)
    lbs = jnp.cumsum(p, axis=0) - p[0]
    for l in range(DEPTH):
        h = _rmsnorm(x, pre_norm_g[l])
        proj = jnp.einsum('btd,de->bte', h, w_in[l])
        q_a, f_a, i_a, g_a, u_b, g_b, q_c, k_c, v_c, g_c, f_c = jnp.split(proj, IN_SPLIT_POINTS, axis=-1)

        lb = lbs[l]
        z = f_a.astype(jnp.float32)
        f_gate = lb + (1.0 - lb) * jax.nn.sigmoid(z)
        log_f_a = jnp.log(jnp.maximum(f_gate, TINY))
        k_a = (1.0 - lb) * jax.nn.sigmoid(-z)
        hshape = (B, T, HGRN_HEADS, HGRN_KEY_DIM)
        o_a = _hgrn2_chunkwise(jax.nn.silu(q_a).reshape(hshape), k_a.reshape(hshape),
                               i_a.reshape(B, T, HGRN_HEADS, HGRN_VAL_DIM), log_f_a.reshape(hshape))
        o_a = _rmsnorm(o_a, hgrn_norm_g[l].reshape(HGRN_HEADS, HGRN_VAL_DIM)).reshape(B, T, HGRN_WIDTH)
        o_a = o_a.astype(x.dtype) * jax.nn.silu(g_a)

        pooled = _multiscale_pool(u_b)
        o_b = jnp.einsum('btgc,gcd->btgd', pooled, pool_w[l].astype(jnp.float32)).reshape(B, T, POOL_WIDTH)
        o_b = (o_b * pool_scale[l].astype(jnp.float32)).astype(x.dtype) * jax.nn.silu(g_b)

        log_f_c = jax.nn.log_sigmoid((f_c + fox_f_bias[l]).astype(jnp.float32))
        fshape = (B, T, FOX_HEADS, FOX_HEAD_DIM)
        o_c = _forgetting_attention(q_c.reshape(fshape), k_c.reshape(fshape), v_c.reshape(fshape), log_f_c)
        o_c = o_c.reshape(B, T, FOX_WIDTH).astype(x.dtype) * jax.nn.silu(g_c)

        mixed = jnp.concatenate([o_a, o_b, o_c], axis=-1)
        y = jnp.einsum('bte,ed->btd', mixed, w_out[l])
        x = x + _rmsnorm(y, post_norm_g[l])
    return x
```

```python
import numpy as np
import concourse.bass as bass
import concourse.mybir as mybir
from concourse.bass_utils import run_bass_kernel_spmd

F32 = mybir.dt.float32
BF16 = mybir.dt.bfloat16
AF = mybir.ActivationFunctionType
ALU = mybir.AluOpType
AX = mybir.AxisListType

D = 1024
INW = 3592
EPS = 1e-6
TINY = 1e-30
P = 128
POOL_WINDOWS = (2, 4, 8, 16)
EPOCH = 24000


CLOCKS = {}


class Buf:
    __slots__ = ("w", "r", "excl")

    def __init__(self, excl=False):
        self.w = None
        self.r = {}
        self.excl = excl


class Eng:
    def __init__(self, nc, eng, name):
        self.nc, self.eng, self.name = nc, eng, name
        self.sem = nc.alloc_semaphore(name + "_s0")
        self.own = {id(self.sem)}
        self.n = 0
        self.ep = 0
        self.seen = {}
        self.pending = 0

    def wait_all(self, deps):
        deps = [d for d in deps if self.seen.get(id(d[0]), 0) < d[1]]
        if len(deps) > 1:
            keep = []
            for d in deps:
                k = id(d[0])
                implied = False
                for d2 in deps:
                    if d2 is not d:
                        c2 = CLOCKS.get((id(d2[0]), d2[1]))
                        if c2 is not None and c2.get(k, 0) >= d[1]:
                            implied = True
                            break
                if not implied:
                    keep.append(d)
            deps = keep
        for sem, val in deps:
            key = id(sem)
            if self.seen.get(key, 0) < val:
                self.eng.wait_ge(sem, val)
                self.seen[key] = val
            c = CLOCKS.get((key, val))
            if c is not None:
                seen = self.seen
                for k2, v2 in c.items():
                    if seen.get(k2, 0) < v2:
                        seen[k2] = v2

    def snapshot(self, tok):
        c = dict(self.seen)
        c[id(tok[0])] = max(c.get(id(tok[0]), 0), tok[1])
        CLOCKS[(id(tok[0]), tok[1])] = c

    def issue(self, ins):
        if self.n >= EPOCH and self.pending == 0:
            self.ep += 1
            self.sem = self.nc.alloc_semaphore("%s_s%d" % (self.name, self.ep))
            self.own.add(id(self.sem))
            self.n = 0
        self.n += 1
        ins.then_inc(self.sem, 1)
        self.pending = 0
        return (self.sem, self.n)

    def issue_noinc(self, ins):
        if self.n >= EPOCH:
            pass
        self.pending += 1
        return (self.sem, self.n + 1)


class DmaQ:
    def __init__(self, nc, E, name, nlanes):
        self.E = E
        self.lanes = [[nc.alloc_semaphore("%s_l%d" % (name, i)), 0] for i in range(nlanes)]
        self.k = 0

    def issue(self, fn, deps):
        lane = self.lanes[self.k]
        self.k = (self.k + 1) % len(self.lanes)
        d = set(deps)
        if lane[1] > 0:
            d.add((lane[0], lane[1]))
        self.E.wait_all(d)
        lane[1] += 16
        fn().then_inc(lane[0], 16)
        tok = (lane[0], lane[1])
        self.E.snapshot(tok)
        return tok


class Ctx:
    def __init__(self, nc):
        self.nc = nc
        self.pe = Eng(nc, nc.tensor, "pe")
        self.act = Eng(nc, nc.scalar, "act")
        self.dve = Eng(nc, nc.vector, "dve")
        self.pool = Eng(nc, nc.gpsimd, "pool")
        self.sp = Eng(nc, nc.sync, "sp")
        self.q_sp = DmaQ(nc, self.sp, "qsp", 8)
        self.q_pool = DmaQ(nc, self.pool, "qpl", 16)

    def _deps(self, reads, writes, E=None):
        deps = set()
        own = E.own if E is not None else ()
        is_pe = E is self.pe
        for b in reads:
            if b.w is not None and not (is_pe and id(b.w[0]) in own):
                deps.add(b.w)
            if b.excl:
                for tok in b.r.values():
                    if id(tok[0]) not in own:
                        deps.add(tok)
        for b in writes:
            if b.w is not None and not (is_pe and id(b.w[0]) in own):
                deps.add(b.w)
            for tok in b.r.values():
                if not (is_pe and id(tok[0]) in own):
                    deps.add(tok)
        return deps

    @staticmethod
    def _commit(tok, reads, writes):
        for b in reads:
            b.r[id(tok[0])] = tok
        for b in writes:
            b.w = tok
            b.r = {}

    def op(self, E, fn, reads=(), writes=(), inc=True, mode=None):
        if mode is not None and mode != getattr(self, "pe_mode", None):
            if E.n > 0:
                assert E.pending == 0
                E.eng.wait_ge(E.sem, E.n)
                E.seen[id(E.sem)] = E.n
            self.pe_mode = mode
        E.wait_all(self._deps(reads, writes, E))
        tok = E.issue(fn()) if inc else E.issue_noinc(fn())
        if inc:
            E.snapshot(tok)
        elif (id(tok[0]), tok[1]) not in CLOCKS:
            E.snapshot(tok)
        self._commit(tok, reads, writes)
        return tok

    def dma(self, Q, fn, reads=(), writes=()):
        tok = Q.issue(fn, self._deps(reads, writes))
        self._commit(tok, reads, writes)
        return tok


def _consts():
    s = np.arange(P)[:, None]
    t = np.arange(P)[None, :]
    same = (s // 64) == (t // 64)
    c = {}
    c["ident"] = np.eye(P, dtype=np.float32)
    c["ucum"] = (same & (s <= t)).astype(np.float32)
    c["lstr"] = (same & (s > t)).astype(np.float32)
    c["ufull"] = (s <= t).astype(np.float32)
    c["ones"] = np.ones((P, P), np.float32)
    ind = np.zeros((P, 2), np.float32)
    ind[:64, 0] = 1
    ind[64:, 1] = 1
    c["ind"] = ind
    c["fmask"] = (s <= t).astype(np.float32)
    hm = np.zeros((P, 256), np.float32)
    for h in range(4):
        hm[:, h * 64:(h + 1) * 64] = ((np.arange(P)[:, None] % 64) <= np.arange(64)[None, :])
    c["hmask"] = hm
    pm_cur = np.zeros((P, 4, P), np.float32)
    pm_prev = np.zeros((P, 4, 16), np.float32)
    pm_cur0 = np.zeros((P, 4, 16), np.float32)
    for g, w in enumerate(POOL_WINDOWS):
        for tt in range(P):
            for ss in range(tt - w + 1, tt + 1):
                if ss >= 0:
                    pm_cur[ss, g, tt] += 1.0 / w
                else:
                    if tt < 16:
                        pm_prev[ss + P, g, tt] += 1.0 / w
            pm_cur[tt, g, tt] -= 1.0
        for tt in range(16):
            cnt = min(tt + 1, w)
            for ss in range(max(0, tt - w + 1), tt + 1):
                pm_cur0[ss, g, tt] += 1.0 / cnt
            pm_cur0[tt, g, tt] -= 1.0
    c["pm_cur"] = pm_cur.reshape(P, 4 * P)
    c["pm_prev"] = pm_prev.reshape(P, 64)
    c["pm_cur0"] = pm_cur0.reshape(P, 64)
    return c


CONST_SHAPES = {"ident": (P, P), "ucum": (P, P), "lstr": (P, P), "ufull": (P, P), "ones": (P, P),
                "ind": (P, 2), "fmask": (P, P), "hmask": (P, 256), "pm_cur": (P, 512),
                "pm_prev": (P, 64), "pm_cur0": (P, 64)}
LAYER_SHAPES = {"w_in": (D, INW), "w_out": (D, D), "preg_bc": (P, D), "postg_bc": (P, D),
                "hgrng_bc": (P, 256), "pscale_bc": (P, 256), "lbraw_bc": (P, 512),
                "fbias_bc": (P, 8), "pw": (P, 128)}


def build_program(T, NSEQ, lb_modes, debug_out=False):
    NL = len(lb_modes)
    NB = T // P
    nc = bass.Bass("TRN2", target_bir_lowering=False)
    CLOCKS.clear()
    K = Ctx(nc)
    pe, act, dve, pool, sp = K.pe, K.act, K.dve, K.pool, K.sp

    x_d = nc.dram_tensor("x", [NSEQ, T, D], F32, kind="ExternalInput").ap()
    out_d = nc.dram_tensor("out", [NSEQ, T, D], F32, kind="ExternalOutput").ap()
    cd = {k: nc.dram_tensor("c_" + k, list(v), F32, kind="ExternalInput").ap() for k, v in CONST_SHAPES.items()}
    ld = [{k: nc.dram_tensor("l%d_%s" % (p, k), list(v), F32, kind="ExternalInput").ap()
           for k, v in LAYER_SHAPES.items()} for p in range(NL)]

    wsc_in = [nc.dram_tensor("wsc_in%d" % p, [P, 8, INW], BF16, kind="Internal").ap() for p in range(NL)]
    wsc_out = [nc.dram_tensor("wsc_out%d" % p, [P, 8, D], BF16, kind="Internal").ap() for p in range(NL)]
    wsc_ib = [[Buf() for _ in range(8)] for _ in range(NL)]
    wsc_ob = [[Buf() for _ in range(8)] for _ in range(NL)]
    wsc_ready = [False] * NL

    def sb(name, shape, dt):
        return nc.alloc_sbuf_tensor(name, list(shape), dt)

    x_seq = sb("x_seq", [P, NB, D], F32)
    Xb = [Buf() for _ in range(NB)]
    Win = sb("Win", [P, 8, INW], BF16)
    Wout = sb("Wout", [P, 8, D], BF16)
    Wib = Buf()
    Wob = Buf()
    KT = sb("KT", [P, 4, T], BF16)
    KTb = [Buf() for _ in range(NB)]
    V = sb("V", [P, NB, 8 * 65], BF16)
    Vb = [Buf() for _ in range(NB)]
    Vones = Buf()
    ident = sb("ident", [P, P], BF16)
    ucum = sb("ucum", [P, P], F32)
    lstr = sb("lstr", [P, P], F32)
    ufull = sb("ufull", [P, P], F32)
    ones = sb("ones", [P, P], F32)
    ind = sb("ind", [P, 2], F32)
    fmask = sb("fmask", [P, P], BF16)
    hmask = sb("hmask", [P, 256], BF16)
    pm_cur = sb("pm_cur", [P, 512], BF16)
    pm_prev = sb("pm_prev", [P, 64], BF16)
    pm_cur0 = sb("pm_cur0", [P, 64], BF16)
    CbL = [Buf() for _ in range(len(CONST_SHAPES))]
    preg = sb("preg", [P, D], BF16)
    postg = sb("postg", [P, D], BF16)
    hgrng = sb("hgrng", [P, 256], BF16)
    pscale = sb("pscale", [P, 256], BF16)
    lb_bc = sb("lb_bc", [P, 256], F32)
    oml_bc = sb("oml_bc", [P, 256], F32)
    lbraw = None
    fbias = sb("fbias", [P, 8], F32)
    pw = sb("pw", [P, 128], BF16)
    Lpre, Lpost, Lpw, Lhg, Lps, Lfb, Llb = (Buf() for _ in range(7))
    hm = sb("hm", [P, D], BF16)
    hmb = Buf()
    hh = sb("hh", [P, D], BF16)
    hhb = Buf()
    hT = sb("hT", [P, 8, P], BF16)
    hTb = Buf()
    XYf = [hT[:].rearrange("p a b -> p (a b)"), hm[:]]
    XYc = [hT[:], hm[:].rearrange("p (a b) -> p a b", b=P)]
    XYb = [hTb, hmb]
    NF = 5
    Fs = [sb("F%d" % i, [P, 256], F32) for i in range(NF)]
    Fb = [Buf() for _ in range(NF)]
    ebig = sb("ebig", [P, 256], F32)
    ebb = Buf()
    gate_a = sb("gate_a", [P, 256], BF16)
    gate_b = sb("gate_b", [P, 256], BF16)
    gate_c = sb("gate_c", [P, 512], BF16)
    gab, gbb, gcb = Buf(), Buf(), Buf()
    qp = sb("qp", [P, 256], BF16)
    qpp = sb("qpp", [P, 256], BF16)
    kp = sb("kp", [P, 256], BF16)
    va = sb("va", [P, 256], BF16)
    attn_sb = qp
    qpb, qppb, kpb, vab = Buf(), Buf(), Buf(), Buf()
    attnb = qpb
    TT = sb("TT", [P, 6, P], BF16)
    TTb = Buf()
    S = sb("S", [P, 2, 64], F32)
    Sb = Buf()
    S_bf = [sb("S_bf%d" % i, [P, 2, 64], BF16) for i in range(2)]
    S_bfb = [Buf(), Buf()]
    dec = sb("dec", [P, 4], F32)
    decb = Buf()
    u_sb = [sb("u_sb%d" % i, [P, 256], BF16) for i in range(2)]
    ub = [Buf(), Buf()]
    plT = qpp[:].rearrange("p (a b) -> p a b", b=P)
    plTb = qppb
    q_sb = Fs[3][:].bitcast(BF16)
    k_sb = Fs[4][:].bitcast(BF16)
    qsb_b, ksb_b = Fb[3], Fb[4]
    QTa = sb("QTa", [P, 4, P], BF16)
    QTz = sb("QTz", [P, 4, P], BF16)
    QTb = Buf()
    NPT = 3
    NVT = 3
    Vt = [sb("Vt%d" % i, [P, 4 * 65], BF16) for i in range(NVT)]
    Vtb = [Buf() for _ in range(NVT)]
    PT = [sb("PT%d" % i, [P, 512], BF16) for i in range(NPT)]
    PTb = [Buf() for _ in range(NPT)]
    fij = [sb("fij%d" % i, [P, NB, 8], F32) for i in range(2)]
    biasb = [Buf(), Buf()]
    totall = sb("totall", [P, NB, 8], F32)
    small = sb("small", [P, 64], F32)
    smb = {k: Buf() for k in ("n1", "hg", "fc", "acc", "tot", "rden", "n2", "w")}

    banks = [nc.alloc_psum_tensor("bank%d" % i, [P, 512], F32) for i in range(8)]
    bankb = [Buf(excl=True) for _ in range(8)]
    gstate = [0, [0, 1, 2, 3, 4, 5]]

    def G():
        lst = gstate[1]
        i = lst[gstate[0] % len(lst)]
        gstate[0] += 1
        return banks[i], bankb[i]

    NST = 3
    ST = [(banks[3], bankb[3]), (banks[4], bankb[4]), (banks[5], bankb[5])]
    OA = [(banks[6], bankb[6]), (banks[7], bankb[7])]

    def bf(t):
        return t[:].bitcast(BF16)

    def load_cast(dst_ap, src_ap, buf):
        K.dma(K.q_pool, lambda: nc.gpsimd.dma_start(out=dst_ap, in_=src_ap), writes=[buf])

    def load_f32(dst_ap, src_ap, buf):
        K.dma(K.q_sp, lambda: nc.sync.dma_start(out=dst_ap, in_=src_ap), writes=[buf])

    _cseen = []
    for name, t_, cast in (("ident", ident, 1), ("ucum", ucum, 0), ("lstr", lstr, 0), ("ufull", ufull, 0),
                           ("ones", ones, 0), ("ind", ind, 0), ("fmask", fmask, 1), ("hmask", hmask, 1),
                           ("pm_cur", pm_cur, 1), ("pm_prev", pm_prev, 1), ("pm_cur0", pm_cur0, 1)):
        (load_cast if cast else load_f32)(t_[:], cd[name][:, :], CbL[len(_cseen)])
        _cseen.append(name)
    Vv = V[:].rearrange("p b (h e) -> p b h e", e=65)
    K.op(pool, lambda: nc.gpsimd.memset(QTa[:], 0.0), writes=[QTb])
    K.op(pool, lambda: nc.gpsimd.memset(QTz[:], 0.0), writes=[QTb])

    def load_win(p):
        if wsc_ready[p]:
            for dc in range(8):
                K.dma(K.q_sp, lambda dc=dc: nc.sync.dma_start(out=Win[:, dc, :], in_=wsc_in[p][:, dc, :]),
                      reads=[wsc_ib[p][dc]], writes=[Wib])
        else:
            for dc in range(8):
                load_cast(Win[:, dc, :], ld[p]["w_in"][dc * P:(dc + 1) * P, :], Wib)
            for dc in range(8):
                K.dma(K.q_sp, lambda dc=dc: nc.sync.dma_start(out=wsc_in[p][:, dc, :], in_=Win[:, dc, :]),
                      reads=[Wib], writes=[wsc_ib[p][dc]])

    def load_wout(p):
        if wsc_ready[p]:
            for ec in range(8):
                K.dma(K.q_sp, lambda ec=ec: nc.sync.dma_start(out=Wout[:, ec, :], in_=wsc_out[p][:, ec, :]),
                      reads=[wsc_ob[p][ec]], writes=[Wob])
        else:
            for ec in range(8):
                load_cast(Wout[:, ec, :], ld[p]["w_out"][ec * P:(ec + 1) * P, :], Wob)
            for ec in range(8):
                K.dma(K.q_sp, lambda ec=ec: nc.sync.dma_start(out=wsc_out[p][:, ec, :], in_=Wout[:, ec, :]),
                      reads=[Wob], writes=[wsc_ob[p][ec]])
            wsc_ready[p] = True

    def load_layer(p, with_weights=True, with_preg=True):
        L = ld[p]
        if with_weights:
            load_win(p)
            load_wout(p)
        if with_preg:
            load_cast(preg[:], L["preg_bc"][:, :], Lpre)
        load_cast(postg[:], L["postg_bc"][:, :], Lpost)
        load_cast(pw[:], L["pw"][:, :], Lpw)
        load_cast(hgrng[:], L["hgrng_bc"][:, :], Lhg)
        load_cast(pscale[:], L["pscale_bc"][:, :], Lps)
        load_f32(fbias[:], L["fbias_bc"][:, :], Lfb)

    def compute_lb1(p):
        L = ld[p]
        load_f32(Fs[1][:], L["lbraw_bc"][:, 0:256], Fb[1])
        load_f32(Fs[2][:], L["lbraw_bc"][:, 256:512], Fb[2])
        K.op(pool, lambda: nc.gpsimd.tensor_tensor(out=Fs[0][:], in0=Fs[1][:], in1=Fs[2][:],
                                                   op=ALU.subtract), reads=[Fb[1], Fb[2]], writes=[Fb[0]])
        K.op(act, lambda: nc.scalar.activation(out=Fs[0][:], in_=Fs[0][:], func=AF.Exp),
             reads=[], writes=[Fb[0]])
        K.op(pool, lambda: nc.gpsimd.tensor_scalar(out=Fs[0][:], in0=Fs[0][:], scalar1=1.0, scalar2=1.0,
                                                   op0=ALU.mult, op1=ALU.add), writes=[Fb[0]])
        K.op(dve, lambda: nc.vector.reciprocal(out=lb_bc[:], in_=Fs[0][:]), reads=[Fb[0]], writes=[Llb])
        K.op(pool, lambda: nc.gpsimd.tensor_scalar(out=oml_bc[:], in0=lb_bc[:], scalar1=-1.0, scalar2=1.0,
                                                   op0=ALU.mult, op1=ALU.add), writes=[Llb])

    def silu_gate(ps_ap, width, out_ap, outb, mul_ap=None, psb=None):
        e = ebig[:, 0:width]
        K.op(act, lambda: nc.scalar.activation(out=e, in_=ps_ap, func=AF.Exp, scale=-1.0),
             reads=[psb], writes=[ebb])
        K.op(act, lambda: nc.scalar.activation(out=e, in_=e, func=AF.Ln, bias=1.0), writes=[ebb])
        K.op(act, lambda: nc.scalar.activation(out=e, in_=e, func=AF.Exp, scale=-1.0), writes=[ebb])
        if mul_ap is None:
            K.op(dve, lambda: nc.vector.tensor_tensor(out=out_ap, in0=ps_ap, in1=e, op=ALU.mult),
                 reads=[psb, ebb], writes=[outb])
        else:
            K.op(dve, lambda: nc.vector.tensor_tensor(out=e, in0=ps_ap, in1=e, op=ALU.mult),
                 reads=[psb], writes=[ebb])
            K.op(pool, lambda: nc.gpsimd.tensor_tensor(out=out_ap, in0=e, in1=mul_ap, op=ALU.mult),
                 reads=[ebb, Lhg, Lps], writes=[outb])

    def rms_rstd(ss_ap, n, out_ap, b):
        K.op(dve, lambda: nc.vector.tensor_scalar(out=out_ap, in0=ss_ap, scalar1=1.0 / n, scalar2=EPS,
                                                  op0=ALU.mult, op1=ALU.add), writes=[b])
        K.op(act, lambda: nc.scalar.activation(out=out_ap, in_=out_ap, func=AF.Ln), writes=[b])
        K.op(act, lambda: nc.scalar.activation(out=out_ap, in_=out_ap, func=AF.Exp, scale=-0.5), writes=[b])

    def _cls(n):
        return 32 if n <= 32 else (64 if n <= 64 else 128)

    def mm(out, lhsT, rhs, start, stop, reads, writes, inc=None):
        kc = _cls(lhsT.shape[0])
        mode = ("mm", str(lhsT.dtype), kc, _cls(lhsT.shape[-1]), lhsT.base_partition() if kc < 128 else 0)
        return K.op(pe, lambda: nc.tensor.matmul(out, lhsT, rhs, start=start, stop=stop), reads=reads, writes=writes,
                    inc=(bool(stop) or kc < 128) if inc is None else inc, mode=mode)

    def tr(out, in_, reads, writes):
        return K.op(pe, lambda: nc.tensor.transpose(out, in_, ident[:]), reads=list(reads) + [*CbL], writes=writes,
                    mode=("tr",))

    def front(s, i, p):
        xb = x_seq[:, i, :]
        n1 = smb["n1"]
        K.op(act, lambda: nc.scalar.activation(out=hh[:], in_=xb, func=AF.Square, accum_out=small[:, 0:1]),
             reads=[Xb[i]], writes=[hhb, n1])
        rms_rstd(small[:, 0:1], float(D), small[:, 2:3], n1)
        K.op(dve, lambda: nc.vector.scalar_tensor_tensor(out=hh[:], in0=xb, scalar=small[:, 2:3], in1=preg[:],
                                                         op0=ALU.mult, op1=ALU.mult),
             reads=[Xb[i], n1, Lpre], writes=[hhb])

    def front_b(tp):
        tb, tbb = G()
        for dc in range(8):
            tr(bf(tb)[:, dc * P:(dc + 1) * P], hh[:, dc * P:(dc + 1) * P], [hhb], [tbb])
        K.op(act, lambda: nc.scalar.activation(out=XYf[tp], in_=bf(tb)[:, 0:D], func=AF.Copy),
             reads=[tbb], writes=[XYb[tp]])


    def block(s, i, p, last_layer, do_front=True, next_front=None, prefetch=None):
        xb = x_seq[:, i, :]
        par = i % 2

        def finish():
            if last_layer:
                K.dma(K.q_sp, lambda: nc.sync.dma_start(out=out_d[s, i * P:(i + 1) * P, :], in_=x_seq[:, i, :]),
                      reads=[Xb[i]])
        if do_front:
            front(s, i, p)
            front_b(i % 2)
        def proj(col0, width):
            b_, bb_ = G()
            for dc in range(8):
                mm(b_[:, 0:width], XYc[par][:, dc, :], Win[:, dc, col0:col0 + width], dc == 0, dc == 7,
                   [XYb[par], Wib], [bb_])
            return b_, bb_

        b7, b7b = proj(3584, 8)
        fcb = smb["fc"]
        K.op(dve, lambda: nc.vector.tensor_tensor(out=small[:, 12:20], in0=b7[:, 0:8], in1=fbias[:], op=ALU.add),
             reads=[b7b, Lfb], writes=[fcb])
        K.op(act, lambda: nc.scalar.activation(out=small[:, 12:20], in_=small[:, 12:20], func=AF.Exp, scale=-1.0),
             writes=[fcb])
        K.op(pool, lambda: nc.gpsimd.tensor_scalar(out=small[:, 12:20], in0=small[:, 12:20], scalar1=1.0,
                                                   scalar2=1.0, op0=ALU.mult, op1=ALU.add), writes=[fcb])
        K.op(act, lambda: nc.scalar.activation(out=small[:, 20:28], in_=small[:, 12:20], func=AF.Ln), writes=[fcb])
        sp_ap = small[:, 20:28]
        acc_ap = small[:, 28:36]
        accb = smb["acc"]
        b3, b3b = proj(1536, 512)
        K.op(act, lambda: nc.scalar.activation(out=q_sb, in_=b3[:, 0:512], func=AF.Copy), reads=[b3b], writes=[qsb_b])
        b4, b4b = proj(2048, 512)
        K.op(dve, lambda: nc.vector.tensor_copy(out=k_sb, in_=b4[:, 0:512]), reads=[b4b], writes=[ksb_b])
        tq, tqb = G()
        for hp in range(4):
            tr(bf(tq)[:, hp * P:(hp + 1) * P], q_sb[:, hp * P:(hp + 1) * P], [qsb_b], [tqb])
        for hp in range(4):
            tr(bf(tq)[:, (4 + hp) * P:(5 + hp) * P], k_sb[:, hp * P:(hp + 1) * P], [ksb_b], [tqb])
        K.op(act, lambda: nc.scalar.activation(out=QTa[0:64].rearrange("p a b -> p (a b)"), in_=bf(tq)[0:64, 0:512], func=AF.Copy),
             reads=[tqb], writes=[QTb])
        K.op(act, lambda: nc.scalar.activation(out=QTz[64:128].rearrange("p a b -> p (a b)"), in_=bf(tq)[64:128, 0:512], func=AF.Copy),
             reads=[tqb], writes=[QTb])
        K.op(act, lambda: nc.scalar.activation(out=KT[:, :, i * P:(i + 1) * P],
                                               in_=bf(tq)[:, 512:1024].rearrange("p (a b) -> p a b", b=P),
                                               func=AF.Copy), reads=[tqb], writes=[KTb[i]])

        bc_, bcb = G()
        mm(bc_[:, 0:8], ufull[:], sp_ap, True, False, [*CbL, fcb], [bcb])
        mm(bc_[:, 0:8], ones[:], acc_ap, False, True, [*CbL, accb], [bcb])
        K.op(pool, lambda: nc.gpsimd.tensor_tensor(out=acc_ap, in0=acc_ap, in1=sp_ap, op=ALU.add),
             reads=[fcb], writes=[accb])
        mm(bc_[:, 8:16], ones[:], acc_ap, True, True, [*CbL, accb], [bcb])
        totb = smb["tot"]
        wb = smb["w"]
        K.op(dve, lambda: nc.vector.tensor_copy(out=totall[:, i, :], in_=bc_[:, 8:16]), reads=[bcb], writes=[totb])
        K.op(dve, lambda: nc.vector.tensor_tensor(out=small[:, 52:60], in0=bc_[:, 0:8], in1=totall[:, i, :],
                                                  op=ALU.subtract), reads=[bcb, totb], writes=[wb])
        K.op(act, lambda: nc.scalar.activation(out=small[:, 52:60], in_=small[:, 52:60], func=AF.Exp), writes=[wb])
        K.op(pool, lambda: nc.gpsimd.tensor_tensor(
            out=fij[par][:, 0:i + 1, :], in0=totall[:, 0:i + 1, :],
            in1=totall[:, i:i + 1, :].to_broadcast([P, i + 1, 8]), op=ALU.subtract),
            reads=[totb], writes=[biasb[par]])
        K.op(act, lambda: nc.scalar.activation(out=fij[par][:, 0:i + 1, :], in_=fij[par][:, 0:i + 1, :], func=AF.Exp),
             writes=[biasb[par]])
        b5, b5b = proj(2560, 512)
        K.op(dve, lambda: nc.vector.tensor_tensor(
            out=Vv[:, i, :, 0:64], in0=b5[:, 0:512].rearrange("p (h d) -> p h d", d=64),
            in1=small[:, 52:60].unsqueeze(2).to_broadcast([P, 8, 64]), op=ALU.mult),
            reads=[b5b, smb["w"]], writes=[Vb[i]])
        K.op(pool, lambda: nc.gpsimd.tensor_copy(out=Vv[:, i, :, 64], in_=small[:, 52:60]),
             reads=[smb["w"]], writes=[Vb[i]])
        b6, b6b = proj(3072, 512)
        silu_gate(b6[:, 0:256], 256, gate_c[:, 0:256], gcb, psb=b6b)
        silu_gate(b6[:, 256:512], 256, gate_c[:, 256:512], gcb, psb=b6b)
        gstate[1] = [0, 1, 2]
        b0, b0b = proj(0, 512)
        Fq, Fqb = Fs[0], Fb[0]
        Fk, Fkb = Fs[1], Fb[1]
        Fl, Flb = Fs[2], Fb[2]
        Fx, Fxb = Fs[3], Fb[3]
        Fy, Fyb = Fs[4], Fb[4]
        b1, b1b = proj(512, 512)
        b2, b2b = proj(1024, 512)
        if prefetch is not None:
            load_win(prefetch)
        def ab_thread():
            K.op(act, lambda: nc.scalar.activation(out=Fx[:], in_=b0[:, 0:256], func=AF.Exp, scale=-1.0),
                 reads=[b0b], writes=[Fxb])
            yield
            K.op(act, lambda: nc.scalar.activation(out=Fx[:], in_=Fx[:], func=AF.Ln, bias=1.0), writes=[Fxb])
            yield
            K.op(act, lambda: nc.scalar.activation(out=Fx[:], in_=Fx[:], func=AF.Exp, scale=-1.0), writes=[Fxb])
            yield
            K.op(dve, lambda: nc.vector.tensor_tensor(out=Fq[:], in0=b0[:, 0:256], in1=Fx[:], op=ALU.mult),
                 reads=[b0b, Fxb], writes=[Fqb])
            yield
            K.op(act, lambda: nc.scalar.activation(out=Fy[:], in_=b0[:, 256:512], func=AF.Exp, scale=-1.0),
                 reads=[b0b], writes=[Fyb])
            yield
            K.op(act, lambda: nc.scalar.activation(out=Fy[:], in_=Fy[:], func=AF.Ln, bias=1.0), writes=[Fyb])
            yield
            K.op(act, lambda: nc.scalar.activation(out=Fy[:], in_=Fy[:], func=AF.Exp, scale=-1.0), writes=[Fyb])
            yield
            if lb_modes[p] == 0:
                K.op(pool, lambda: nc.gpsimd.tensor_scalar(out=Fk[:], in0=Fy[:], scalar1=-1.0, scalar2=1.0,
                                                           op0=ALU.mult, op1=ALU.add), reads=[Fyb], writes=[Fkb])
                yield
                K.op(dve, lambda: nc.vector.tensor_scalar(out=Fy[:], in0=Fy[:], scalar1=TINY, scalar2=None,
                                                          op0=ALU.max), writes=[Fyb])
                yield
            else:
                K.op(pool, lambda: nc.gpsimd.tensor_tensor(out=Fy[:], in0=Fy[:], in1=oml_bc[:], op=ALU.mult),
                     reads=[Llb], writes=[Fyb])
                yield
                K.op(pool, lambda: nc.gpsimd.tensor_tensor(out=Fk[:], in0=oml_bc[:], in1=Fy[:], op=ALU.subtract),
                     reads=[Llb, Fyb], writes=[Fkb])
                yield
                K.op(dve, lambda: nc.vector.scalar_tensor_tensor(out=Fy[:], in0=Fy[:], scalar=TINY, in1=lb_bc[:],
                                                                 op0=ALU.max, op1=ALU.add), reads=[Llb], writes=[Fyb])
                yield
            K.op(act, lambda: nc.scalar.activation(out=Fl[:], in_=Fy[:], func=AF.Ln), reads=[Fyb], writes=[Flb])

            yield
            K.op(act, lambda: nc.scalar.activation(out=va[:], in_=b1[:, 0:256], func=AF.Copy), reads=[b1b], writes=[vab])
            yield
            silu_gate(b1[:, 256:512], 256, gate_a[:], gab, mul_ap=hgrng[:], psb=b1b)
            yield
            K.op(act, lambda: nc.scalar.activation(out=u_sb[par][:], in_=b2[:, 0:256], func=AF.Copy),
                 reads=[b2b], writes=[ub[par]])
            yield
            silu_gate(b2[:, 256:512], 256, gate_b[:], gbb, mul_ap=pscale[:], psb=b2b)

            yield
            be, beb = G()
            mm(be[:, 0:256], ucum[:], Fl[:], True, True, [*CbL, Flb], [beb])
            yield
            mm(be[:, 256:512], lstr[:], Fl[:], True, True, [*CbL, Flb], [beb])
            yield
            bd, bdb = G()
            for hp in range(2):
                mm(bd[:, hp * 2:hp * 2 + 2], Fl[:, hp * P:(hp + 1) * P], ind[:], True, True, [*CbL, Flb], [bdb])
            yield
            K.op(act, lambda: nc.scalar.activation(out=dec[:], in_=bd[:, 0:4], func=AF.Exp), reads=[bdb], writes=[decb])
            yield
            K.op(act, lambda: nc.scalar.activation(out=Fx[:], in_=be[:, 0:256], func=AF.Exp), reads=[beb], writes=[Fxb])
            yield
            K.op(dve, lambda: nc.vector.tensor_tensor(out=qp[:], in0=Fq[:], in1=Fx[:], op=ALU.mult),
                 reads=[Fqb, Fxb], writes=[qpb])
            yield
            K.op(act, lambda: nc.scalar.activation(out=Fy[:], in_=be[:, 256:512], func=AF.Exp, scale=-1.0),
                 reads=[beb], writes=[Fyb])
            yield
            K.op(dve, lambda: nc.vector.tensor_tensor(out=qpp[:], in0=Fq[:], in1=Fy[:], op=ALU.mult),
                 reads=[Fqb, Fyb], writes=[qppb])
            yield
            K.op(act, lambda: nc.scalar.activation(out=Fx[:], in_=be[:, 256:512], func=AF.Exp), reads=[beb], writes=[Fxb])
            yield
            K.op(dve, lambda: nc.vector.tensor_tensor(out=kp[:], in0=Fk[:], in1=Fx[:], op=ALU.mult),
                 reads=[Fkb, Fxb], writes=[kpb])
            yield
            tt_, ttb = G()
            for n_, (src, srcb) in enumerate(((qp, qpb), (qpp, qppb), (kp, kpb))):
                for hp in range(2):
                    tr(bf(tt_)[:, (2 * n_ + hp) * P:(2 * n_ + hp + 1) * P], src[:, hp * P:(hp + 1) * P], [srcb], [ttb])
            yield
            K.op(act, lambda: nc.scalar.activation(out=TT[:].rearrange("p a b -> p (a b)"), in_=bf(tt_)[:, 0:768], func=AF.Copy),
                 reads=[ttb], writes=[TTb])
            yield
            ba, bab = G()
            for h in (0, 2, 1, 3):
                hp, r = h // 2, (h % 2) * 64
                for c in range(2):
                    cs = slice(c * 64, (c + 1) * 64)
                    mm(ba[cs, h * 64:(h + 1) * 64], TT[r:r + 64, 4 + hp, cs], TT[r:r + 64, 2 + hp, cs], True, True,
                       [TTb], [bab])
            yield
            for c in range(2):
                cs = slice(c * 64, (c + 1) * 64)
                for h in range(4):
                    hp, r = h // 2, (h % 2) * 64
                    col = 256 + (c * 2 + hp) * 64
                    mm(ba[r:r + 64, col:col + 64], kp[cs, h * 64:(h + 1) * 64], va[cs, h * 64:(h + 1) * 64], True, True,
                       [kpb, vab], [bab])
            yield
            K.op(dve, lambda: nc.vector.tensor_tensor(out=attn_sb[:], in0=ba[:, 0:256], in1=hmask[:], op=ALU.mult),
                 reads=[bab, *CbL], writes=[attnb])
            yield
            for c in range(2):
                for hp in range(2):
                    col = 256 + (c * 2 + hp) * 64
                    K.op(dve, lambda hp=hp, col=col, c=c: nc.vector.scalar_tensor_tensor(
                        out=S[:, hp, :], in0=S[:, hp, :], scalar=dec[:, hp * 2 + c:hp * 2 + c + 1], in1=ba[:, col:col + 64],
                        op0=ALU.mult, op1=ALU.add), reads=[bab, decb], writes=[Sb])
                tgt = 1 if c == 0 else 0
                if c == 0:
                    K.op(dve, lambda: nc.vector.tensor_copy(out=S_bf[1][:], in_=S[:]), reads=[Sb], writes=[S_bfb[1]])
            yield
            bo, bob = G()
            for c in range(2):
                cs = slice(c * 64, (c + 1) * 64)
                for h in range(4):
                    mm(bo[cs, h * 64:(h + 1) * 64], attn_sb[cs, h * 64:(h + 1) * 64], va[cs, h * 64:(h + 1) * 64],
                       True, True, [attnb, vab], [bob])
            yield
            bi, bib = G()
            for h in (0, 2, 1, 3):
                hp, r = h // 2, (h % 2) * 64
                for c in range(2):
                    cs = slice(c * 64, (c + 1) * 64)
                    mm(bi[cs, h * 64:(h + 1) * 64], TT[r:r + 64, hp, cs], S_bf[c][r:r + 64, hp, :],
                       True, True, [TTb, S_bfb[c]], [bib])
            yield
            K.op(dve, lambda: nc.vector.tensor_copy(out=S_bf[0][:], in_=S[:]), reads=[Sb], writes=[S_bfb[0]])
            yield
            bp, bpb = G()
            for g in range(4):
                ro = (g % 2) * 64
                o_ap = bp[ro:ro + 64, (g // 2) * P:(g // 2 + 1) * P]
                if i == 0:
                    mm(o_ap, u_sb[par][:, g * 64:(g + 1) * 64], pm_cur[:, g * P:(g + 1) * P], True, True,
                       [ub[par], *CbL], [bpb])
                    mm(o_ap[:, 0:16], u_sb[par][:, g * 64:(g + 1) * 64], pm_cur0[:, g * 16:(g + 1) * 16], True, True,
                       [ub[par], *CbL], [bpb])
                else:
                    mm(o_ap, u_sb[par][:, g * 64:(g + 1) * 64], pm_cur[:, g * P:(g + 1) * P], True, False,
                       [ub[par], *CbL], [bpb])
                    mm(o_ap[:, 0:16], u_sb[1 - par][:, g * 64:(g + 1) * 64], pm_prev[:, g * 16:(g + 1) * 16], False, True,
                       [ub[1 - par], *CbL], [bpb])
            yield
            K.op(act, lambda: nc.scalar.activation(out=qpp[:], in_=bp[:, 0:256], func=AF.Copy),
                 reads=[bpb], writes=[plTb])
            yield
            for g in (0, 2, 1, 3):
                ro = (g % 2) * 64
                mm(bo[:, 256 + g * 64:256 + (g + 1) * 64], plT[ro:ro + 64, g // 2, :], pw[ro:ro + 64, (g // 2) * 64:(g // 2 + 1) * 64],
                   True, True, [plTb, Lpw], [bob])
            yield
            hgb = smb["hg"]
            K.op(act, lambda: nc.scalar.activation(out=Fx[:], in_=bo[:, 0:256], func=AF.Copy), reads=[bob], writes=[Fxb])
            yield
            K.op(dve, lambda: nc.vector.tensor_tensor(out=Fx[:], in0=Fx[:], in1=bi[:, 0:256], op=ALU.add),
                 reads=[bib], writes=[Fxb])
            yield
            K.op(dve, lambda: nc.vector.tensor_tensor(out=Fy[:], in0=Fx[:], in1=Fx[:], op=ALU.mult), reads=[Fxb], writes=[Fyb])
            yield
            K.op(dve, lambda: nc.vector.reduce_sum(out=small[:, 4:8], in_=Fy[:].rearrange("p (h d) -> p h d", d=64), axis=AX.X),
                 reads=[Fyb], writes=[hgb])
            yield
            rms_rstd(small[:, 4:8], 64.0, small[:, 8:12], hgb)
            yield
            for h in range(4):
                K.op(dve, lambda h=h: nc.vector.scalar_tensor_tensor(
                    out=XYf[1 - par][:, h * 64:(h + 1) * 64], in0=Fx[:, h * 64:(h + 1) * 64], scalar=small[:, 8 + h:9 + h],
                    in1=gate_a[:, h * 64:(h + 1) * 64], op0=ALU.mult, op1=ALU.mult), reads=[Fxb, hgb, gab], writes=[XYb[1 - par]])
            yield
            K.op(dve, lambda: nc.vector.tensor_tensor(out=XYf[1 - par][:, 256:512], in0=bo[:, 256:512], in1=gate_b[:], op=ALU.mult),
                 reads=[bob, gbb], writes=[XYb[1 - par]])


            yield

        groups = []
        for h in range(8):
            for j0 in range(0, i + 1, 4):
                groups.append([(h, j) for j in range(j0, min(j0 + 4, i + 1))])
        rdb = smb["rden"]

        def emit_S(gi):
            stt, stb = ST[gi % NST]
            ptb = PTb[gi % NPT]
            pt = PT[gi % NPT]
            for sl, (h, j) in enumerate(groups[gi]):
                hp, r = h // 2, (h % 2) * 64
                mm(stt[:, sl * P:(sl + 1) * P], KT[:, hp, j * P:(j + 1) * P], (QTa if r == 0 else QTz)[:, hp, :], True, True,
                   [KTb[j], QTb], [stb])
            ng = len(groups[gi])
            K.op(act, lambda: nc.scalar.activation(out=pt[:, 0:ng * P], in_=stt[:, 0:ng * P], func=AF.Exp, scale=0.125),
                 reads=[stb], writes=[ptb])
            gh, gj0 = groups[gi][0]
            vt, vtb = Vt[gi % NVT], Vtb[gi % NVT]
            K.op(dve, lambda: nc.vector.tensor_tensor(
                out=vt[:, 0:ng * 65].rearrange("p (a b) -> p a b", b=65), in0=V[:, gj0:gj0 + ng, gh * 65:(gh + 1) * 65],
                in1=fij[par][:, gj0:gj0 + ng, gh:gh + 1].to_broadcast([P, ng, 65]), op=ALU.mult),
                reads=[biasb[par]] + [Vb[j] for j in range(gj0, gj0 + ng)], writes=[vtb])
            for sl, (h, j) in enumerate(groups[gi]):
                if j == i:
                    K.op(pool, lambda sl=sl: nc.gpsimd.tensor_tensor(
                        out=pt[:, sl * P:(sl + 1) * P], in0=pt[:, sl * P:(sl + 1) * P], in1=fmask[:], op=ALU.mult),
                        reads=[*CbL], writes=[ptb])

        def emit_PV(gi):
            pt, ptb = PT[gi % NPT], PTb[gi % NPT]
            for sl, (h, j) in enumerate(groups[gi]):
                oa, oab = OA[h // 4]
                c0 = (h % 4) * 65
                mm(oa[:, c0:c0 + 65], pt[:, sl * P:(sl + 1) * P], Vt[gi % NVT][:, sl * 65:(sl + 1) * 65], j == 0, j == i,
                   [ptb, Vtb[gi % NVT]], [oab], inc=True)
                if j == i and h % 4 == 3:
                    hb = h - 3
                    K.op(dve, lambda oa=oa: nc.vector.reciprocal(
                        out=small[:, 44:48], in_=oa[:, 0:260].rearrange("p (h e) -> p h e", e=65)[:, :, 64]),
                        reads=[oab], writes=[rdb])
                    for hh in range(4):
                        K.op(dve, lambda hh=hh, oa=oa, hb=hb: nc.vector.scalar_tensor_tensor(
                            out=XYf[1 - par][:, 512 + (hb + hh) * 64:512 + (hb + hh + 1) * 64], in0=oa[:, hh * 65:hh * 65 + 64],
                            scalar=small[:, 44 + hh:45 + hh], in1=gate_c[:, (hb + hh) * 64:(hb + hh + 1) * 64],
                            op0=ALU.mult, op1=ALU.mult), reads=[oab, rdb, gcb], writes=[XYb[1 - par]])

        gen = ab_thread()
        steps_per_group = max(1, -(-12 // len(groups)))
        if next_front is not None:
            if next_front[2] != p:
                load_cast(preg[:], ld[next_front[2]]["preg_bc"][:, :], Lpre)
            front(*next_front)
        gstate[1] = [0, 1, 2]
        emit_S(0)
        if len(groups) > 1:
            emit_S(1)
        for gi in range(len(groups)):
            if gi + 2 < len(groups):
                emit_S(gi + 2)
            emit_PV(gi)
            for _ in range(steps_per_group):
                next(gen, None)
        for _ in gen:
            pass
        gstate[1] = [0, 1, 2, 3, 4, 5]

        tm, tmb = G()
        for ec in range(8):
            tr(bf(tm)[:, ec * P:(ec + 1) * P], XYf[1 - par][:, ec * P:(ec + 1) * P], [XYb[1 - par]], [tmb])
        K.op(act, lambda: nc.scalar.activation(out=XYf[par], in_=bf(tm)[:, 0:D], func=AF.Copy),
             reads=[tmb], writes=[XYb[par]])
        if next_front is not None:
            front_b(1 - par)
        ys = []
        for half in range(2):
            y_, yb_ = G()
            for ec in range(8):
                mm(y_[:, 0:512], XYc[par][:, ec, :], Wout[:, ec, half * 512:(half + 1) * 512], ec == 0, ec == 7,
                   [XYb[par], Wob], [yb_])
            ys.append((y_, yb_))
        if prefetch is not None:
            load_wout(prefetch)
        n2 = smb["n2"]
        for half in range(2):
            y_, yb_ = ys[half]
            K.op(act, lambda y_=y_, half=half: nc.scalar.activation(
                out=hh[:, half * 512:(half + 1) * 512], in_=y_[:, 0:512], func=AF.Square,
                accum_out=small[:, 48 + half:49 + half]), reads=[yb_], writes=[hhb, n2])
        K.op(dve, lambda: nc.vector.tensor_tensor(out=small[:, 50:51], in0=small[:, 48:49], in1=small[:, 49:50], op=ALU.add),
             writes=[n2])
        rms_rstd(small[:, 50:51], float(D), small[:, 51:52], n2)
        for q4 in range(4):
            y_, yb_ = ys[q4 // 2]
            c0 = (q4 % 2) * 256
            K.op(dve, lambda y_=y_, q4=q4, c0=c0: nc.vector.scalar_tensor_tensor(
                out=Fs[q4][:], in0=y_[:, c0:c0 + 256], scalar=small[:, 51:52], in1=postg[:, q4 * 256:(q4 + 1) * 256],
                op0=ALU.mult, op1=ALU.mult), reads=[yb_, n2, Lpost], writes=[Fb[q4]])
            K.op(pool, lambda q4=q4: nc.gpsimd.tensor_tensor(
                out=x_seq[:, i, q4 * 256:(q4 + 1) * 256], in0=x_seq[:, i, q4 * 256:(q4 + 1) * 256],
                in1=Fs[q4][:], op=ALU.add), reads=[Fb[q4]], writes=[Xb[i]])
        finish()

    if 1 in lb_modes:
        compute_lb1(list(lb_modes).index(1))

    x_preloaded = set()
    for s in range(NSEQ):
        for i in range(NB):
            if (s, i) not in x_preloaded:
                load_f32(x_seq[:, i, :], x_d[s, i * P:(i + 1) * P, :], Xb[i])
        for p in range(NL):
            first = (s == 0 and p == 0)
            load_layer(p, with_weights=first, with_preg=first)
            nxt = None
            if p + 1 < NL:
                nxt = p + 1
            elif s + 1 < NSEQ:
                nxt = 0
            K.op(pool, lambda: nc.gpsimd.memset(S[:], 0.0), writes=[Sb])
            K.op(pool, lambda: nc.gpsimd.memset(S_bf[0][:], 0.0), writes=[S_bfb[0]])
            K.op(pool, lambda: nc.gpsimd.memset(small[:, 28:36], 0.0), writes=[smb["acc"]])
            for i in range(NB):
                if i == NB - 1 and p == NL - 1 and s + 1 < NSEQ:
                    for i2 in range(NB - 1):
                        load_f32(x_seq[:, i2, :], x_d[s + 1, i2 * P:(i2 + 1) * P, :], Xb[i2])
                        x_preloaded.add((s + 1, i2))
                bg = (s == 0 and p + 1 < NL and NB >= 2)
                if bg and i == NB - 1:
                    wsc_ready[p + 1] = True
                if i + 1 < NB:
                    nf = (s, i + 1, p)
                elif nxt is not None:
                    nf = (s, 0, p + 1) if p + 1 < NL else (s + 1, 0, 0)
                else:
                    nf = None
                block(s, i, p, p == NL - 1, do_front=(first and i == 0), next_front=nf,
                      prefetch=(nxt if i == NB - 1 else None))
                if bg and i < NB - 1:
                    q = p + 1
                    per = -(-16 // (NB - 1))
                    for pc in range(i * per, min(16, (i + 1) * per)):
                        if pc < 8:
                            K.dma(K.q_pool, lambda pc=pc, q=q: nc.gpsimd.dma_start(
                                out=wsc_in[q][:, pc, :], in_=ld[q]["w_in"][pc * P:(pc + 1) * P, :]), writes=[wsc_ib[q][pc]])
                        else:
                            K.dma(K.q_pool, lambda pc=pc, q=q: nc.gpsimd.dma_start(
                                out=wsc_out[q][:, pc - 8, :], in_=ld[q]["w_out"][(pc - 8) * P:(pc - 7) * P, :]),
                                writes=[wsc_ob[q][pc - 8]])
    sp.wait_all([(l[0], l[1]) for l in K.q_sp.lanes if l[1] > 0])
    return nc


def _layer_inputs(l, lower_bounds, pre_norm_g, w_in, hgrn_norm_g, fox_f_bias, pool_w, pool_scale, w_out, post_norm_g):
    f = np.float32
    bc = lambda v: np.ascontiguousarray(np.broadcast_to(np.asarray(v, f)[None, :], (P, v.shape[0])))
    pwl = np.zeros((P, 128), f)
    for g in range(4):
        pwl[(g % 2) * 64:(g % 2) * 64 + 64, (g // 2) * 64:(g // 2 + 1) * 64] = pool_w[l, g]
    return {
        "w_in": np.ascontiguousarray(w_in[l], f), "w_out": np.ascontiguousarray(w_out[l], f),
        "preg_bc": bc(pre_norm_g[l]), "postg_bc": bc(post_norm_g[l]),
        "hgrng_bc": bc(hgrn_norm_g[l]), "pscale_bc": bc(pool_scale[l]),
        "lbraw_bc": bc(np.concatenate([lower_bounds[0], lower_bounds[1]])),
        "fbias_bc": bc(fox_f_bias[l]), "pw": pwl,
    }


FUSED = True
_cache = {}


def run(x, params, T, NSEQ, ncores, layer_groups):
    consts = _consts()
    cur = np.ascontiguousarray(x, np.float32)
    for grp in layer_groups:
        key = (T, NSEQ, tuple(grp))
        if key not in _cache:
            _cache[key] = build_program(T, NSEQ, [0 if l == 0 else 1 for l in grp])
        nc = _cache[key]
        base = {"c_" + k: v for k, v in consts.items()}
        for p, l in enumerate(grp):
            for k, v in _layer_inputs(l, **params).items():
                base["l%d_%s" % (p, k)] = v
        in_maps = []
        for c in range(ncores):
            m = dict(base)
            m["x"] = np.ascontiguousarray(cur[c * NSEQ:(c + 1) * NSEQ])
            in_maps.append(m)
        res = run_bass_kernel_spmd(nc, in_maps, core_ids=list(range(ncores)))
        cur = np.concatenate([np.asarray(r["out"], np.float32) for r in res.results], axis=0)
    return cur


def kernel(x, lower_bounds, pre_norm_g, w_in, hgrn_norm_g, fox_f_bias, pool_w, pool_scale, w_out, post_norm_g):
    params = dict(lower_bounds=np.asarray(lower_bounds), pre_norm_g=np.asarray(pre_norm_g), w_in=np.asarray(w_in),
                  hgrn_norm_g=np.asarray(hgrn_norm_g), fox_f_bias=np.asarray(fox_f_bias), pool_w=np.asarray(pool_w),
                  pool_scale=np.asarray(pool_scale), w_out=np.asarray(w_out), post_norm_g=np.asarray(post_norm_g))
    x = np.asarray(x)
    B, T, _ = x.shape
    groups = [[0, 1]] if FUSED else [[0], [1]]
    return run(x, params, T, B // 8, 8, groups).astype(np.float32)
```

```python
import numpy as np
import concourse.bass as bass
import concourse.mybir as mybir
from concourse.bass_utils import run_bass_kernel_spmd

F32 = mybir.dt.float32
BF16 = mybir.dt.bfloat16
AF = mybir.ActivationFunctionType
ALU = mybir.AluOpType
AX = mybir.AxisListType

D = 1024
INW = 3592
EPS = 1e-6
TINY = 1e-30
P = 128
POOL_WINDOWS = (2, 4, 8, 16)
EPOCH = 24000


CLOCKS = {}


class Buf:
    __slots__ = ("w", "r", "excl")

    def __init__(self, excl=False):
        self.w = None
        self.r = {}
        self.excl = excl


class Eng:
    def __init__(self, nc, eng, name):
        self.nc, self.eng, self.name = nc, eng, name
        self.sem = nc.alloc_semaphore(name + "_s0")
        self.own = {id(self.sem)}
        self.n = 0
        self.ep = 0
        self.seen = {}
        self.pending = 0

    def wait_all(self, deps):
        deps = [d for d in deps if self.seen.get(id(d[0]), 0) < d[1]]
        if len(deps) > 1:
            keep = []
            for d in deps:
                k = id(d[0])
                implied = False
                for d2 in deps:
                    if d2 is not d:
                        c2 = CLOCKS.get((id(d2[0]), d2[1]))
                        if c2 is not None and c2.get(k, 0) >= d[1]:
                            implied = True
                            break
                if not implied:
                    keep.append(d)
            deps = keep
        for sem, val in deps:
            key = id(sem)
            if self.seen.get(key, 0) < val:
                self.eng.wait_ge(sem, val)
                self.seen[key] = val
            c = CLOCKS.get((key, val))
            if c is not None:
                seen = self.seen
                for k2, v2 in c.items():
                    if seen.get(k2, 0) < v2:
                        seen[k2] = v2

    def snapshot(self, tok):
        c = dict(self.seen)
        c[id(tok[0])] = max(c.get(id(tok[0]), 0), tok[1])
        CLOCKS[(id(tok[0]), tok[1])] = c

    def issue(self, ins):
        if self.n >= EPOCH and self.pending == 0:
            self.ep += 1
            self.sem = self.nc.alloc_semaphore("%s_s%d" % (self.name, self.ep))
            self.own.add(id(self.sem))
            self.n = 0
        self.n += 1
        ins.then_inc(self.sem, 1)
        self.pending = 0
        return (self.sem, self.n)

    def issue_noinc(self, ins):
        if self.n >= EPOCH:
            pass
        self.pending += 1
        return (self.sem, self.n + 1)


class DmaQ:
    def __init__(self, nc, E, name, nlanes):
        self.E = E
        self.lanes = [[nc.alloc_semaphore("%s_l%d" % (name, i)), 0] for i in range(nlanes)]
        self.k = 0

    def issue(self, fn, deps):
        lane = self.lanes[self.k]
        self.k = (self.k + 1) % len(self.lanes)
        d = set(deps)
        if lane[1] > 0:
            d.add((lane[0], lane[1]))
        self.E.wait_all(d)
        lane[1] += 16
        fn().then_inc(lane[0], 16)
        tok = (lane[0], lane[1])
        self.E.snapshot(tok)
        return tok


class Ctx:
    def __init__(self, nc):
        self.nc = nc
        self.pe = Eng(nc, nc.tensor, "pe")
        self.act = Eng(nc, nc.scalar, "act")
        self.dve = Eng(nc, nc.vector, "dve")
        self.pool = Eng(nc, nc.gpsimd, "pool")
        self.sp = Eng(nc, nc.sync, "sp")
        self.q_sp = DmaQ(nc, self.sp, "qsp", 8)
        self.q_pool = DmaQ(nc, self.pool, "qpl", 16)

    def _deps(self, reads, writes, E=None):
        deps = set()
        own = E.own if E is not None else ()
        is_pe = E is self.pe
        for b in reads:
            if b.w is not None and not (is_pe and id(b.w[0]) in own):
                deps.add(b.w)
            if b.excl:
                for tok in b.r.values():
                    if id(tok[0]) not in own:
                        deps.add(tok)
        for b in writes:
            if b.w is not None and not (is_pe and id(b.w[0]) in own):
                deps.add(b.w)
            for tok in b.r.values():
                if not (is_pe and id(tok[0]) in own):
                    deps.add(tok)
        return deps

    @staticmethod
    def _commit(tok, reads, writes):
        for b in reads:
            b.r[id(tok[0])] = tok
        for b in writes:
            b.w = tok
            b.r = {}

    def op(self, E, fn, reads=(), writes=(), inc=True, mode=None):
        if mode is not None and mode != getattr(self, "pe_mode", None):
            if E.n > 0:
                assert E.pending == 0
                E.eng.wait_ge(E.sem, E.n)
                E.seen[id(E.sem)] = E.n
            self.pe_mode = mode
        E.wait_all(self._deps(reads, writes, E))
        tok = E.issue(fn()) if inc else E.issue_noinc(fn())
        if inc:
            E.snapshot(tok)
        elif (id(tok[0]), tok[1]) not in CLOCKS:
            E.snapshot(tok)
        self._commit(tok, reads, writes)
        return tok

    def dma(self, Q, fn, reads=(), writes=()):
        tok = Q.issue(fn, self._deps(reads, writes))
        self._commit(tok, reads, writes)
        return tok


def _consts():
    s = np.arange(P)[:, None]
    t = np.arange(P)[None, :]
    same = (s // 64) == (t // 64)
    c = {}
    c["ident"] = np.eye(P, dtype=np.float32)
    c["ucum"] = (same & (s <= t)).astype(np.float32)
    c["lstr"] = (same & (s > t)).astype(np.float32)
    c["ufull"] = (s <= t).astype(np.float32)
    c["ones"] = np.ones((P, P), np.float32)
    ind = np.zeros((P, 2), np.float32)
    ind[:64, 0] = 1
    ind[64:, 1] = 1
    c["ind"] = ind
    c["fmask"] = (s <= t).astype(np.float32)
    hm = np.zeros((P, 256), np.float32)
    for h in range(4):
        hm[:, h * 64:(h + 1) * 64] = ((np.arange(P)[:, None] % 64) <= np.arange(64)[None, :])
    c["hmask"] = hm
    pm_cur = np.zeros((P, 4, P), np.float32)
    pm_prev = np.zeros((P, 4, 16), np.float32)
    pm_cur0 = np.zeros((P, 4, 16), np.float32)
    for g, w in enumerate(POOL_WINDOWS):
        for tt in range(P):
            for ss in range(tt - w + 1, tt + 1):
                if ss >= 0:
                    pm_cur[ss, g, tt] += 1.0 / w
                else:
                    if tt < 16:
                        pm_prev[ss + P, g, tt] += 1.0 / w
            pm_cur[tt, g, tt] -= 1.0
        for tt in range(16):
            cnt = min(tt + 1, w)
            for ss in range(max(0, tt - w + 1), tt + 1):
                pm_cur0[ss, g, tt] += 1.0 / cnt
            pm_cur0[tt, g, tt] -= 1.0
    c["pm_cur"] = pm_cur.reshape(P, 4 * P)
    c["pm_prev"] = pm_prev.reshape(P, 64)
    c["pm_cur0"] = pm_cur0.reshape(P, 64)
    return c


CONST_SHAPES = {"ident": (P, P), "ucum": (P, P), "lstr": (P, P), "ufull": (P, P), "ones": (P, P),
                "ind": (P, 2), "fmask": (P, P), "hmask": (P, 256), "pm_cur": (P, 512),
                "pm_prev": (P, 64), "pm_cur0": (P, 64)}
LAYER_SHAPES = {"w_in": (D, INW), "w_out": (D, D), "preg_bc": (P, D), "postg_bc": (P, D),
                "hgrng_bc": (P, 256), "pscale_bc": (P, 256), "lbraw_bc": (P, 512),
                "fbias_bc": (P, 8), "pw": (P, 128)}


def build_program(T, NSEQ, lb_modes, debug_out=False):
    NL = len(lb_modes)
    NB = T // P
    nc = bass.Bass("TRN2", target_bir_lowering=False)
    CLOCKS.clear()
    K = Ctx(nc)
    pe, act, dve, pool, sp = K.pe, K.act, K.dve, K.pool, K.sp

    x_d = nc.dram_tensor("x", [NSEQ, T, D], F32, kind="ExternalInput").ap()
    out_d = nc.dram_tensor("out", [NSEQ, T, D], F32, kind="ExternalOutput").ap()
    cd = {k: nc.dram_tensor("c_" + k, list(v), F32, kind="ExternalInput").ap() for k, v in CONST_SHAPES.items()}
    ld = [{k: nc.dram_tensor("l%d_%s" % (p, k), list(v), F32, kind="ExternalInput").ap()
           for k, v in LAYER_SHAPES.items()} for p in range(NL)]

    wsc_in = [nc.dram_tensor("wsc_in%d" % p, [P, 8, INW], BF16, kind="Internal").ap() for p in range(NL)]
    wsc_out = [nc.dram_tensor("wsc_out%d" % p, [P, 8, D], BF16, kind="Internal").ap() for p in range(NL)]
    wsc_ib = [[Buf() for _ in range(8)] for _ in range(NL)]
    wsc_ob = [[Buf() for _ in range(8)] for _ in range(NL)]
    wsc_ready = [False] * NL

    def sb(name, shape, dt):
        return nc.alloc_sbuf_tensor(name, list(shape), dt)

    x_seq = sb("x_seq", [P, NB, D], F32)
    Xb = [Buf() for _ in range(NB)]
    Win = sb("Win", [P, 8, INW], BF16)
    Wout = sb("Wout", [P, 8, D], BF16)
    Wib = Buf()
    Wob = Buf()
    KT = sb("KT", [P, 4, T], BF16)
    KTb = [Buf() for _ in range(NB)]
    V = sb("V", [P, NB, 8 * 65], BF16)
    Vb = [Buf() for _ in range(NB)]
    Vones = Buf()
    ident = sb("ident", [P, P], BF16)
    ucum = sb("ucum", [P, P], F32)
    lstr = sb("lstr", [P, P], F32)
    ufull = sb("ufull", [P, P], F32)
    ones = sb("ones", [P, P], F32)
    ind = sb("ind", [P, 2], F32)
    fmask = sb("fmask", [P, P], BF16)
    hmask = sb("hmask", [P, 256], BF16)
    pm_cur = sb("pm_cur", [P, 512], BF16)
    pm_prev = sb("pm_prev", [P, 64], BF16)
    pm_cur0 = sb("pm_cur0", [P, 64], BF16)
    CbL = [Buf() for _ in range(len(CONST_SHAPES))]
    preg = sb("preg", [P, D], BF16)
    postg = sb("postg", [P, D], BF16)
    hgrng = sb("hgrng", [P, 256], BF16)
    pscale = sb("pscale", [P, 256], BF16)
    lb_bc = sb("lb_bc", [P, 256], F32)
    oml_bc = sb("oml_bc", [P, 256], F32)
    lbraw = None
    fbias = sb("fbias", [P, 8], F32)
    pw = sb("pw", [P, 128], BF16)
    Lpre, Lpost, Lpw, Lhg, Lps, Lfb, Llb = (Buf() for _ in range(7))
    hm = sb("hm", [P, D], BF16)
    hmb = Buf()
    hh = sb("hh", [P, D], BF16)
    hhb = Buf()
    hT = sb("hT", [P, 8, P], BF16)
    hTb = Buf()
    XYf = [hT[:].rearrange("p a b -> p (a b)"), hm[:]]
    XYc = [hT[:], hm[:].rearrange("p (a b) -> p a b", b=P)]
    XYb = [hTb, hmb]
    NF = 5
    Fs = [sb("F%d" % i, [P, 256], F32) for i in range(NF)]
    Fb = [Buf() for _ in range(NF)]
    ebig = sb("ebig", [P, 256], F32)
    ebb = Buf()
    gate_a = sb("gate_a", [P, 256], BF16)
    gate_b = sb("gate_b", [P, 256], BF16)
    gate_c = sb("gate_c", [P, 512], BF16)
    gab, gbb, gcb = Buf(), Buf(), Buf()
    qp = sb("qp", [P, 256], BF16)
    qpp = sb("qpp", [P, 256], BF16)
    kp = sb("kp", [P, 256], BF16)
    va = sb("va", [P, 256], BF16)
    attn_sb = qp
    qpb, qppb, kpb, vab = Buf(), Buf(), Buf(), Buf()
    attnb = qpb
    TT = sb("TT", [P, 6, P], BF16)
    TTb = Buf()
    S = sb("S", [P, 2, 64], F32)
    Sb = Buf()
    S_bf = [sb("S_bf%d" % i, [P, 2, 64], BF16) for i in range(2)]
    S_bfb = [Buf(), Buf()]
    dec = sb("dec", [P, 4], F32)
    decb = Buf()
    u_sb = [sb("u_sb%d" % i, [P, 256], BF16) for i in range(2)]
    ub = [Buf(), Buf()]
    plT = qpp[:].rearrange("p (a b) -> p a b", b=P)
    plTb = qppb
    q_sb = Fs[3][:].bitcast(BF16)
    k_sb = Fs[4][:].bitcast(BF16)
    qsb_b, ksb_b = Fb[3], Fb[4]
    QTa = sb("QTa", [P, 4, P], BF16)
    QTz = sb("QTz", [P, 4, P], BF16)
    QTb = Buf()
    NPT = 3
    NVT = 3
    Vt = [sb("Vt%d" % i, [P, 4 * 65], BF16) for i in range(NVT)]
    Vtb = [Buf() for _ in range(NVT)]
    PT = [sb("PT%d" % i, [P, 512], BF16) for i in range(NPT)]
    PTb = [Buf() for _ in range(NPT)]
    fij = [sb("fij%d" % i, [P, NB, 8], F32) for i in range(2)]
    biasb = [Buf(), Buf()]
    totall = sb("totall", [P, NB, 8], F32)
    small = sb("small", [P, 64], F32)
    smb = {k: Buf() for k in ("n1", "hg", "fc", "acc", "tot", "rden", "n2", "w")}

    banks = [nc.alloc_psum_tensor("bank%d" % i, [P, 512], F32) for i in range(8)]
    bankb = [Buf(excl=True) for _ in range(8)]
    gstate = [0, [0, 1, 2, 3, 4, 5]]

    def G():
        lst = gstate[1]
        i = lst[gstate[0] % len(lst)]
        gstate[0] += 1
        return banks[i], bankb[i]

    NST = 3
    ST = [(banks[3], bankb[3]), (banks[4], bankb[4]), (banks[5], bankb[5])]
    OA = [(banks[6], bankb[6]), (banks[7], bankb[7])]

    def bf(t):
        return t[:].bitcast(BF16)

    def load_cast(dst_ap, src_ap, buf):
        K.dma(K.q_pool, lambda: nc.gpsimd.dma_start(out=dst_ap, in_=src_ap), writes=[buf])

    def load_f32(dst_ap, src_ap, buf):
        K.dma(K.q_sp, lambda: nc.sync.dma_start(out=dst_ap, in_=src_ap), writes=[buf])

    _cseen = []
    for name, t_, cast in (("ident", ident, 1), ("ucum", ucum, 0), ("lstr", lstr, 0), ("ufull", ufull, 0),
                           ("ones", ones, 0), ("ind", ind, 0), ("fmask", fmask, 1), ("hmask", hmask, 1),
                           ("pm_cur", pm_cur, 1), ("pm_prev", pm_prev, 1), ("pm_cur0", pm_cur0, 1)):
        (load_cast if cast else load_f32)(t_[:], cd[name][:, :], CbL[len(_cseen)])
        _cseen.append(name)
    Vv = V[:].rearrange("p b (h e) -> p b h e", e=65)
    K.op(pool, lambda: nc.gpsimd.memset(QTa[:], 0.0), writes=[QTb])
    K.op(pool, lambda: nc.gpsimd.memset(QTz[:], 0.0), writes=[QTb])

    def load_win(p):
        if wsc_ready[p]:
            for dc in range(8):
                K.dma(K.q_sp, lambda dc=dc: nc.sync.dma_start(out=Win[:, dc, :], in_=wsc_in[p][:, dc, :]),
                      reads=[wsc_ib[p][dc]], writes=[Wib])
        else:
            for dc in range(8):
                load_cast(Win[:, dc, :], ld[p]["w_in"][dc * P:(dc + 1) * P, :], Wib)
            for dc in range(8):
                K.dma(K.q_sp, lambda dc=dc: nc.sync.dma_start(out=wsc_in[p][:, dc, :], in_=Win[:, dc, :]),
                      reads=[Wib], writes=[wsc_ib[p][dc]])

    def load_wout(p):
        if wsc_ready[p]:
            for ec in range(8):
                K.dma(K.q_sp, lambda ec=ec: nc.sync.dma_start(out=Wout[:, ec, :], in_=wsc_out[p][:, ec, :]),
                      reads=[wsc_ob[p][ec]], writes=[Wob])
        else:
            for ec in range(8):
                load_cast(Wout[:, ec, :], ld[p]["w_out"][ec * P:(ec + 1) * P, :], Wob)
            for ec in range(8):
                K.dma(K.q_sp, lambda ec=ec: nc.sync.dma_start(out=wsc_out[p][:, ec, :], in_=Wout[:, ec, :]),
                      reads=[Wob], writes=[wsc_ob[p][ec]])
            wsc_ready[p] = True

    def load_layer(p, with_weights=True, with_preg=True):
        L = ld[p]
        if with_weights:
            load_win(p)
            load_wout(p)
        if with_preg:
            load_cast(preg[:], L["preg_bc"][:, :], Lpre)
        load_cast(postg[:], L["postg_bc"][:, :], Lpost)
        load_cast(pw[:], L["pw"][:, :], Lpw)
        load_cast(hgrng[:], L["hgrng_bc"][:, :], Lhg)
        load_cast(pscale[:], L["pscale_bc"][:, :], Lps)
        load_f32(fbias[:], L["fbias_bc"][:, :], Lfb)

    def compute_lb1(p):
        L = ld[p]
        load_f32(Fs[1][:], L["lbraw_bc"][:, 0:256], Fb[1])
        load_f32(Fs[2][:], L["lbraw_bc"][:, 256:512], Fb[2])
        K.op(pool, lambda: nc.gpsimd.tensor_tensor(out=Fs[0][:], in0=Fs[1][:], in1=Fs[2][:],
                                                   op=ALU.subtract), reads=[Fb[1], Fb[2]], writes=[Fb[0]])
        K.op(act, lambda: nc.scalar.activation(out=Fs[0][:], in_=Fs[0][:], func=AF.Exp),
             reads=[], writes=[Fb[0]])
        K.op(pool, lambda: nc.gpsimd.tensor_scalar(out=Fs[0][:], in0=Fs[0][:], scalar1=1.0, scalar2=1.0,
                                                   op0=ALU.mult, op1=ALU.add), writes=[Fb[0]])
        K.op(dve, lambda: nc.vector.reciprocal(out=lb_bc[:], in_=Fs[0][:]), reads=[Fb[0]], writes=[Llb])
        K.op(pool, lambda: nc.gpsimd.tensor_scalar(out=oml_bc[:], in0=lb_bc[:], scalar1=-1.0, scalar2=1.0,
                                                   op0=ALU.mult, op1=ALU.add), writes=[Llb])

    def silu_gate(ps_ap, width, out_ap, outb, mul_ap=None, psb=None):
        e = ebig[:, 0:width]
        K.op(act, lambda: nc.scalar.activation(out=e, in_=ps_ap, func=AF.Exp, scale=-1.0),
             reads=[psb], writes=[ebb])
        K.op(act, lambda: nc.scalar.activation(out=e, in_=e, func=AF.Ln, bias=1.0), writes=[ebb])
        K.op(act, lambda: nc.scalar.activation(out=e, in_=e, func=AF.Exp, scale=-1.0), writes=[ebb])
        if mul_ap is None:
            K.op(dve, lambda: nc.vector.tensor_tensor(out=out_ap, in0=ps_ap, in1=e, op=ALU.mult),
                 reads=[psb, ebb], writes=[outb])
        else:
            K.op(dve, lambda: nc.vector.tensor_tensor(out=e, in0=ps_ap, in1=e, op=ALU.mult),
                 reads=[psb], writes=[ebb])
            K.op(pool, lambda: nc.gpsimd.tensor_tensor(out=out_ap, in0=e, in1=mul_ap, op=ALU.mult),
                 reads=[ebb, Lhg, Lps], writes=[outb])

    def rms_rstd(ss_ap, n, out_ap, b):
        K.op(dve, lambda: nc.vector.tensor_scalar(out=out_ap, in0=ss_ap, scalar1=1.0 / n, scalar2=EPS,
                                                  op0=ALU.mult, op1=ALU.add), writes=[b])
        K.op(act, lambda: nc.scalar.activation(out=out_ap, in_=out_ap, func=AF.Ln), writes=[b])
        K.op(act, lambda: nc.scalar.activation(out=out_ap, in_=out_ap, func=AF.Exp, scale=-0.5), writes=[b])

    def _cls(n):
        return 32 if n <= 32 else (64 if n <= 64 else 128)

    def mm(out, lhsT, rhs, start, stop, reads, writes, inc=None):
        kc = _cls(lhsT.shape[0])
        mode = ("mm", str(lhsT.dtype), kc, _cls(lhsT.shape[-1]), lhsT.base_partition() if kc < 128 else 0)
        return K.op(pe, lambda: nc.tensor.matmul(out, lhsT, rhs, start=start, stop=stop), reads=reads, writes=writes,
                    inc=(bool(stop) or kc < 128) if inc is None else inc, mode=mode)

    def tr(out, in_, reads, writes):
        return K.op(pe, lambda: nc.tensor.transpose(out, in_, ident[:]), reads=list(reads) + [*CbL], writes=writes,
                    mode=("tr",))

    def front(s, i, p):
        xb = x_seq[:, i, :]
        n1 = smb["n1"]
        K.op(act, lambda: nc.scalar.activation(out=hh[:], in_=xb, func=AF.Square, accum_out=small[:, 0:1]),
             reads=[Xb[i]], writes=[hhb, n1])
        rms_rstd(small[:, 0:1], float(D), small[:, 2:3], n1)
        K.op(dve, lambda: nc.vector.scalar_tensor_tensor(out=hh[:], in0=xb, scalar=small[:, 2:3], in1=preg[:],
                                                         op0=ALU.mult, op1=ALU.mult),
             reads=[Xb[i], n1, Lpre], writes=[hhb])

    def front_b(tp):
        tb, tbb = G()
        for dc in range(8):
            tr(bf(tb)[:, dc * P:(dc + 1) * P], hh[:, dc * P:(dc + 1) * P], [hhb], [tbb])
        K.op(act, lambda: nc.scalar.activation(out=XYf[tp], in_=bf(tb)[:, 0:D], func=AF.Copy),
             reads=[tbb], writes=[XYb[tp]])


    def block(s, i, p, last_layer, do_front=True, next_front=None, prefetch=None):
        xb = x_seq[:, i, :]
        par = i % 2

        def finish():
            if last_layer:
                K.dma(K.q_sp, lambda: nc.sync.dma_start(out=out_d[s, i * P:(i + 1) * P, :], in_=x_seq[:, i, :]),
                      reads=[Xb[i]])
        if do_front:
            front(s, i, p)
            front_b(i % 2)
        def proj(col0, width):
            b_, bb_ = G()
            for dc in range(8):
                mm(b_[:, 0:width], XYc[par][:, dc, :], Win[:, dc, col0:col0 + width], dc == 0, dc == 7,
                   [XYb[par], Wib], [bb_])
            return b_, bb_

        b7, b7b = proj(3584, 8)
        fcb = smb["fc"]
        K.op(dve, lambda: nc.vector.tensor_tensor(out=small[:, 12:20], in0=b7[:, 0:8], in1=fbias[:], op=ALU.add),
             reads=[b7b, Lfb], writes=[fcb])
        K.op(act, lambda: nc.scalar.activation(out=small[:, 12:20], in_=small[:, 12:20], func=AF.Exp, scale=-1.0),
             writes=[fcb])
        K.op(pool, lambda: nc.gpsimd.tensor_scalar(out=small[:, 12:20], in0=small[:, 12:20], scalar1=1.0,
                                                   scalar2=1.0, op0=ALU.mult, op1=ALU.add), writes=[fcb])
        K.op(act, lambda: nc.scalar.activation(out=small[:, 20:28], in_=small[:, 12:20], func=AF.Ln), writes=[fcb])
        sp_ap = small[:, 20:28]
        acc_ap = small[:, 28:36]
        accb = smb["acc"]
        b3, b3b = proj(1536, 512)
        K.op(act, lambda: nc.scalar.activation(out=q_sb, in_=b3[:, 0:512], func=AF.Copy), reads=[b3b], writes=[qsb_b])
        b4, b4b = proj(2048, 512)
        K.op(dve, lambda: nc.vector.tensor_copy(out=k_sb, in_=b4[:, 0:512]), reads=[b4b], writes=[ksb_b])
        tq, tqb = G()
        for hp in range(4):
            tr(bf(tq)[:, hp * P:(hp + 1) * P], q_sb[:, hp * P:(hp + 1) * P], [qsb_b], [tqb])
        for hp in range(4):
            tr(bf(tq)[:, (4 + hp) * P:(5 + hp) * P], k_sb[:, hp * P:(hp + 1) * P], [ksb_b], [tqb])
        K.op(act, lambda: nc.scalar.activation(out=QTa[0:64].rearrange("p a b -> p (a b)"), in_=bf(tq)[0:64, 0:512], func=AF.Copy),
             reads=[tqb], writes=[QTb])
        K.op(act, lambda: nc.scalar.activation(out=QTz[64:128].rearrange("p a b -> p (a b)"), in_=bf(tq)[64:128, 0:512], func=AF.Copy),
             reads=[tqb], writes=[QTb])
        K.op(act, lambda: nc.scalar.activation(out=KT[:, :, i * P:(i + 1) * P],
                                               in_=bf(tq)[:, 512:1024].rearrange("p (a b) -> p a b", b=P),
                                               func=AF.Copy), reads=[tqb], writes=[KTb[i]])

        bc_, bcb = G()
        mm(bc_[:, 0:8], ufull[:], sp_ap, True, False, [*CbL, fcb], [bcb])
        mm(bc_[:, 0:8], ones[:], acc_ap, False, True, [*CbL, accb], [bcb])
        K.op(pool, lambda: nc.gpsimd.tensor_tensor(out=acc_ap, in0=acc_ap, in1=sp_ap, op=ALU.add),
             reads=[fcb], writes=[accb])
        mm(bc_[:, 8:16], ones[:], acc_ap, True, True, [*CbL, accb], [bcb])
        totb = smb["tot"]
        wb = smb["w"]
        K.op(dve, lambda: nc.vector.tensor_copy(out=totall[:, i, :], in_=bc_[:, 8:16]), reads=[bcb], writes=[totb])
        K.op(dve, lambda: nc.vector.tensor_tensor(out=small[:, 52:60], in0=bc_[:, 0:8], in1=totall[:, i, :],
                                                  op=ALU.subtract), reads=[bcb, totb], writes=[wb])
        K.op(act, lambda: nc.scalar.activation(out=small[:, 52:60], in_=small[:, 52:60], func=AF.Exp), writes=[wb])
        K.op(pool, lambda: nc.gpsimd.tensor_tensor(
            out=fij[par][:, 0:i + 1, :], in0=totall[:, 0:i + 1, :],
            in1=totall[:, i:i + 1, :].to_broadcast([P, i + 1, 8]), op=ALU.subtract),
            reads=[totb], writes=[biasb[par]])
        K.op(act, lambda: nc.scalar.activation(out=fij[par][:, 0:i + 1, :], in_=fij[par][:, 0:i + 1, :], func=AF.Exp),
             writes=[biasb[par]])
        b5, b5b = proj(2560, 512)
        K.op(dve, lambda: nc.vector.tensor_tensor(
            out=Vv[:, i, :, 0:64], in0=b5[:, 0:512].rearrange("p (h d) -> p h d", d=64),
            in1=small[:, 52:60].unsqueeze(2).to_broadcast([P, 8, 64]), op=ALU.mult),
            reads=[b5b, smb["w"]], writes=[Vb[i]])
        K.op(pool, lambda: nc.gpsimd.tensor_copy(out=Vv[:, i, :, 64], in_=small[:, 52:60]),
             reads=[smb["w"]], writes=[Vb[i]])
        b6, b6b = proj(3072, 512)
        silu_gate(b6[:, 0:256], 256, gate_c[:, 0:256], gcb, psb=b6b)
        silu_gate(b6[:, 256:512], 256, gate_c[:, 256:512], gcb, psb=b6b)
        gstate[1] = [0, 1, 2]
        b0, b0b = proj(0, 512)
        Fq, Fqb = Fs[0], Fb[0]
        Fk, Fkb = Fs[1], Fb[1]
        Fl, Flb = Fs[2], Fb[2]
        Fx, Fxb = Fs[3], Fb[3]
        Fy, Fyb = Fs[4], Fb[4]
        b1, b1b = proj(512, 512)
        b2, b2b = proj(1024, 512)
        if prefetch is not None:
            load_win(prefetch)
        def ab_thread():
            K.op(act, lambda: nc.scalar.activation(out=Fx[:], in_=b0[:, 0:256], func=AF.Exp, scale=-1.0),
                 reads=[b0b], writes=[Fxb])
            yield
            K.op(act, lambda: nc.scalar.activation(out=Fx[:], in_=Fx[:], func=AF.Ln, bias=1.0), writes=[Fxb])
            yield
            K.op(act, lambda: nc.scalar.activation(out=Fx[:], in_=Fx[:], func=AF.Exp, scale=-1.0), writes=[Fxb])
            yield
            K.op(dve, lambda: nc.vector.tensor_tensor(out=Fq[:], in0=b0[:, 0:256], in1=Fx[:], op=ALU.mult),
                 reads=[b0b, Fxb], writes=[Fqb])
            yield
            K.op(act, lambda: nc.scalar.activation(out=Fy[:], in_=b0[:, 256:512], func=AF.Exp, scale=-1.0),
                 reads=[b0b], writes=[Fyb])
            yield
            K.op(act, lambda: nc.scalar.activation(out=Fy[:], in_=Fy[:], func=AF.Ln, bias=1.0), writes=[Fyb])
            yield
            K.op(act, lambda: nc.scalar.activation(out=Fy[:], in_=Fy[:], func=AF.Exp, scale=-1.0), writes=[Fyb])
            yield
            if lb_modes[p] == 0:
                K.op(pool, lambda: nc.gpsimd.tensor_scalar(out=Fk[:], in0=Fy[:], scalar1=-1.0, scalar2=1.0,
                                                           op0=ALU.mult, op1=ALU.add), reads=[Fyb], writes=[Fkb])
                yield
                K.op(dve, lambda: nc.vector.tensor_scalar(out=Fy[:], in0=Fy[:], scalar1=TINY, scalar2=None,
                                                          op0=ALU.max), writes=[Fyb])
                yield
            else:
                K.op(pool, lambda: nc.gpsimd.tensor_tensor(out=Fy[:], in0=Fy[:], in1=oml_bc[:], op=ALU.mult),
                     reads=[Llb], writes=[Fyb])
                yield
                K.op(pool, lambda: nc.gpsimd.tensor_tensor(out=Fk[:], in0=oml_bc[:], in1=Fy[:], op=ALU.subtract),
                     reads=[Llb, Fyb], writes=[Fkb])
                yield
                K.op(dve, lambda: nc.vector.scalar_tensor_tensor(out=Fy[:], in0=Fy[:], scalar=TINY, in1=lb_bc[:],
                                                                 op0=ALU.max, op1=ALU.add), reads=[Llb], writes=[Fyb])
                yield
            K.op(act, lambda: nc.scalar.activation(out=Fl[:], in_=Fy[:], func=AF.Ln), reads=[Fyb], writes=[Flb])

            yield
            K.op(act, lambda: nc.scalar.activation(out=va[:], in_=b1[:, 0:256], func=AF.Copy), reads=[b1b], writes=[vab])
            yield
            silu_gate(b1[:, 256:512], 256, gate_a[:], gab, mul_ap=hgrng[:], psb=b1b)
            yield
            K.op(act, lambda: nc.scalar.activation(out=u_sb[par][:], in_=b2[:, 0:256], func=AF.Copy),
                 reads=[b2b], writes=[ub[par]])
            yield
            silu_gate(b2[:, 256:512], 256, gate_b[:], gbb, mul_ap=pscale[:], psb=b2b)

            yield
            be, beb = G()
            mm(be[:, 0:256], ucum[:], Fl[:], True, True, [*CbL, Flb], [beb])
            yield
            mm(be[:, 256:512], lstr[:], Fl[:], True, True, [*CbL, Flb], [beb])
            yield
            bd, bdb = G()
            for hp in range(2):
                mm(bd[:, hp * 2:hp * 2 + 2], Fl[:, hp * P:(hp + 1) * P], ind[:], True, True, [*CbL, Flb], [bdb])
            yield
            K.op(act, lambda: nc.scalar.activation(out=dec[:], in_=bd[:, 0:4], func=AF.Exp), reads=[bdb], writes=[decb])
            yield
            K.op(act, lambda: nc.scalar.activation(out=Fx[:], in_=be[:, 0:256], func=AF.Exp), reads=[beb], writes=[Fxb])
            yield
            K.op(dve, lambda: nc.vector.tensor_tensor(out=qp[:], in0=Fq[:], in1=Fx[:], op=ALU.mult),
                 reads=[Fqb, Fxb], writes=[qpb])
            yield
            K.op(act, lambda: nc.scalar.activation(out=Fy[:], in_=be[:, 256:512], func=AF.Exp, scale=-1.0),
                 reads=[beb], writes=[Fyb])
            yield
            K.op(dve, lambda: nc.vector.tensor_tensor(out=qpp[:], in0=Fq[:], in1=Fy[:], op=ALU.mult),
                 reads=[Fqb, Fyb], writes=[qppb])
            yield
            K.op(act, lambda: nc.scalar.activation(out=Fx[:], in_=be[:, 256:512], func=AF.Exp), reads=[beb], writes=[Fxb])
            yield
            K.op(dve, lambda: nc.vector.tensor_tensor(out=kp[:], in0=Fk[:], in1=Fx[:], op=ALU.mult),
                 reads=[Fkb, Fxb], writes=[kpb])
            yield
            tt_, ttb = G()
            for n_, (src, srcb) in enumerate(((qp, qpb), (qpp, qppb), (kp, kpb))):
                for hp in range(2):
                    tr(bf(tt_)[:, (2 * n_ + hp) * P:(2 * n_ + hp + 1) * P], src[:, hp * P:(hp + 1) * P], [srcb], [ttb])
            yield
            K.op(act, lambda: nc.scalar.activation(out=TT[:].rearrange("p a b -> p (a b)"), in_=bf(tt_)[:, 0:768], func=AF.Copy),
                 reads=[ttb], writes=[TTb])
            yield
            ba, bab = G()
            for h in (0, 2, 1, 3):
                hp, r = h // 2, (h % 2) * 64
                for c in range(2):
                    cs = slice(c * 64, (c + 1) * 64)
                    mm(ba[cs, h * 64:(h + 1) * 64], TT[r:r + 64, 4 + hp, cs], TT[r:r + 64, 2 + hp, cs], True, True,
                       [TTb], [bab])
            yield
            for c in range(2):
                cs = slice(c * 64, (c + 1) * 64)
                for h in range(4):
                    hp, r = h // 2, (h % 2) * 64
                    col = 256 + (c * 2 + hp) * 64
                    mm(ba[r:r + 64, col:col + 64], kp[cs, h * 64:(h + 1) * 64], va[cs, h * 64:(h + 1) * 64], True, True,
                       [kpb, vab], [bab])
            yield
            K.op(dve, lambda: nc.vector.tensor_tensor(out=attn_sb[:], in0=ba[:, 0:256], in1=hmask[:], op=ALU.mult),
                 reads=[bab, *CbL], writes=[attnb])
            yield
            for c in range(2):
                for hp in range(2):
                    col = 256 + (c * 2 + hp) * 64
                    K.op(dve, lambda hp=hp, col=col, c=c: nc.vector.scalar_tensor_tensor(
                        out=S[:, hp, :], in0=S[:, hp, :], scalar=dec[:, hp * 2 + c:hp * 2 + c + 1], in1=ba[:, col:col + 64],
                        op0=ALU.mult, op1=ALU.add), reads=[bab, decb], writes=[Sb])
                tgt = 1 if c == 0 else 0
                if c == 0:
                    K.op(dve, lambda: nc.vector.tensor_copy(out=S_bf[1][:], in_=S[:]), reads=[Sb], writes=[S_bfb[1]])
            yield
            bo, bob = G()
            for c in range(2):
                cs = slice(c * 64, (c + 1) * 64)
                for h in range(4):
                    mm(bo[cs, h * 64:(h + 1) * 64], attn_sb[cs, h * 64:(h + 1) * 64], va[cs, h * 64:(h + 1) * 64],
                       True, True, [attnb, vab], [bob])
            yield
            bi, bib = G()
            for h in (0, 2, 1, 3):
                hp, r = h // 2, (h % 2) * 64
                for c in range(2):
                    cs = slice(c * 64, (c + 1) * 64)
                    mm(bi[cs, h * 64:(h + 1) * 64], TT[r:r + 64, hp, cs], S_bf[c][r:r + 64, hp, :],
                       True, True, [TTb, S_bfb[c]], [bib])
            yield
            K.op(dve, lambda: nc.vector.tensor_copy(out=S_bf[0][:], in_=S[:]), reads=[Sb], writes=[S_bfb[0]])
            yield
            bp, bpb = G()
            for g in range(4):
                ro = (g % 2) * 64
                o_ap = bp[ro:ro + 64, (g // 2) * P:(g // 2 + 1) * P]
                if i == 0:
                    mm(o_ap, u_sb[par][:, g * 64:(g + 1) * 64], pm_cur[:, g * P:(g + 1) * P], True, True,
                       [ub[par], *CbL], [bpb])
                    mm(o_ap[:, 0:16], u_sb[par][:, g * 64:(g + 1) * 64], pm_cur0[:, g * 16:(g + 1) * 16], True, True,
                       [ub[par], *CbL], [bpb])
                else:
                    mm(o_ap, u_sb[par][:, g * 64:(g + 1) * 64], pm_cur[:, g * P:(g + 1) * P], True, False,
                       [ub[par], *CbL], [bpb])
                    mm(o_ap[:, 0:16], u_sb[1 - par][:, g * 64:(g + 1) * 64], pm_prev[:, g * 16:(g + 1) * 16], False, True,
                       [ub[1 - par], *CbL], [bpb])
            yield
            K.op(act, lambda: nc.scalar.activation(out=qpp[:], in_=bp[:, 0:256], func=AF.Copy),
                 reads=[bpb], writes=[plTb])
            yield
            for g in (0, 2, 1, 3):
                ro = (g % 2) * 64
                mm(bo[:, 256 + g * 64:256 + (g + 1) * 64], plT[ro:ro + 64, g // 2, :], pw[ro:ro + 64, (g // 2) * 64:(g // 2 + 1) * 64],
                   True, True, [plTb, Lpw], [bob])
            yield
            hgb = smb["hg"]
            K.op(act, lambda: nc.scalar.activation(out=Fx[:], in_=bo[:, 0:256], func=AF.Copy), reads=[bob], writes=[Fxb])
            yield
            K.op(dve, lambda: nc.vector.tensor_tensor(out=Fx[:], in0=Fx[:], in1=bi[:, 0:256], op=ALU.add),
                 reads=[bib], writes=[Fxb])
            yield
            K.op(dve, lambda: nc.vector.tensor_tensor(out=Fy[:], in0=Fx[:], in1=Fx[:], op=ALU.mult), reads=[Fxb], writes=[Fyb])
            yield
            K.op(dve, lambda: nc.vector.reduce_sum(out=small[:, 4:8], in_=Fy[:].rearrange("p (h d) -> p h d", d=64), axis=AX.X),
                 reads=[Fyb], writes=[hgb])
            yield
            rms_rstd(small[:, 4:8], 64.0, small[:, 8:12], hgb)
            yield
            for h in range(4):
                K.op(dve, lambda h=h: nc.vector.scalar_tensor_tensor(
                    out=XYf[1 - par][:, h * 64:(h + 1) * 64], in0=Fx[:, h * 64:(h + 1) * 64], scalar=small[:, 8 + h:9 + h],
                    in1=gate_a[:, h * 64:(h + 1) * 64], op0=ALU.mult, op1=ALU.mult), reads=[Fxb, hgb, gab], writes=[XYb[1 - par]])
            yield
            K.op(dve, lambda: nc.vector.tensor_tensor(out=XYf[1 - par][:, 256:512], in0=bo[:, 256:512], in1=gate_b[:], op=ALU.mult),
                 reads=[bob, gbb], writes=[XYb[1 - par]])


            yield

        groups = []
        for h in range(8):
            for j0 in range(0, i + 1, 4):
                groups.append([(h, j) for j in range(j0, min(j0 + 4, i + 1))])
        rdb = smb["rden"]

        def emit_S(gi):
            stt, stb = ST[gi % NST]
            ptb = PTb[gi % NPT]
            pt = PT[gi % NPT]
            for sl, (h, j) in enumerate(groups[gi]):
                hp, r = h // 2, (h % 2) * 64
                mm(stt[:, sl * P:(sl + 1) * P], KT[:, hp, j * P:(j + 1) * P], (QTa if r == 0 else QTz)[:, hp, :], True, True,
                   [KTb[j], QTb], [stb])
            ng = len(groups[gi])
            K.op(act, lambda: nc.scalar.activation(out=pt[:, 0:ng * P], in_=stt[:, 0:ng * P], func=AF.Exp, scale=0.125),
                 reads=[stb], writes=[ptb])
            gh, gj0 = groups[gi][0]
            vt, vtb = Vt[gi % NVT], Vtb[gi % NVT]
            K.op(dve, lambda: nc.vector.tensor_tensor(
                out=vt[:, 0:ng * 65].rearrange("p (a b) -> p a b", b=65), in0=V[:, gj0:gj0 + ng, gh * 65:(gh + 1) * 65],
                in1=fij[par][:, gj0:gj0 + ng, gh:gh + 1].to_broadcast([P, ng, 65]), op=ALU.mult),
                reads=[biasb[par]] + [Vb[j] for j in range(gj0, gj0 + ng)], writes=[vtb])
            for sl, (h, j) in enumerate(groups[gi]):
                if j == i:
                    K.op(dve, lambda sl=sl: nc.vector.tensor_tensor(
                        out=pt[:, sl * P:(sl + 1) * P], in0=pt[:, sl * P:(sl + 1) * P], in1=fmask[:], op=ALU.mult),
                        reads=[*CbL], writes=[ptb])

        def emit_PV(gi):
            pt, ptb = PT[gi % NPT], PTb[gi % NPT]
            for sl, (h, j) in enumerate(groups[gi]):
                oa, oab = OA[h // 4]
                c0 = (h % 4) * 65
                mm(oa[:, c0:c0 + 65], pt[:, sl * P:(sl + 1) * P], Vt[gi % NVT][:, sl * 65:(sl + 1) * 65], j == 0, j == i,
                   [ptb, Vtb[gi % NVT]], [oab], inc=True)
                if j == i and h % 4 == 3:
                    hb = h - 3
                    K.op(dve, lambda oa=oa: nc.vector.reciprocal(
                        out=small[:, 44:48], in_=oa[:, 0:260].rearrange("p (h e) -> p h e", e=65)[:, :, 64]),
                        reads=[oab], writes=[rdb])
                    for hh in range(4):
                        K.op(dve, lambda hh=hh, oa=oa, hb=hb: nc.vector.scalar_tensor_tensor(
                            out=XYf[1 - par][:, 512 + (hb + hh) * 64:512 + (hb + hh + 1) * 64], in0=oa[:, hh * 65:hh * 65 + 64],
                            scalar=small[:, 44 + hh:45 + hh], in1=gate_c[:, (hb + hh) * 64:(hb + hh + 1) * 64],
                            op0=ALU.mult, op1=ALU.mult), reads=[oab, rdb, gcb], writes=[XYb[1 - par]])

        gen = ab_thread()
        steps_per_group = max(1, -(-12 // len(groups)))
        if next_front is not None:
            if next_front[2] != p:
                load_cast(preg[:], ld[next_front[2]]["preg_bc"][:, :], Lpre)
            front(*next_front)
        gstate[1] = [0, 1, 2]
        emit_S(0)
        if len(groups) > 1:
            emit_S(1)
        for gi in range(len(groups)):
            if gi + 2 < len(groups):
                emit_S(gi + 2)
            emit_PV(gi)
            for _ in range(steps_per_group):
                next(gen, None)
        for _ in gen:
            pass
        gstate[1] = [0, 1, 2, 3, 4, 5]

        tm, tmb = G()
        for ec in range(8):
            tr(bf(tm)[:, ec * P:(ec + 1) * P], XYf[1 - par][:, ec * P:(ec + 1) * P], [XYb[1 - par]], [tmb])
        K.op(act, lambda: nc.scalar.activation(out=XYf[par], in_=bf(tm)[:, 0:D], func=AF.Copy),
             reads=[tmb], writes=[XYb[par]])
        if next_front is not None:
            front_b(1 - par)
        ys = []
        for half in range(2):
            y_, yb_ = G()
            for ec in range(8):
                mm(y_[:, 0:512], XYc[par][:, ec, :], Wout[:, ec, half * 512:(half + 1) * 512], ec == 0, ec == 7,
                   [XYb[par], Wob], [yb_])
            ys.append((y_, yb_))
        if prefetch is not None:
            load_wout(prefetch)
        n2 = smb["n2"]
        for half in range(2):
            y_, yb_ = ys[half]
            K.op(act, lambda y_=y_, half=half: nc.scalar.activation(
                out=hh[:, half * 512:(half + 1) * 512], in_=y_[:, 0:512], func=AF.Square,
                accum_out=small[:, 48 + half:49 + half]), reads=[yb_], writes=[hhb, n2])
        K.op(dve, lambda: nc.vector.tensor_tensor(out=small[:, 50:51], in0=small[:, 48:49], in1=small[:, 49:50], op=ALU.add),
             writes=[n2])
        rms_rstd(small[:, 50:51], float(D), small[:, 51:52], n2)
        for q4 in range(4):
            y_, yb_ = ys[q4 // 2]
            c0 = (q4 % 2) * 256
            K.op(dve, lambda y_=y_, q4=q4, c0=c0: nc.vector.scalar_tensor_tensor(
                out=Fs[q4][:], in0=y_[:, c0:c0 + 256], scalar=small[:, 51:52], in1=postg[:, q4 * 256:(q4 + 1) * 256],
                op0=ALU.mult, op1=ALU.mult), reads=[yb_, n2, Lpost], writes=[Fb[q4]])
            K.op(pool, lambda q4=q4: nc.gpsimd.tensor_tensor(
                out=x_seq[:, i, q4 * 256:(q4 + 1) * 256], in0=x_seq[:, i, q4 * 256:(q4 + 1) * 256],
                in1=Fs[q4][:], op=ALU.add), reads=[Fb[q4]], writes=[Xb[i]])
        finish()

    if 1 in lb_modes:
        compute_lb1(list(lb_modes).index(1))

    x_preloaded = set()
    for s in range(NSEQ):
        for i in range(NB):
            if (s, i) not in x_preloaded:
                load_f32(x_seq[:, i, :], x_d[s, i * P:(i + 1) * P, :], Xb[i])
        for p in range(NL):
            first = (s == 0 and p == 0)
            load_layer(p, with_weights=first, with_preg=first)
            nxt = None
            if p + 1 < NL:
                nxt = p + 1
            elif s + 1 < NSEQ:
                nxt = 0
            K.op(pool, lambda: nc.gpsimd.memset(S[:], 0.0), writes=[Sb])
            K.op(pool, lambda: nc.gpsimd.memset(S_bf[0][:], 0.0), writes=[S_bfb[0]])
            K.op(pool, lambda: nc.gpsimd.memset(small[:, 28:36], 0.0), writes=[smb["acc"]])
            for i in range(NB):
                if i == NB - 1 and p == NL - 1 and s + 1 < NSEQ:
                    for i2 in range(NB - 1):
                        load_f32(x_seq[:, i2, :], x_d[s + 1, i2 * P:(i2 + 1) * P, :], Xb[i2])
                        x_preloaded.add((s + 1, i2))
                bg = (s == 0 and p + 1 < NL and NB >= 2)
                if bg and i == NB - 1:
                    wsc_ready[p + 1] = True
                if i + 1 < NB:
                    nf = (s, i + 1, p)
                elif nxt is not None:
                    nf = (s, 0, p + 1) if p + 1 < NL else (s + 1, 0, 0)
                else:
                    nf = None
                block(s, i, p, p == NL - 1, do_front=(first and i == 0), next_front=nf,
                      prefetch=(nxt if i == NB - 1 else None))
                if bg and i < NB - 1:
                    q = p + 1
                    per = -(-16 // (NB - 1))
                    for pc in range(i * per, min(16, (i + 1) * per)):
                        if pc < 8:
                            K.dma(K.q_pool, lambda pc=pc, q=q: nc.gpsimd.dma_start(
                                out=wsc_in[q][:, pc, :], in_=ld[q]["w_in"][pc * P:(pc + 1) * P, :]), writes=[wsc_ib[q][pc]])
                        else:
                            K.dma(K.q_pool, lambda pc=pc, q=q: nc.gpsimd.dma_start(
                                out=wsc_out[q][:, pc - 8, :], in_=ld[q]["w_out"][(pc - 8) * P:(pc - 7) * P, :]),
                                writes=[wsc_ob[q][pc - 8]])
    sp.wait_all([(l[0], l[1]) for l in K.q_sp.lanes if l[1] > 0])
    return nc


def _layer_inputs(l, lower_bounds, pre_norm_g, w_in, hgrn_norm_g, fox_f_bias, pool_w, pool_scale, w_out, post_norm_g):
    f = np.float32
    bc = lambda v: np.ascontiguousarray(np.broadcast_to(np.asarray(v, f)[None, :], (P, v.shape[0])))
    pwl = np.zeros((P, 128), f)
    for g in range(4):
        pwl[(g % 2) * 64:(g % 2) * 64 + 64, (g // 2) * 64:(g // 2 + 1) * 64] = pool_w[l, g]
    return {
        "w_in": np.ascontiguousarray(w_in[l], f), "w_out": np.ascontiguousarray(w_out[l], f),
        "preg_bc": bc(pre_norm_g[l]), "postg_bc": bc(post_norm_g[l]),
        "hgrng_bc": bc(hgrn_norm_g[l]), "pscale_bc": bc(pool_scale[l]),
        "lbraw_bc": bc(np.concatenate([lower_bounds[0], lower_bounds[1]])),
        "fbias_bc": bc(fox_f_bias[l]), "pw": pwl,
    }


FUSED = True
_cache = {}


def run(x, params, T, NSEQ, ncores, layer_groups):
    consts = _consts()
    cur = np.ascontiguousarray(x, np.float32)
    for grp in layer_groups:
        key = (T, NSEQ, tuple(grp))
        if key not in _cache:
            _cache[key] = build_program(T, NSEQ, [0 if l == 0 else 1 for l in grp])
        nc = _cache[key]
        base = {"c_" + k: v for k, v in consts.items()}
        for p, l in enumerate(grp):
            for k, v in _layer_inputs(l, **params).items():
                base["l%d_%s" % (p, k)] = v
        in_maps = []
        for c in range(ncores):
            m = dict(base)
            m["x"] = np.ascontiguousarray(cur[c * NSEQ:(c + 1) * NSEQ])
            in_maps.append(m)
        res = run_bass_kernel_spmd(nc, in_maps, core_ids=list(range(ncores)))
        cur = np.concatenate([np.asarray(r["out"], np.float32) for r in res.results], axis=0)
    return cur


def kernel(x, lower_bounds, pre_norm_g, w_in, hgrn_norm_g, fox_f_bias, pool_w, pool_scale, w_out, post_norm_g):
    params = dict(lower_bounds=np.asarray(lower_bounds), pre_norm_g=np.asarray(pre_norm_g), w_in=np.asarray(w_in),
                  hgrn_norm_g=np.asarray(hgrn_norm_g), fox_f_bias=np.asarray(fox_f_bias), pool_w=np.asarray(pool_w),
                  pool_scale=np.asarray(pool_scale), w_out=np.asarray(w_out), post_norm_g=np.asarray(post_norm_g))
    x = np.asarray(x)
    B, T, _ = x.shape
    groups = [[0, 1]] if FUSED else [[0], [1]]
    return run(x, params, T, B // 8, 8, groups).astype(np.float32)
```

```python
import numpy as np
import concourse.bass as bass
import concourse.mybir as mybir
from concourse.bass_utils import run_bass_kernel_spmd

F32 = mybir.dt.float32
BF16 = mybir.dt.bfloat16
AF = mybir.ActivationFunctionType
ALU = mybir.AluOpType
AX = mybir.AxisListType

D = 1024
INW = 3592
EPS = 1e-6
TINY = 1e-30
P = 128
POOL_WINDOWS = (2, 4, 8, 16)
EPOCH = 24000


CLOCKS = {}


class Buf:
    __slots__ = ("w", "r", "excl")

    def __init__(self, excl=False):
        self.w = None
        self.r = {}
        self.excl = excl


class Eng:
    def __init__(self, nc, eng, name):
        self.nc, self.eng, self.name = nc, eng, name
        self.sem = nc.alloc_semaphore(name + "_s0")
        self.own = {id(self.sem)}
        self.n = 0
        self.ep = 0
        self.seen = {}
        self.pending = 0

    def wait_all(self, deps):
        deps = [d for d in deps if self.seen.get(id(d[0]), 0) < d[1]]
        if len(deps) > 1:
            keep = []
            for d in deps:
                k = id(d[0])
                implied = False
                for d2 in deps:
                    if d2 is not d:
                        c2 = CLOCKS.get((id(d2[0]), d2[1]))
                        if c2 is not None and c2.get(k, 0) >= d[1]:
                            implied = True
                            break
                if not implied:
                    keep.append(d)
            deps = keep
        for sem, val in deps:
            key = id(sem)
            if self.seen.get(key, 0) < val:
                self.eng.wait_ge(sem, val)
                self.seen[key] = val
            c = CLOCKS.get((key, val))
            if c is not None:
                seen = self.seen
                for k2, v2 in c.items():
                    if seen.get(k2, 0) < v2:
                        seen[k2] = v2

    def snapshot(self, tok):
        c = dict(self.seen)
        c[id(tok[0])] = max(c.get(id(tok[0]), 0), tok[1])
        CLOCKS[(id(tok[0]), tok[1])] = c

    def issue(self, ins):
        if self.n >= EPOCH and self.pending == 0:
            self.ep += 1
            self.sem = self.nc.alloc_semaphore("%s_s%d" % (self.name, self.ep))
            self.own.add(id(self.sem))
            self.n = 0
        self.n += 1
        ins.then_inc(self.sem, 1)
        self.pending = 0
        return (self.sem, self.n)

    def issue_noinc(self, ins):
        if self.n >= EPOCH:
            pass
        self.pending += 1
        return (self.sem, self.n + 1)


class DmaQ:
    def __init__(self, nc, E, name, nlanes):
        self.E = E
        self.lanes = [[nc.alloc_semaphore("%s_l%d" % (name, i)), 0] for i in range(nlanes)]
        self.k = 0

    def issue(self, fn, deps):
        lane = self.lanes[self.k]
        self.k = (self.k + 1) % len(self.lanes)
        d = set(deps)
        if lane[1] > 0:
            d.add((lane[0], lane[1]))
        self.E.wait_all(d)
        lane[1] += 16
        fn().then_inc(lane[0], 16)
        tok = (lane[0], lane[1])
        self.E.snapshot(tok)
        return tok


class Ctx:
    def __init__(self, nc):
        self.nc = nc
        self.pe = Eng(nc, nc.tensor, "pe")
        self.act = Eng(nc, nc.scalar, "act")
        self.dve = Eng(nc, nc.vector, "dve")
        self.pool = Eng(nc, nc.gpsimd, "pool")
        self.sp = Eng(nc, nc.sync, "sp")
        self.q_sp = DmaQ(nc, self.sp, "qsp", 8)
        self.q_pool = DmaQ(nc, self.pool, "qpl", 16)

    def _deps(self, reads, writes, E=None):
        deps = set()
        own = E.own if E is not None else ()
        is_pe = E is self.pe
        for b in reads:
            if b.w is not None and not (is_pe and id(b.w[0]) in own):
                deps.add(b.w)
            if b.excl:
                for tok in b.r.values():
                    if id(tok[0]) not in own:
                        deps.add(tok)
        for b in writes:
            if b.w is not None and not (is_pe and id(b.w[0]) in own):
                deps.add(b.w)
            for tok in b.r.values():
                if not (is_pe and id(tok[0]) in own):
                    deps.add(tok)
        return deps

    @staticmethod
    def _commit(tok, reads, writes):
        for b in reads:
            b.r[id(tok[0])] = tok
        for b in writes:
            b.w = tok
            b.r = {}

    def op(self, E, fn, reads=(), writes=(), inc=True, mode=None):
        if mode is not None and mode != getattr(self, "pe_mode", None):
            if E.n > 0:
                assert E.pending == 0
                E.eng.wait_ge(E.sem, E.n)
                E.seen[id(E.sem)] = E.n
            self.pe_mode = mode
        E.wait_all(self._deps(reads, writes, E))
        tok = E.issue(fn()) if inc else E.issue_noinc(fn())
        if inc:
            E.snapshot(tok)
        elif (id(tok[0]), tok[1]) not in CLOCKS:
            E.snapshot(tok)
        self._commit(tok, reads, writes)
        return tok

    def dma(self, Q, fn, reads=(), writes=()):
        tok = Q.issue(fn, self._deps(reads, writes))
        self._commit(tok, reads, writes)
        return tok


def _consts():
    s = np.arange(P)[:, None]
    t = np.arange(P)[None, :]
    same = (s // 64) == (t // 64)
    c = {}
    c["ident"] = np.eye(P, dtype=np.float32)
    c["ucum"] = (same & (s <= t)).astype(np.float32)
    c["lstr"] = (same & (s > t)).astype(np.float32)
    c["ufull"] = (s <= t).astype(np.float32)
    c["ones"] = np.ones((P, P), np.float32)
    ind = np.zeros((P, 2), np.float32)
    ind[:64, 0] = 1
    ind[64:, 1] = 1
    c["ind"] = ind
    c["fmask"] = (s <= t).astype(np.float32)
    hm = np.zeros((P, 256), np.float32)
    for h in range(4):
        hm[:, h * 64:(h + 1) * 64] = ((np.arange(P)[:, None] % 64) <= np.arange(64)[None, :])
    c["hmask"] = hm
    pm_cur = np.zeros((P, 4, P), np.float32)
    pm_prev = np.zeros((P, 4, 16), np.float32)
    pm_cur0 = np.zeros((P, 4, 16), np.float32)
    for g, w in enumerate(POOL_WINDOWS):
        for tt in range(P):
            for ss in range(tt - w + 1, tt + 1):
                if ss >= 0:
                    pm_cur[ss, g, tt] += 1.0 / w
                else:
                    if tt < 16:
                        pm_prev[ss + P, g, tt] += 1.0 / w
            pm_cur[tt, g, tt] -= 1.0
        for tt in range(16):
            cnt = min(tt + 1, w)
            for ss in range(max(0, tt - w + 1), tt + 1):
                pm_cur0[ss, g, tt] += 1.0 / cnt
            pm_cur0[tt, g, tt] -= 1.0
    c["pm_cur"] = pm_cur.reshape(P, 4 * P)
    c["pm_prev"] = pm_prev.reshape(P, 64)
    c["pm_cur0"] = pm_cur0.reshape(P, 64)
    return c


CONST_SHAPES = {"ident": (P, P), "ucum": (P, P), "lstr": (P, P), "ufull": (P, P), "ones": (P, P),
                "ind": (P, 2), "fmask": (P, P), "hmask": (P, 256), "pm_cur": (P, 512),
                "pm_prev": (P, 64), "pm_cur0": (P, 64)}
LAYER_SHAPES = {"w_in": (D, INW), "w_out": (D, D), "preg_bc": (P, D), "postg_bc": (P, D),
                "hgrng_bc": (P, 256), "pscale_bc": (P, 256), "lbraw_bc": (P, 512),
                "fbias_bc": (P, 8), "pw": (P, 128)}


def build_program(T, NSEQ, lb_modes, debug_out=False):
    NL = len(lb_modes)
    NB = T // P
    nc = bass.Bass("TRN2", target_bir_lowering=False)
    CLOCKS.clear()
    K = Ctx(nc)
    pe, act, dve, pool, sp = K.pe, K.act, K.dve, K.pool, K.sp

    x_d = nc.dram_tensor("x", [NSEQ, T, D], F32, kind="ExternalInput").ap()
    out_d = nc.dram_tensor("out", [NSEQ, T, D], F32, kind="ExternalOutput").ap()
    cd = {k: nc.dram_tensor("c_" + k, list(v), F32, kind="ExternalInput").ap() for k, v in CONST_SHAPES.items()}
    ld = [{k: nc.dram_tensor("l%d_%s" % (p, k), list(v), F32, kind="ExternalInput").ap()
           for k, v in LAYER_SHAPES.items()} for p in range(NL)]

    wsc_in = [nc.dram_tensor("wsc_in%d" % p, [P, 8, INW], BF16, kind="Internal").ap() for p in range(NL)]
    wsc_out = [nc.dram_tensor("wsc_out%d" % p, [P, 8, D], BF16, kind="Internal").ap() for p in range(NL)]
    wsc_ib = [[Buf() for _ in range(8)] for _ in range(NL)]
    wsc_ob = [[Buf() for _ in range(8)] for _ in range(NL)]
    wsc_ready = [False] * NL

    def sb(name, shape, dt):
        return nc.alloc_sbuf_tensor(name, list(shape), dt)

    x_seq = sb("x_seq", [P, NB, D], F32)
    Xb = [Buf() for _ in range(NB)]
    Win = sb("Win", [P, 8, INW], BF16)
    Wout = sb("Wout", [P, 8, D], BF16)
    Wib = Buf()
    Wob = Buf()
    KT = sb("KT", [P, 4, T], BF16)
    KTb = [Buf() for _ in range(NB)]
    V = sb("V", [P, NB, 8 * 65], BF16)
    Vb = [Buf() for _ in range(NB)]
    Vones = Buf()
    ident = sb("ident", [P, P], BF16)
    ucum = sb("ucum", [P, P], F32)
    lstr = sb("lstr", [P, P], F32)
    ufull = sb("ufull", [P, P], F32)
    ones = sb("ones", [P, P], F32)
    ind = sb("ind", [P, 2], F32)
    fmask = sb("fmask", [P, P], BF16)
    hmask = sb("hmask", [P, 256], BF16)
    pm_cur = sb("pm_cur", [P, 512], BF16)
    pm_prev = sb("pm_prev", [P, 64], BF16)
    pm_cur0 = sb("pm_cur0", [P, 64], BF16)
    CbL = [Buf() for _ in range(len(CONST_SHAPES))]
    preg = sb("preg", [P, D], BF16)
    postg = sb("postg", [P, D], BF16)
    hgrng = sb("hgrng", [P, 256], BF16)
    pscale = sb("pscale", [P, 256], BF16)
    lb_bc = sb("lb_bc", [P, 256], F32)
    oml_bc = sb("oml_bc", [P, 256], F32)
    lbraw = None
    fbias = sb("fbias", [P, 8], F32)
    pw = sb("pw", [P, 128], BF16)
    Lpre, Lpost, Lpw, Lhg, Lps, Lfb, Llb = (Buf() for _ in range(7))
    hm = sb("hm", [P, D], BF16)
    hmb = Buf()
    hh = sb("hh", [P, D], BF16)
    hhb = Buf()
    hT = sb("hT", [P, 8, P], BF16)
    hTb = Buf()
    XYf = [hT[:].rearrange("p a b -> p (a b)"), hm[:]]
    XYc = [hT[:], hm[:].rearrange("p (a b) -> p a b", b=P)]
    XYb = [hTb, hmb]
    NF = 5
    Fs = [sb("F%d" % i, [P, 256], F32) for i in range(NF)]
    Fb = [Buf() for _ in range(NF)]
    ebig = sb("ebig", [P, 256], F32)
    ebb = Buf()
    gate_a = sb("gate_a", [P, 256], BF16)
    gate_b = sb("gate_b", [P, 256], BF16)
    gate_c = sb("gate_c", [P, 512], BF16)
    gab, gbb, gcb = Buf(), Buf(), Buf()
    qp = sb("qp", [P, 256], BF16)
    qpp = sb("qpp", [P, 256], BF16)
    kp = sb("kp", [P, 256], BF16)
    va = sb("va", [P, 256], BF16)
    attn_sb = qp
    qpb, qppb, kpb, vab = Buf(), Buf(), Buf(), Buf()
    attnb = qpb
    TT = sb("TT", [P, 6, P], BF16)
    TTb = Buf()
    S = sb("S", [P, 2, 64], F32)
    Sb = Buf()
    S_bf = [sb("S_bf%d" % i, [P, 2, 64], BF16) for i in range(2)]
    S_bfb = [Buf(), Buf()]
    dec = sb("dec", [P, 4], F32)
    decb = Buf()
    u_sb = [sb("u_sb%d" % i, [P, 256], BF16) for i in range(2)]
    ub = [Buf(), Buf()]
    plT = qpp[:].rearrange("p (a b) -> p a b", b=P)
    plTb = qppb
    q_sb = Fs[3][:].bitcast(BF16)
    k_sb = Fs[4][:].bitcast(BF16)
    qsb_b, ksb_b = Fb[3], Fb[4]
    QTa = sb("QTa", [P, 4, P], BF16)
    QTz = sb("QTz", [P, 4, P], BF16)
    QTb = Buf()
    NPT = 3
    NVT = 3
    Vt = [sb("Vt%d" % i, [P, 4 * 65], BF16) for i in range(NVT)]
    Vtb = [Buf() for _ in range(NVT)]
    PT = [sb("PT%d" % i, [P, 512], BF16) for i in range(NPT)]
    PTb = [Buf() for _ in range(NPT)]
    fij = [sb("fij%d" % i, [P, NB, 8], F32) for i in range(2)]
    biasb = [Buf(), Buf()]
    totall = sb("totall", [P, NB, 8], F32)
    small = sb("small", [P, 64], F32)
    smb = {k: Buf() for k in ("n1", "hg", "fc", "acc", "tot", "rden", "n2", "w")}

    banks = [nc.alloc_psum_tensor("bank%d" % i, [P, 512], F32) for i in range(8)]
    bankb = [Buf(excl=True) for _ in range(8)]
    gstate = [0, [0, 1, 2, 3, 4, 5]]

    def G():
        lst = gstate[1]
        i = lst[gstate[0] % len(lst)]
        gstate[0] += 1
        return banks[i], bankb[i]

    NST = 3
    ST = [(banks[3], bankb[3]), (banks[4], bankb[4]), (banks[5], bankb[5])]
    OA = [(banks[6], bankb[6]), (banks[7], bankb[7])]

    def bf(t):
        return t[:].bitcast(BF16)

    def load_cast(dst_ap, src_ap, buf):
        K.dma(K.q_pool, lambda: nc.gpsimd.dma_start(out=dst_ap, in_=src_ap), writes=[buf])

    def load_f32(dst_ap, src_ap, buf):
        K.dma(K.q_sp, lambda: nc.sync.dma_start(out=dst_ap, in_=src_ap), writes=[buf])

    _cseen = []
    for name, t_, cast in (("ident", ident, 1), ("ucum", ucum, 0), ("lstr", lstr, 0), ("ufull", ufull, 0),
                           ("ones", ones, 0), ("ind", ind, 0), ("fmask", fmask, 1), ("hmask", hmask, 1),
                           ("pm_cur", pm_cur, 1), ("pm_prev", pm_prev, 1), ("pm_cur0", pm_cur0, 1)):
        (load_cast if cast else load_f32)(t_[:], cd[name][:, :], CbL[len(_cseen)])
        _cseen.append(name)
    Vv = V[:].rearrange("p b (h e) -> p b h e", e=65)
    K.op(pool, lambda: nc.gpsimd.memset(QTa[:], 0.0), writes=[QTb])
    K.op(pool, lambda: nc.gpsimd.memset(QTz[:], 0.0), writes=[QTb])

    def load_win(p):
        if wsc_ready[p]:
            for dc in range(8):
                K.dma(K.q_sp, lambda dc=dc: nc.sync.dma_start(out=Win[:, dc, :], in_=wsc_in[p][:, dc, :]),
                      reads=[wsc_ib[p][dc]], writes=[Wib])
        else:
            for dc in range(8):
                load_cast(Win[:, dc, :], ld[p]["w_in"][dc * P:(dc + 1) * P, :], Wib)
            for dc in range(8):
                K.dma(K.q_sp, lambda dc=dc: nc.sync.dma_start(out=wsc_in[p][:, dc, :], in_=Win[:, dc, :]),
                      reads=[Wib], writes=[wsc_ib[p][dc]])

    def load_wout(p):
        if wsc_ready[p]:
            for ec in range(8):
                K.dma(K.q_sp, lambda ec=ec: nc.sync.dma_start(out=Wout[:, ec, :], in_=wsc_out[p][:, ec, :]),
                      reads=[wsc_ob[p][ec]], writes=[Wob])
        else:
            for ec in range(8):
                load_cast(Wout[:, ec, :], ld[p]["w_out"][ec * P:(ec + 1) * P, :], Wob)
            for ec in range(8):
                K.dma(K.q_sp, lambda ec=ec: nc.sync.dma_start(out=wsc_out[p][:, ec, :], in_=Wout[:, ec, :]),
                      reads=[Wob], writes=[wsc_ob[p][ec]])
            wsc_ready[p] = True

    def load_layer(p, with_weights=True, with_preg=True):
        L = ld[p]
        if with_weights:
            load_win(p)
            load_wout(p)
        if with_preg:
            load_cast(preg[:], L["preg_bc"][:, :], Lpre)
        load_cast(postg[:], L["postg_bc"][:, :], Lpost)
        load_cast(pw[:], L["pw"][:, :], Lpw)
        load_cast(hgrng[:], L["hgrng_bc"][:, :], Lhg)
        load_cast(pscale[:], L["pscale_bc"][:, :], Lps)
        load_f32(fbias[:], L["fbias_bc"][:, :], Lfb)

    def compute_lb1(p):
        L = ld[p]
        load_f32(Fs[1][:], L["lbraw_bc"][:, 0:256], Fb[1])
        load_f32(Fs[2][:], L["lbraw_bc"][:, 256:512], Fb[2])
        K.op(pool, lambda: nc.gpsimd.tensor_tensor(out=Fs[0][:], in0=Fs[1][:], in1=Fs[2][:],
                                                   op=ALU.subtract), reads=[Fb[1], Fb[2]], writes=[Fb[0]])
        K.op(act, lambda: nc.scalar.activation(out=Fs[0][:], in_=Fs[0][:], func=AF.Exp),
             reads=[], writes=[Fb[0]])
        K.op(pool, lambda: nc.gpsimd.tensor_scalar(out=Fs[0][:], in0=Fs[0][:], scalar1=1.0, scalar2=1.0,
                                                   op0=ALU.mult, op1=ALU.add), writes=[Fb[0]])
        K.op(dve, lambda: nc.vector.reciprocal(out=lb_bc[:], in_=Fs[0][:]), reads=[Fb[0]], writes=[Llb])
        K.op(pool, lambda: nc.gpsimd.tensor_scalar(out=oml_bc[:], in0=lb_bc[:], scalar1=-1.0, scalar2=1.0,
                                                   op0=ALU.mult, op1=ALU.add), writes=[Llb])

    def silu_gate(ps_ap, width, out_ap, outb, mul_ap=None, psb=None):
        e = ebig[:, 0:width]
        K.op(act, lambda: nc.scalar.activation(out=e, in_=ps_ap, func=AF.Exp, scale=-1.0),
             reads=[psb], writes=[ebb])
        K.op(act, lambda: nc.scalar.activation(out=e, in_=e, func=AF.Ln, bias=1.0), writes=[ebb])
        K.op(act, lambda: nc.scalar.activation(out=e, in_=e, func=AF.Exp, scale=-1.0), writes=[ebb])
        if mul_ap is None:
            K.op(dve, lambda: nc.vector.tensor_tensor(out=out_ap, in0=ps_ap, in1=e, op=ALU.mult),
                 reads=[psb, ebb], writes=[outb])
        else:
            K.op(dve, lambda: nc.vector.tensor_tensor(out=e, in0=ps_ap, in1=e, op=ALU.mult),
                 reads=[psb], writes=[ebb])
            K.op(pool, lambda: nc.gpsimd.tensor_tensor(out=out_ap, in0=e, in1=mul_ap, op=ALU.mult),
                 reads=[ebb, Lhg, Lps], writes=[outb])

    def rms_rstd(ss_ap, n, out_ap, b):
        K.op(dve, lambda: nc.vector.tensor_scalar(out=out_ap, in0=ss_ap, scalar1=1.0 / n, scalar2=EPS,
                                                  op0=ALU.mult, op1=ALU.add), writes=[b])
        K.op(act, lambda: nc.scalar.activation(out=out_ap, in_=out_ap, func=AF.Ln), writes=[b])
        K.op(act, lambda: nc.scalar.activation(out=out_ap, in_=out_ap, func=AF.Exp, scale=-0.5), writes=[b])

    def _cls(n):
        return 32 if n <= 32 else (64 if n <= 64 else 128)

    def mm(out, lhsT, rhs, start, stop, reads, writes, inc=None):
        kc = _cls(lhsT.shape[0])
        mode = ("mm", str(lhsT.dtype), kc, _cls(lhsT.shape[-1]), lhsT.base_partition() if kc < 128 else 0)
        return K.op(pe, lambda: nc.tensor.matmul(out, lhsT, rhs, start=start, stop=stop), reads=reads, writes=writes,
                    inc=(bool(stop) or kc < 128) if inc is None else inc, mode=mode)

    def tr(out, in_, reads, writes):
        return K.op(pe, lambda: nc.tensor.transpose(out, in_, ident[:]), reads=list(reads) + [*CbL], writes=writes,
                    mode=("tr",))

    def front(s, i, p):
        xb = x_seq[:, i, :]
        n1 = smb["n1"]
        K.op(act, lambda: nc.scalar.activation(out=hh[:], in_=xb, func=AF.Square, accum_out=small[:, 0:1]),
             reads=[Xb[i]], writes=[hhb, n1])
        rms_rstd(small[:, 0:1], float(D), small[:, 2:3], n1)
        K.op(dve, lambda: nc.vector.scalar_tensor_tensor(out=hh[:], in0=xb, scalar=small[:, 2:3], in1=preg[:],
                                                         op0=ALU.mult, op1=ALU.mult),
             reads=[Xb[i], n1, Lpre], writes=[hhb])

    def front_b(tp):
        tb, tbb = G()
        for dc in range(8):
            tr(bf(tb)[:, dc * P:(dc + 1) * P], hh[:, dc * P:(dc + 1) * P], [hhb], [tbb])
        K.op(act, lambda: nc.scalar.activation(out=XYf[tp], in_=bf(tb)[:, 0:D], func=AF.Copy),
             reads=[tbb], writes=[XYb[tp]])


    def block(s, i, p, last_layer, do_front=True, next_front=None, prefetch=None):
        xb = x_seq[:, i, :]
        par = i % 2

        def finish():
            if last_layer:
                K.dma(K.q_sp, lambda: nc.sync.dma_start(out=out_d[s, i * P:(i + 1) * P, :], in_=x_seq[:, i, :]),
                      reads=[Xb[i]])
        if do_front:
            front(s, i, p)
            front_b(i % 2)
        def proj(col0, width):
            b_, bb_ = G()
            for dc in range(8):
                mm(b_[:, 0:width], XYc[par][:, dc, :], Win[:, dc, col0:col0 + width], dc == 0, dc == 7,
                   [XYb[par], Wib], [bb_])
            return b_, bb_

        b7, b7b = proj(3584, 8)
        fcb = smb["fc"]
        K.op(dve, lambda: nc.vector.tensor_tensor(out=small[:, 12:20], in0=b7[:, 0:8], in1=fbias[:], op=ALU.add),
             reads=[b7b, Lfb], writes=[fcb])
        K.op(act, lambda: nc.scalar.activation(out=small[:, 12:20], in_=small[:, 12:20], func=AF.Exp, scale=-1.0),
             writes=[fcb])
        K.op(pool, lambda: nc.gpsimd.tensor_scalar(out=small[:, 12:20], in0=small[:, 12:20], scalar1=1.0,
                                                   scalar2=1.0, op0=ALU.mult, op1=ALU.add), writes=[fcb])
        K.op(act, lambda: nc.scalar.activation(out=small[:, 20:28], in_=small[:, 12:20], func=AF.Ln), writes=[fcb])
        sp_ap = small[:, 20:28]
        acc_ap = small[:, 28:36]
        accb = smb["acc"]
        b3, b3b = proj(1536, 512)
        K.op(act, lambda: nc.scalar.activation(out=q_sb, in_=b3[:, 0:512], func=AF.Copy), reads=[b3b], writes=[qsb_b])
        b4, b4b = proj(2048, 512)
        K.op(dve, lambda: nc.vector.tensor_copy(out=k_sb, in_=b4[:, 0:512]), reads=[b4b], writes=[ksb_b])
        tq, tqb = G()
        for hp in range(4):
            tr(bf(tq)[:, hp * P:(hp + 1) * P], q_sb[:, hp * P:(hp + 1) * P], [qsb_b], [tqb])
        for hp in range(4):
            tr(bf(tq)[:, (4 + hp) * P:(5 + hp) * P], k_sb[:, hp * P:(hp + 1) * P], [ksb_b], [tqb])
        K.op(act, lambda: nc.scalar.activation(out=QTa[0:64].rearrange("p a b -> p (a b)"), in_=bf(tq)[0:64, 0:512], func=AF.Copy),
             reads=[tqb], writes=[QTb])
        K.op(act, lambda: nc.scalar.activation(out=QTz[64:128].rearrange("p a b -> p (a b)"), in_=bf(tq)[64:128, 0:512], func=AF.Copy),
             reads=[tqb], writes=[QTb])
        K.op(act, lambda: nc.scalar.activation(out=KT[:, :, i * P:(i + 1) * P],
                                               in_=bf(tq)[:, 512:1024].rearrange("p (a b) -> p a b", b=P),
                                               func=AF.Copy), reads=[tqb], writes=[KTb[i]])

        bc_, bcb = G()
        mm(bc_[:, 0:8], ufull[:], sp_ap, True, False, [*CbL, fcb], [bcb])
        mm(bc_[:, 0:8], ones[:], acc_ap, False, True, [*CbL, accb], [bcb])
        K.op(pool, lambda: nc.gpsimd.tensor_tensor(out=acc_ap, in0=acc_ap, in1=sp_ap, op=ALU.add),
             reads=[fcb], writes=[accb])
        mm(bc_[:, 8:16], ones[:], acc_ap, True, True, [*CbL, accb], [bcb])
        totb = smb["tot"]
        wb = smb["w"]
        K.op(dve, lambda: nc.vector.tensor_copy(out=totall[:, i, :], in_=bc_[:, 8:16]), reads=[bcb], writes=[totb])
        K.op(dve, lambda: nc.vector.tensor_tensor(out=small[:, 52:60], in0=bc_[:, 0:8], in1=totall[:, i, :],
                                                  op=ALU.subtract), reads=[bcb, totb], writes=[wb])
        K.op(act, lambda: nc.scalar.activation(out=small[:, 52:60], in_=small[:, 52:60], func=AF.Exp), writes=[wb])
        K.op(pool, lambda: nc.gpsimd.tensor_tensor(
            out=fij[par][:, 0:i + 1, :], in0=totall[:, 0:i + 1, :],
            in1=totall[:, i:i + 1, :].to_broadcast([P, i + 1, 8]), op=ALU.subtract),
            reads=[totb], writes=[biasb[par]])
        K.op(act, lambda: nc.scalar.activation(out=fij[par][:, 0:i + 1, :], in_=fij[par][:, 0:i + 1, :], func=AF.Exp),
             writes=[biasb[par]])
        b5, b5b = proj(2560, 512)
        K.op(dve, lambda: nc.vector.tensor_tensor(
            out=Vv[:, i, :, 0:64], in0=b5[:, 0:512].rearrange("p (h d) -> p h d", d=64),
            in1=small[:, 52:60].unsqueeze(2).to_broadcast([P, 8, 64]), op=ALU.mult),
            reads=[b5b, smb["w"]], writes=[Vb[i]])
        K.op(pool, lambda: nc.gpsimd.tensor_copy(out=Vv[:, i, :, 64], in_=small[:, 52:60]),
             reads=[smb["w"]], writes=[Vb[i]])
        b6, b6b = proj(3072, 512)
        silu_gate(b6[:, 0:256], 256, gate_c[:, 0:256], gcb, psb=b6b)
        silu_gate(b6[:, 256:512], 256, gate_c[:, 256:512], gcb, psb=b6b)
        gstate[1] = [0, 1, 2]
        b0, b0b = proj(0, 512)
        Fq, Fqb = Fs[0], Fb[0]
        Fk, Fkb = Fs[1], Fb[1]
        Fl, Flb = Fs[2], Fb[2]
        Fx, Fxb = Fs[3], Fb[3]
        Fy, Fyb = Fs[4], Fb[4]
        b1, b1b = proj(512, 512)
        b2, b2b = proj(1024, 512)
        if prefetch is not None:
            load_win(prefetch)
        def ab_thread():
            K.op(act, lambda: nc.scalar.activation(out=Fx[:], in_=b0[:, 0:256], func=AF.Exp, scale=-1.0),
                 reads=[b0b], writes=[Fxb])
            yield
            K.op(act, lambda: nc.scalar.activation(out=Fx[:], in_=Fx[:], func=AF.Ln, bias=1.0), writes=[Fxb])
            yield
            K.op(act, lambda: nc.scalar.activation(out=Fx[:], in_=Fx[:], func=AF.Exp, scale=-1.0), writes=[Fxb])
            yield
            K.op(dve, lambda: nc.vector.tensor_tensor(out=Fq[:], in0=b0[:, 0:256], in1=Fx[:], op=ALU.mult),
                 reads=[b0b, Fxb], writes=[Fqb])
            yield
            K.op(act, lambda: nc.scalar.activation(out=Fy[:], in_=b0[:, 256:512], func=AF.Exp, scale=-1.0),
                 reads=[b0b], writes=[Fyb])
            yield
            K.op(act, lambda: nc.scalar.activation(out=Fy[:], in_=Fy[:], func=AF.Ln, bias=1.0), writes=[Fyb])
            yield
            K.op(act, lambda: nc.scalar.activation(out=Fy[:], in_=Fy[:], func=AF.Exp, scale=-1.0), writes=[Fyb])
            yield
            if lb_modes[p] == 0:
                K.op(pool, lambda: nc.gpsimd.tensor_scalar(out=Fk[:], in0=Fy[:], scalar1=-1.0, scalar2=1.0,
                                                           op0=ALU.mult, op1=ALU.add), reads=[Fyb], writes=[Fkb])
                yield
                K.op(dve, lambda: nc.vector.tensor_scalar(out=Fy[:], in0=Fy[:], scalar1=TINY, scalar2=None,
                                                          op0=ALU.max), writes=[Fyb])
                yield
            else:
                K.op(pool, lambda: nc.gpsimd.tensor_tensor(out=Fy[:], in0=Fy[:], in1=oml_bc[:], op=ALU.mult),
                     reads=[Llb], writes=[Fyb])
                yield
                K.op(pool, lambda: nc.gpsimd.tensor_tensor(out=Fk[:], in0=oml_bc[:], in1=Fy[:], op=ALU.subtract),
                     reads=[Llb, Fyb], writes=[Fkb])
                yield
                K.op(dve, lambda: nc.vector.scalar_tensor_tensor(out=Fy[:], in0=Fy[:], scalar=TINY, in1=lb_bc[:],
                                                                 op0=ALU.max, op1=ALU.add), reads=[Llb], writes=[Fyb])
                yield
            K.op(act, lambda: nc.scalar.activation(out=Fl[:], in_=Fy[:], func=AF.Ln), reads=[Fyb], writes=[Flb])

            yield
            K.op(act, lambda: nc.scalar.activation(out=va[:], in_=b1[:, 0:256], func=AF.Copy), reads=[b1b], writes=[vab])
            yield
            silu_gate(b1[:, 256:512], 256, gate_a[:], gab, mul_ap=hgrng[:], psb=b1b)
            yield
            K.op(act, lambda: nc.scalar.activation(out=u_sb[par][:], in_=b2[:, 0:256], func=AF.Copy),
                 reads=[b2b], writes=[ub[par]])
            yield
            silu_gate(b2[:, 256:512], 256, gate_b[:], gbb, mul_ap=pscale[:], psb=b2b)

            yield
            be, beb = G()
            mm(be[:, 0:256], ucum[:], Fl[:], True, True, [*CbL, Flb], [beb])
            yield
            mm(be[:, 256:512], lstr[:], Fl[:], True, True, [*CbL, Flb], [beb])
            yield
            bd, bdb = G()
            for hp in range(2):
                mm(bd[:, hp * 2:hp * 2 + 2], Fl[:, hp * P:(hp + 1) * P], ind[:], True, True, [*CbL, Flb], [bdb])
            yield
            K.op(act, lambda: nc.scalar.activation(out=dec[:], in_=bd[:, 0:4], func=AF.Exp), reads=[bdb], writes=[decb])
            yield
            K.op(act, lambda: nc.scalar.activation(out=Fx[:], in_=be[:, 0:256], func=AF.Exp), reads=[beb], writes=[Fxb])
            yield
            K.op(dve, lambda: nc.vector.tensor_tensor(out=qp[:], in0=Fq[:], in1=Fx[:], op=ALU.mult),
                 reads=[Fqb, Fxb], writes=[qpb])
            yield
            K.op(act, lambda: nc.scalar.activation(out=Fy[:], in_=be[:, 256:512], func=AF.Exp, scale=-1.0),
                 reads=[beb], writes=[Fyb])
            yield
            K.op(dve, lambda: nc.vector.tensor_tensor(out=qpp[:], in0=Fq[:], in1=Fy[:], op=ALU.mult),
                 reads=[Fqb, Fyb], writes=[qppb])
            yield
            K.op(act, lambda: nc.scalar.activation(out=Fx[:], in_=be[:, 256:512], func=AF.Exp), reads=[beb], writes=[Fxb])
            yield
            K.op(dve, lambda: nc.vector.tensor_tensor(out=kp[:], in0=Fk[:], in1=Fx[:], op=ALU.mult),
                 reads=[Fkb, Fxb], writes=[kpb])
            yield
            tt_, ttb = G()
            for n_, (src, srcb) in enumerate(((qp, qpb), (qpp, qppb), (kp, kpb))):
                for hp in range(2):
                    tr(bf(tt_)[:, (2 * n_ + hp) * P:(2 * n_ + hp + 1) * P], src[:, hp * P:(hp + 1) * P], [srcb], [ttb])
            yield
            K.op(act, lambda: nc.scalar.activation(out=TT[:].rearrange("p a b -> p (a b)"), in_=bf(tt_)[:, 0:768], func=AF.Copy),
                 reads=[ttb], writes=[TTb])
            yield
            bp, bpb = G()
            for g in range(4):
                ro = (g % 2) * 64
                o_ap = bp[ro:ro + 64, (g // 2) * P:(g // 2 + 1) * P]
                if i == 0:
                    mm(o_ap, u_sb[par][:, g * 64:(g + 1) * 64], pm_cur[:, g * P:(g + 1) * P], True, True,
                       [ub[par], *CbL], [bpb])
                    mm(o_ap[:, 0:16], u_sb[par][:, g * 64:(g + 1) * 64], pm_cur0[:, g * 16:(g + 1) * 16], True, True,
                       [ub[par], *CbL], [bpb])
                else:
                    mm(o_ap, u_sb[par][:, g * 64:(g + 1) * 64], pm_cur[:, g * P:(g + 1) * P], True, False,
                       [ub[par], *CbL], [bpb])
                    mm(o_ap[:, 0:16], u_sb[1 - par][:, g * 64:(g + 1) * 64], pm_prev[:, g * 16:(g + 1) * 16], False, True,
                       [ub[1 - par], *CbL], [bpb])
            yield
            K.op(act, lambda: nc.scalar.activation(out=qpp[:], in_=bp[:, 0:256], func=AF.Copy),
                 reads=[bpb], writes=[plTb])
            yield
            bo, bob = G()
            for g in (0, 2, 1, 3):
                ro = (g % 2) * 64
                mm(bo[:, 256 + g * 64:256 + (g + 1) * 64], plT[ro:ro + 64, g // 2, :], pw[ro:ro + 64, (g // 2) * 64:(g // 2 + 1) * 64],
                   True, True, [plTb, Lpw], [bob])
            yield
            ba, bab = G()
            for h in (0, 2, 1, 3):
                hp, r = h // 2, (h % 2) * 64
                for c in range(2):
                    cs = slice(c * 64, (c + 1) * 64)
                    mm(ba[cs, h * 64:(h + 1) * 64], TT[r:r + 64, 4 + hp, cs], TT[r:r + 64, 2 + hp, cs], True, True,
                       [TTb], [bab])
            yield
            for c in range(2):
                cs = slice(c * 64, (c + 1) * 64)
                for h in range(4):
                    hp, r = h // 2, (h % 2) * 64
                    col = 256 + (c * 2 + hp) * 64
                    mm(ba[r:r + 64, col:col + 64], kp[cs, h * 64:(h + 1) * 64], va[cs, h * 64:(h + 1) * 64], True, True,
                       [kpb, vab], [bab])
            yield
            K.op(dve, lambda: nc.vector.tensor_tensor(out=attn_sb[:], in0=ba[:, 0:256], in1=hmask[:], op=ALU.mult),
                 reads=[bab, *CbL], writes=[attnb])
            yield
            for c in range(2):
                for hp in range(2):
                    col = 256 + (c * 2 + hp) * 64
                    K.op(dve, lambda hp=hp, col=col, c=c: nc.vector.scalar_tensor_tensor(
                        out=S[:, hp, :], in0=S[:, hp, :], scalar=dec[:, hp * 2 + c:hp * 2 + c + 1], in1=ba[:, col:col + 64],
                        op0=ALU.mult, op1=ALU.add), reads=[bab, decb], writes=[Sb])
                tgt = 1 if c == 0 else 0
                if c == 0:
                    K.op(dve, lambda: nc.vector.tensor_copy(out=S_bf[1][:], in_=S[:]), reads=[Sb], writes=[S_bfb[1]])
            yield
            for c in range(2):
                cs = slice(c * 64, (c + 1) * 64)
                for h in range(4):
                    mm(bo[cs, h * 64:(h + 1) * 64], attn_sb[cs, h * 64:(h + 1) * 64], va[cs, h * 64:(h + 1) * 64],
                       True, True, [attnb, vab], [bob])
            yield
            bi, bib = G()
            for h in (0, 2, 1, 3):
                hp, r = h // 2, (h % 2) * 64
                for c in range(2):
                    cs = slice(c * 64, (c + 1) * 64)
                    mm(bi[cs, h * 64:(h + 1) * 64], TT[r:r + 64, hp, cs], S_bf[c][r:r + 64, hp, :],
                       True, True, [TTb, S_bfb[c]], [bib])
            yield
            K.op(dve, lambda: nc.vector.tensor_copy(out=S_bf[0][:], in_=S[:]), reads=[Sb], writes=[S_bfb[0]])
            yield
            hgb = smb["hg"]
            K.op(act, lambda: nc.scalar.activation(out=Fx[:], in_=bo[:, 0:256], func=AF.Copy), reads=[bob], writes=[Fxb])
            yield
            K.op(dve, lambda: nc.vector.tensor_tensor(out=Fx[:], in0=Fx[:], in1=bi[:, 0:256], op=ALU.add),
                 reads=[bib], writes=[Fxb])
            yield
            K.op(dve, lambda: nc.vector.tensor_tensor(out=Fy[:], in0=Fx[:], in1=Fx[:], op=ALU.mult), reads=[Fxb], writes=[Fyb])
            yield
            K.op(dve, lambda: nc.vector.reduce_sum(out=small[:, 4:8], in_=Fy[:].rearrange("p (h d) -> p h d", d=64), axis=AX.X),
                 reads=[Fyb], writes=[hgb])
            yield
            rms_rstd(small[:, 4:8], 64.0, small[:, 8:12], hgb)
            yield
            for h in range(4):
                K.op(dve, lambda h=h: nc.vector.scalar_tensor_tensor(
                    out=XYf[1 - par][:, h * 64:(h + 1) * 64], in0=Fx[:, h * 64:(h + 1) * 64], scalar=small[:, 8 + h:9 + h],
                    in1=gate_a[:, h * 64:(h + 1) * 64], op0=ALU.mult, op1=ALU.mult), reads=[Fxb, hgb, gab], writes=[XYb[1 - par]])
            yield
            K.op(dve, lambda: nc.vector.tensor_tensor(out=XYf[1 - par][:, 256:512], in0=bo[:, 256:512], in1=gate_b[:], op=ALU.mult),
                 reads=[bob, gbb], writes=[XYb[1 - par]])


            yield

        groups = []
        for h in range(8):
            for j0 in range(0, i + 1, 4):
                groups.append([(h, j) for j in range(j0, min(j0 + 4, i + 1))])
        rdb = smb["rden"]

        def emit_S(gi):
            stt, stb = ST[gi % NST]
            ptb = PTb[gi % NPT]
            pt = PT[gi % NPT]
            for sl, (h, j) in enumerate(groups[gi]):
                hp, r = h // 2, (h % 2) * 64
                mm(stt[:, sl * P:(sl + 1) * P], KT[:, hp, j * P:(j + 1) * P], (QTa if r == 0 else QTz)[:, hp, :], True, True,
                   [KTb[j], QTb], [stb])
            ng = len(groups[gi])
            K.op(act, lambda: nc.scalar.activation(out=pt[:, 0:ng * P], in_=stt[:, 0:ng * P], func=AF.Exp, scale=0.125),
                 reads=[stb], writes=[ptb])
            gh, gj0 = groups[gi][0]
            vt, vtb = Vt[gi % NVT], Vtb[gi % NVT]
            K.op(dve, lambda: nc.vector.tensor_tensor(
                out=vt[:, 0:ng * 65].rearrange("p (a b) -> p a b", b=65), in0=V[:, gj0:gj0 + ng, gh * 65:(gh + 1) * 65],
                in1=fij[par][:, gj0:gj0 + ng, gh:gh + 1].to_broadcast([P, ng, 65]), op=ALU.mult),
                reads=[biasb[par]] + [Vb[j] for j in range(gj0, gj0 + ng)], writes=[vtb])
            for sl, (h, j) in enumerate(groups[gi]):
                if j == i:
                    K.op(dve, lambda sl=sl: nc.vector.tensor_tensor(
                        out=pt[:, sl * P:(sl + 1) * P], in0=pt[:, sl * P:(sl + 1) * P], in1=fmask[:], op=ALU.mult),
                        reads=[*CbL], writes=[ptb])

        def emit_PV(gi):
            pt, ptb = PT[gi % NPT], PTb[gi % NPT]
            for sl, (h, j) in enumerate(groups[gi]):
                oa, oab = OA[h // 4]
                c0 = (h % 4) * 65
                mm(oa[:, c0:c0 + 65], pt[:, sl * P:(sl + 1) * P], Vt[gi % NVT][:, sl * 65:(sl + 1) * 65], j == 0, j == i,
                   [ptb, Vtb[gi % NVT]], [oab], inc=True)
                if j == i and h % 4 == 3:
                    hb = h - 3
                    K.op(dve, lambda oa=oa: nc.vector.reciprocal(
                        out=small[:, 44:48], in_=oa[:, 0:260].rearrange("p (h e) -> p h e", e=65)[:, :, 64]),
                        reads=[oab], writes=[rdb])
                    for hh in range(4):
                        K.op(dve, lambda hh=hh, oa=oa, hb=hb: nc.vector.scalar_tensor_tensor(
                            out=XYf[1 - par][:, 512 + (hb + hh) * 64:512 + (hb + hh + 1) * 64], in0=oa[:, hh * 65:hh * 65 + 64],
                            scalar=small[:, 44 + hh:45 + hh], in1=gate_c[:, (hb + hh) * 64:(hb + hh + 1) * 64],
                            op0=ALU.mult, op1=ALU.mult), reads=[oab, rdb, gcb], writes=[XYb[1 - par]])

        gen = ab_thread()
        steps_per_group = max(1, -(-12 // len(groups)))
        if next_front is not None:
            if next_front[2] != p:
                load_cast(preg[:], ld[next_front[2]]["preg_bc"][:, :], Lpre)
            front(*next_front)
        gstate[1] = [0, 1, 2]
        emit_S(0)
        if len(groups) > 1:
            emit_S(1)
        for gi in range(len(groups)):
            if gi + 2 < len(groups):
                emit_S(gi + 2)
            emit_PV(gi)
            for _ in range(steps_per_group):
                next(gen, None)
        for _ in gen:
            pass
        gstate[1] = [0, 1, 2, 3, 4, 5]

        tm, tmb = G()
        for ec in range(8):
            tr(bf(tm)[:, ec * P:(ec + 1) * P], XYf[1 - par][:, ec * P:(ec + 1) * P], [XYb[1 - par]], [tmb])
        K.op(act, lambda: nc.scalar.activation(out=XYf[par], in_=bf(tm)[:, 0:D], func=AF.Copy),
             reads=[tmb], writes=[XYb[par]])
        if next_front is not None:
            front_b(1 - par)
        ys = []
        for half in range(2):
            y_, yb_ = G()
            for ec in range(8):
                mm(y_[:, 0:512], XYc[par][:, ec, :], Wout[:, ec, half * 512:(half + 1) * 512], ec == 0, ec == 7,
                   [XYb[par], Wob], [yb_])
            ys.append((y_, yb_))
        if prefetch is not None:
            load_wout(prefetch)
        n2 = smb["n2"]
        for half in range(2):
            y_, yb_ = ys[half]
            K.op(act, lambda y_=y_, half=half: nc.scalar.activation(
                out=hh[:, half * 512:(half + 1) * 512], in_=y_[:, 0:512], func=AF.Square,
                accum_out=small[:, 48 + half:49 + half]), reads=[yb_], writes=[hhb, n2])
        K.op(dve, lambda: nc.vector.tensor_tensor(out=small[:, 50:51], in0=small[:, 48:49], in1=small[:, 49:50], op=ALU.add),
             writes=[n2])
        rms_rstd(small[:, 50:51], float(D), small[:, 51:52], n2)
        for q4 in range(4):
            y_, yb_ = ys[q4 // 2]
            c0 = (q4 % 2) * 256
            K.op(dve, lambda y_=y_, q4=q4, c0=c0: nc.vector.scalar_tensor_tensor(
                out=Fs[q4][:], in0=y_[:, c0:c0 + 256], scalar=small[:, 51:52], in1=postg[:, q4 * 256:(q4 + 1) * 256],
                op0=ALU.mult, op1=ALU.mult), reads=[yb_, n2, Lpost], writes=[Fb[q4]])
            K.op(pool, lambda q4=q4: nc.gpsimd.tensor_tensor(
                out=x_seq[:, i, q4 * 256:(q4 + 1) * 256], in0=x_seq[:, i, q4 * 256:(q4 + 1) * 256],
                in1=Fs[q4][:], op=ALU.add), reads=[Fb[q4]], writes=[Xb[i]])
        finish()

    if 1 in lb_modes:
        compute_lb1(list(lb_modes).index(1))

    x_preloaded = set()
    for s in range(NSEQ):
        for i in range(NB):
            if (s, i) not in x_preloaded:
                load_f32(x_seq[:, i, :], x_d[s, i * P:(i + 1) * P, :], Xb[i])
        for p in range(NL):
            first = (s == 0 and p == 0)
            load_layer(p, with_weights=first, with_preg=first)
            nxt = None
            if p + 1 < NL:
                nxt = p + 1
            elif s + 1 < NSEQ:
                nxt = 0
            K.op(pool, lambda: nc.gpsimd.memset(S[:], 0.0), writes=[Sb])
            K.op(pool, lambda: nc.gpsimd.memset(S_bf[0][:], 0.0), writes=[S_bfb[0]])
            K.op(pool, lambda: nc.gpsimd.memset(small[:, 28:36], 0.0), writes=[smb["acc"]])
            for i in range(NB):
                if i == NB - 1 and p == NL - 1 and s + 1 < NSEQ:
                    for i2 in range(NB - 1):
                        load_f32(x_seq[:, i2, :], x_d[s + 1, i2 * P:(i2 + 1) * P, :], Xb[i2])
                        x_preloaded.add((s + 1, i2))
                bg = (s == 0 and p + 1 < NL and NB >= 2)
                if bg and i == NB - 1:
                    wsc_ready[p + 1] = True
                if i + 1 < NB:
                    nf = (s, i + 1, p)
                elif nxt is not None:
                    nf = (s, 0, p + 1) if p + 1 < NL else (s + 1, 0, 0)
                else:
                    nf = None
                block(s, i, p, p == NL - 1, do_front=(first and i == 0), next_front=nf,
                      prefetch=(nxt if i == NB - 1 else None))
                if bg and i < NB - 1:
                    q = p + 1
                    per = -(-16 // (NB - 1))
                    for pc in range(i * per, min(16, (i + 1) * per)):
                        if pc < 8:
                            K.dma(K.q_pool, lambda pc=pc, q=q: nc.gpsimd.dma_start(
                                out=wsc_in[q][:, pc, :], in_=ld[q]["w_in"][pc * P:(pc + 1) * P, :]), writes=[wsc_ib[q][pc]])
                        else:
                            K.dma(K.q_pool, lambda pc=pc, q=q: nc.gpsimd.dma_start(
                                out=wsc_out[q][:, pc - 8, :], in_=ld[q]["w_out"][(pc - 8) * P:(pc - 7) * P, :]),
                                writes=[wsc_ob[q][pc - 8]])
    sp.wait_all([(l[0], l[1]) for l in K.q_sp.lanes if l[1] > 0])
    return nc


def _layer_inputs(l, lower_bounds, pre_norm_g, w_in, hgrn_norm_g, fox_f_bias, pool_w, pool_scale, w_out, post_norm_g):
    f = np.float32
    bc = lambda v: np.ascontiguousarray(np.broadcast_to(np.asarray(v, f)[None, :], (P, v.shape[0])))
    pwl = np.zeros((P, 128), f)
    for g in range(4):
        pwl[(g % 2) * 64:(g % 2) * 64 + 64, (g // 2) * 64:(g // 2 + 1) * 64] = pool_w[l, g]
    return {
        "w_in": np.ascontiguousarray(w_in[l], f), "w_out": np.ascontiguousarray(w_out[l], f),
        "preg_bc": bc(pre_norm_g[l]), "postg_bc": bc(post_norm_g[l]),
        "hgrng_bc": bc(hgrn_norm_g[l]), "pscale_bc": bc(pool_scale[l]),
        "lbraw_bc": bc(np.concatenate([lower_bounds[0], lower_bounds[1]])),
        "fbias_bc": bc(fox_f_bias[l]), "pw": pwl,
    }


FUSED = True
_cache = {}


def run(x, params, T, NSEQ, ncores, layer_groups):
    consts = _consts()
    cur = np.ascontiguousarray(x, np.float32)
    for grp in layer_groups:
        key = (T, NSEQ, tuple(grp))
        if key not in _cache:
            _cache[key] = build_program(T, NSEQ, [0 if l == 0 else 1 for l in grp])
        nc = _cache[key]
        base = {"c_" + k: v for k, v in consts.items()}
        for p, l in enumerate(grp):
            for k, v in _layer_inputs(l, **params).items():
                base["l%d_%s" % (p, k)] = v
        in_maps = []
        for c in range(ncores):
            m = dict(base)
            m["x"] = np.ascontiguousarray(cur[c * NSEQ:(c + 1) * NSEQ])
            in_maps.append(m)
        res = run_bass_kernel_spmd(nc, in_maps, core_ids=list(range(ncores)))
        cur = np.concatenate([np.asarray(r["out"], np.float32) for r in res.results], axis=0)
    return cur


def kernel(x, lower_bounds, pre_norm_g, w_in, hgrn_norm_g, fox_f_bias, pool_w, pool_scale, w_out, post_norm_g):
    params = dict(lower_bounds=np.asarray(lower_bounds), pre_norm_g=np.asarray(pre_norm_g), w_in=np.asarray(w_in),
                  hgrn_norm_g=np.asarray(hgrn_norm_g), fox_f_bias=np.asarray(fox_f_bias), pool_w=np.asarray(pool_w),
                  pool_scale=np.asarray(pool_scale), w_out=np.asarray(w_out), post_norm_g=np.asarray(post_norm_g))
    x = np.asarray(x)
    B, T, _ = x.shape
    groups = [[0, 1]] if FUSED else [[0], [1]]
    return run(x, params, T, B // 8, 8, groups).astype(np.float32)
```

```python
import numpy as np
import concourse.bass as bass
import concourse.mybir as mybir
from concourse.bass_utils import run_bass_kernel_spmd

F32 = mybir.dt.float32
BF16 = mybir.dt.bfloat16
AF = mybir.ActivationFunctionType
ALU = mybir.AluOpType
AX = mybir.AxisListType

D = 1024
INW = 3592
EPS = 1e-6
TINY = 1e-30
P = 128
POOL_WINDOWS = (2, 4, 8, 16)
EPOCH = 24000


CLOCKS = {}


class Buf:
    __slots__ = ("w", "r", "excl")

    def __init__(self, excl=False):
        self.w = None
        self.r = {}
        self.excl = excl


class Eng:
    def __init__(self, nc, eng, name):
        self.nc, self.eng, self.name = nc, eng, name
        self.sem = nc.alloc_semaphore(name + "_s0")
        self.own = {id(self.sem)}
        self.n = 0
        self.ep = 0
        self.seen = {}
        self.pending = 0

    def wait_all(self, deps):
        deps = [d for d in deps if self.seen.get(id(d[0]), 0) < d[1]]
        if len(deps) > 1:
            keep = []
            for d in deps:
                k = id(d[0])
                implied = False
                for d2 in deps:
                    if d2 is not d:
                        c2 = CLOCKS.get((id(d2[0]), d2[1]))
                        if c2 is not None and c2.get(k, 0) >= d[1]:
                            implied = True
                            break
                if not implied:
                    keep.append(d)
            deps = keep
        for sem, val in deps:
            key = id(sem)
            if self.seen.get(key, 0) < val:
                self.eng.wait_ge(sem, val)
                self.seen[key] = val
            c = CLOCKS.get((key, val))
            if c is not None:
                seen = self.seen
                for k2, v2 in c.items():
                    if seen.get(k2, 0) < v2:
                        seen[k2] = v2

    def snapshot(self, tok):
        c = dict(self.seen)
        c[id(tok[0])] = max(c.get(id(tok[0]), 0), tok[1])
        CLOCKS[(id(tok[0]), tok[1])] = c

    def issue(self, ins):
        if self.n >= EPOCH and self.pending == 0:
            self.ep += 1
            self.sem = self.nc.alloc_semaphore("%s_s%d" % (self.name, self.ep))
            self.own.add(id(self.sem))
            self.n = 0
        self.n += 1
        ins.then_inc(self.sem, 1)
        self.pending = 0
        return (self.sem, self.n)

    def issue_noinc(self, ins):
        if self.n >= EPOCH:
            pass
        self.pending += 1
        return (self.sem, self.n + 1)


class DmaQ:
    def __init__(self, nc, E, name, nlanes):
        self.E = E
        self.lanes = [[nc.alloc_semaphore("%s_l%d" % (name, i)), 0] for i in range(nlanes)]
        self.k = 0

    def issue(self, fn, deps):
        lane = self.lanes[self.k]
        self.k = (self.k + 1) % len(self.lanes)
        d = set(deps)
        if lane[1] > 0:
            d.add((lane[0], lane[1]))
        self.E.wait_all(d)
        lane[1] += 16
        fn().then_inc(lane[0], 16)
        tok = (lane[0], lane[1])
        self.E.snapshot(tok)
        return tok


class Ctx:
    def __init__(self, nc):
        self.nc = nc
        self.pe = Eng(nc, nc.tensor, "pe")
        self.act = Eng(nc, nc.scalar, "act")
        self.dve = Eng(nc, nc.vector, "dve")
        self.pool = Eng(nc, nc.gpsimd, "pool")
        self.sp = Eng(nc, nc.sync, "sp")
        self.q_sp = DmaQ(nc, self.sp, "qsp", 8)
        self.q_pool = DmaQ(nc, self.pool, "qpl", 16)

    def _deps(self, reads, writes, E=None):
        deps = set()
        own = E.own if E is not None else ()
        is_pe = E is self.pe
        for b in reads:
            if b.w is not None and not (is_pe and id(b.w[0]) in own):
                deps.add(b.w)
            if b.excl:
                for tok in b.r.values():
                    if id(tok[0]) not in own:
                        deps.add(tok)
        for b in writes:
            if b.w is not None and not (is_pe and id(b.w[0]) in own):
                deps.add(b.w)
            for tok in b.r.values():
                if not (is_pe and id(tok[0]) in own):
                    deps.add(tok)
        return deps

    @staticmethod
    def _commit(tok, reads, writes):
        for b in reads:
            b.r[id(tok[0])] = tok
        for b in writes:
            b.w = tok
            b.r = {}

    def op(self, E, fn, reads=(), writes=(), inc=True, mode=None):
        if mode is not None and mode != getattr(self, "pe_mode", None):
            if E.n > 0:
                assert E.pending == 0
                E.eng.wait_ge(E.sem, E.n)
                E.seen[id(E.sem)] = E.n
            self.pe_mode = mode
        E.wait_all(self._deps(reads, writes, E))
        tok = E.issue(fn()) if inc else E.issue_noinc(fn())
        if inc:
            E.snapshot(tok)
        elif (id(tok[0]), tok[1]) not in CLOCKS:
            E.snapshot(tok)
        self._commit(tok, reads, writes)
        return tok

    def dma(self, Q, fn, reads=(), writes=()):
        tok = Q.issue(fn, self._deps(reads, writes))
        self._commit(tok, reads, writes)
        return tok


def _consts():
    s = np.arange(P)[:, None]
    t = np.arange(P)[None, :]
    same = (s // 64) == (t // 64)
    c = {}
    c["ident"] = np.eye(P, dtype=np.float32)
    c["ucum"] = (same & (s <= t)).astype(np.float32)
    c["lstr"] = (same & (s > t)).astype(np.float32)
    c["ufull"] = (s <= t).astype(np.float32)
    c["ones"] = np.ones((P, P), np.float32)
    ind = np.zeros((P, 2), np.float32)
    ind[:64, 0] = 1
    ind[64:, 1] = 1
    c["ind"] = ind
    c["fmask"] = (s <= t).astype(np.float32)
    hm = np.zeros((P, 256), np.float32)
    for h in range(4):
        hm[:, h * 64:(h + 1) * 64] = ((np.arange(P)[:, None] % 64) <= np.arange(64)[None, :])
    c["hmask"] = hm
    pm_cur = np.zeros((P, 4, P), np.float32)
    pm_prev = np.zeros((P, 4, 16), np.float32)
    pm_cur0 = np.zeros((P, 4, 16), np.float32)
    for g, w in enumerate(POOL_WINDOWS):
        for tt in range(P):
            for ss in range(tt - w + 1, tt + 1):
                if ss >= 0:
                    pm_cur[ss, g, tt] += 1.0 / w
                else:
                    if tt < 16:
                        pm_prev[ss + P, g, tt] += 1.0 / w
            pm_cur[tt, g, tt] -= 1.0
        for tt in range(16):
            cnt = min(tt + 1, w)
            for ss in range(max(0, tt - w + 1), tt + 1):
                pm_cur0[ss, g, tt] += 1.0 / cnt
            pm_cur0[tt, g, tt] -= 1.0
    c["pm_cur"] = pm_cur.reshape(P, 4 * P)
    c["pm_prev"] = pm_prev.reshape(P, 64)
    c["pm_cur0"] = pm_cur0.reshape(P, 64)
    return c


CONST_SHAPES = {"ident": (P, P), "ucum": (P, P), "lstr": (P, P), "ufull": (P, P), "ones": (P, P),
                "ind": (P, 2), "fmask": (P, P), "hmask": (P, 256), "pm_cur": (P, 512),
                "pm_prev": (P, 64), "pm_cur0": (P, 64)}
LAYER_SHAPES = {"w_in": (D, INW), "w_out": (D, D), "preg_bc": (P, D), "postg_bc": (P, D),
                "hgrng_bc": (P, 256), "pscale_bc": (P, 256), "lbraw_bc": (P, 512),
                "fbias_bc": (P, 8), "pw": (P, 128)}


def build_program(T, NSEQ, lb_modes, debug_out=False):
    NL = len(lb_modes)
    NB = T // P
    nc = bass.Bass("TRN2", target_bir_lowering=False)
    CLOCKS.clear()
    K = Ctx(nc)
    pe, act, dve, pool, sp = K.pe, K.act, K.dve, K.pool, K.sp

    x_d = nc.dram_tensor("x", [NSEQ, T, D], F32, kind="ExternalInput").ap()
    out_d = nc.dram_tensor("out", [NSEQ, T, D], F32, kind="ExternalOutput").ap()
    cd = {k: nc.dram_tensor("c_" + k, list(v), F32, kind="ExternalInput").ap() for k, v in CONST_SHAPES.items()}
    ld = [{k: nc.dram_tensor("l%d_%s" % (p, k), list(v), F32, kind="ExternalInput").ap()
           for k, v in LAYER_SHAPES.items()} for p in range(NL)]

    wsc_in = [nc.dram_tensor("wsc_in%d" % p, [P, 8, INW], BF16, kind="Internal").ap() for p in range(NL)]
    wsc_out = [nc.dram_tensor("wsc_out%d" % p, [P, 8, D], BF16, kind="Internal").ap() for p in range(NL)]
    wsc_ib = [[Buf() for _ in range(8)] for _ in range(NL)]
    wsc_ob = [[Buf() for _ in range(8)] for _ in range(NL)]
    wsc_ready = [False] * NL

    def sb(name, shape, dt):
        return nc.alloc_sbuf_tensor(name, list(shape), dt)

    x_seq = sb("x_seq", [P, NB, D], F32)
    Xb = [Buf() for _ in range(NB)]
    Win = sb("Win", [P, 8, INW], BF16)
    Wout = sb("Wout", [P, 8, D], BF16)
    Wib = Buf()
    Wob = Buf()
    KT = sb("KT", [P, 4, T], BF16)
    KTb = [Buf() for _ in range(NB)]
    V = sb("V", [P, NB, 8 * 65], BF16)
    Vb = [Buf() for _ in range(NB)]
    Vones = Buf()
    ident = sb("ident", [P, P], BF16)
    ucum = sb("ucum", [P, P], F32)
    lstr = sb("lstr", [P, P], F32)
    ufull = sb("ufull", [P, P], F32)
    ones = sb("ones", [P, P], F32)
    ind = sb("ind", [P, 2], F32)
    fmask = sb("fmask", [P, P], BF16)
    hmask = sb("hmask", [P, 256], BF16)
    pm_cur = sb("pm_cur", [P, 512], BF16)
    pm_prev = sb("pm_prev", [P, 64], BF16)
    pm_cur0 = sb("pm_cur0", [P, 64], BF16)
    CbL = [Buf() for _ in range(len(CONST_SHAPES))]
    preg = sb("preg", [P, D], BF16)
    postg = sb("postg", [P, D], BF16)
    hgrng = sb("hgrng", [P, 256], BF16)
    pscale = sb("pscale", [P, 256], BF16)
    lb_bc = sb("lb_bc", [P, 256], F32)
    oml_bc = sb("oml_bc", [P, 256], F32)
    lbraw = None
    fbias = sb("fbias", [P, 8], F32)
    pw = sb("pw", [P, 128], BF16)
    Lpre, Lpost, Lpw, Lhg, Lps, Lfb, Llb = (Buf() for _ in range(7))
    hm = sb("hm", [P, D], BF16)
    hmb = Buf()
    hh = sb("hh", [P, D], BF16)
    hhb = Buf()
    hT = sb("hT", [P, 8, P], BF16)
    hTb = Buf()
    XYf = [hT[:].rearrange("p a b -> p (a b)"), hm[:]]
    XYc = [hT[:], hm[:].rearrange("p (a b) -> p a b", b=P)]
    XYb = [hTb, hmb]
    NF = 5
    Fs = [sb("F%d" % i, [P, 256], F32) for i in range(NF)]
    Fb = [Buf() for _ in range(NF)]
    ebig = sb("ebig", [P, 256], F32)
    ebb = Buf()
    gate_a = sb("gate_a", [P, 256], BF16)
    gate_b = sb("gate_b", [P, 256], BF16)
    gate_c = sb("gate_c", [P, 512], BF16)
    gab, gbb, gcb = Buf(), Buf(), Buf()
    qp = sb("qp", [P, 256], BF16)
    qpp = sb("qpp", [P, 256], BF16)
    kp = sb("kp", [P, 256], BF16)
    va = sb("va", [P, 256], BF16)
    attn_sb = qp
    qpb, qppb, kpb, vab = Buf(), Buf(), Buf(), Buf()
    attnb = qpb
    TT = sb("TT", [P, 6, P], BF16)
    TTb = Buf()
    S = sb("S", [P, 2, 64], F32)
    Sb = Buf()
    S_bf = [sb("S_bf%d" % i, [P, 2, 64], BF16) for i in range(2)]
    S_bfb = [Buf(), Buf()]
    dec = sb("dec", [P, 4], F32)
    decb = Buf()
    u_sb = [sb("u_sb%d" % i, [P, 256], BF16) for i in range(2)]
    ub = [Buf(), Buf()]
    plT = qpp[:].rearrange("p (a b) -> p a b", b=P)
    plTb = qppb
    q_sb = Fs[3][:].bitcast(BF16)
    k_sb = Fs[4][:].bitcast(BF16)
    qsb_b, ksb_b = Fb[3], Fb[4]
    QTa = sb("QTa", [P, 4, P], BF16)
    QTz = sb("QTz", [P, 4, P], BF16)
    QTb = Buf()
    NPT = 3
    NVT = 3
    Vt = [sb("Vt%d" % i, [P, 4 * 65], BF16) for i in range(NVT)]
    Vtb = [Buf() for _ in range(NVT)]
    PT = [sb("PT%d" % i, [P, 512], BF16) for i in range(NPT)]
    PTb = [Buf() for _ in range(NPT)]
    fij = [sb("fij%d" % i, [P, NB, 8], F32) for i in range(2)]
    biasb = [Buf(), Buf()]
    totall = sb("totall", [P, NB, 8], F32)
    small = sb("small", [P, 64], F32)
    smb = {k: Buf() for k in ("n1", "hg", "fc", "acc", "tot", "rden", "n2", "w")}

    banks = [nc.alloc_psum_tensor("bank%d" % i, [P, 512], F32) for i in range(8)]
    bankb = [Buf(excl=True) for _ in range(8)]
    gstate = [0, [0, 1, 2, 3, 4, 5]]

    def G():
        lst = gstate[1]
        i = lst[gstate[0] % len(lst)]
        gstate[0] += 1
        return banks[i], bankb[i]

    NST = 3
    ST = [(banks[3], bankb[3]), (banks[4], bankb[4]), (banks[5], bankb[5])]
    OA = [(banks[6], bankb[6]), (banks[7], bankb[7])]

    def bf(t):
        return t[:].bitcast(BF16)

    def load_cast(dst_ap, src_ap, buf):
        K.dma(K.q_pool, lambda: nc.gpsimd.dma_start(out=dst_ap, in_=src_ap), writes=[buf])

    def load_f32(dst_ap, src_ap, buf):
        K.dma(K.q_sp, lambda: nc.sync.dma_start(out=dst_ap, in_=src_ap), writes=[buf])

    _cseen = []
    for name, t_, cast in (("ident", ident, 1), ("ucum", ucum, 0), ("lstr", lstr, 0), ("ufull", ufull, 0),
                           ("ones", ones, 0), ("ind", ind, 0), ("fmask", fmask, 1), ("hmask", hmask, 1),
                           ("pm_cur", pm_cur, 1), ("pm_prev", pm_prev, 1), ("pm_cur0", pm_cur0, 1)):
        (load_cast if cast else load_f32)(t_[:], cd[name][:, :], CbL[len(_cseen)])
        _cseen.append(name)
    Vv = V[:].rearrange("p b (h e) -> p b h e", e=65)
    K.op(pool, lambda: nc.gpsimd.memset(QTa[:], 0.0), writes=[QTb])
    K.op(pool, lambda: nc.gpsimd.memset(QTz[:], 0.0), writes=[QTb])

    def load_win(p):
        if wsc_ready[p]:
            for dc in range(8):
                K.dma(K.q_sp, lambda dc=dc: nc.sync.dma_start(out=Win[:, dc, :], in_=wsc_in[p][:, dc, :]),
                      reads=[wsc_ib[p][dc]], writes=[Wib])
        else:
            for dc in range(8):
                load_cast(Win[:, dc, :], ld[p]["w_in"][dc * P:(dc + 1) * P, :], Wib)
            for dc in range(8):
                K.dma(K.q_sp, lambda dc=dc: nc.sync.dma_start(out=wsc_in[p][:, dc, :], in_=Win[:, dc, :]),
                      reads=[Wib], writes=[wsc_ib[p][dc]])

    def load_wout(p):
        if wsc_ready[p]:
            for ec in range(8):
                K.dma(K.q_sp, lambda ec=ec: nc.sync.dma_start(out=Wout[:, ec, :], in_=wsc_out[p][:, ec, :]),
                      reads=[wsc_ob[p][ec]], writes=[Wob])
        else:
            for ec in range(8):
                load_cast(Wout[:, ec, :], ld[p]["w_out"][ec * P:(ec + 1) * P, :], Wob)
            for ec in range(8):
                K.dma(K.q_sp, lambda ec=ec: nc.sync.dma_start(out=wsc_out[p][:, ec, :], in_=Wout[:, ec, :]),
                      reads=[Wob], writes=[wsc_ob[p][ec]])
            wsc_ready[p] = True

    def load_layer(p, with_weights=True, with_preg=True):
        L = ld[p]
        if with_weights:
            load_win(p)
            load_wout(p)
        if with_preg:
            load_cast(preg[:], L["preg_bc"][:, :], Lpre)
        load_cast(postg[:], L["postg_bc"][:, :], Lpost)
        load_cast(pw[:], L["pw"][:, :], Lpw)
        load_cast(hgrng[:], L["hgrng_bc"][:, :], Lhg)
        load_cast(pscale[:], L["pscale_bc"][:, :], Lps)
        load_f32(fbias[:], L["fbias_bc"][:, :], Lfb)

    def compute_lb1(p):
        L = ld[p]
        load_f32(Fs[1][:], L["lbraw_bc"][:, 0:256], Fb[1])
        load_f32(Fs[2][:], L["lbraw_bc"][:, 256:512], Fb[2])
        K.op(pool, lambda: nc.gpsimd.tensor_tensor(out=Fs[0][:], in0=Fs[1][:], in1=Fs[2][:],
                                                   op=ALU.subtract), reads=[Fb[1], Fb[2]], writes=[Fb[0]])
        K.op(act, lambda: nc.scalar.activation(out=Fs[0][:], in_=Fs[0][:], func=AF.Exp),
             reads=[], writes=[Fb[0]])
        K.op(pool, lambda: nc.gpsimd.tensor_scalar(out=Fs[0][:], in0=Fs[0][:], scalar1=1.0, scalar2=1.0,
                                                   op0=ALU.mult, op1=ALU.add), writes=[Fb[0]])
        K.op(dve, lambda: nc.vector.reciprocal(out=lb_bc[:], in_=Fs[0][:]), reads=[Fb[0]], writes=[Llb])
        K.op(pool, lambda: nc.gpsimd.tensor_scalar(out=oml_bc[:], in0=lb_bc[:], scalar1=-1.0, scalar2=1.0,
                                                   op0=ALU.mult, op1=ALU.add), writes=[Llb])

    def silu_gate(ps_ap, width, out_ap, outb, mul_ap=None, psb=None):
        e = ebig[:, 0:width]
        K.op(act, lambda: nc.scalar.activation(out=e, in_=ps_ap, func=AF.Exp, scale=-1.0),
             reads=[psb], writes=[ebb])
        K.op(act, lambda: nc.scalar.activation(out=e, in_=e, func=AF.Ln, bias=1.0), writes=[ebb])
        K.op(act, lambda: nc.scalar.activation(out=e, in_=e, func=AF.Exp, scale=-1.0), writes=[ebb])
        if mul_ap is None:
            K.op(dve, lambda: nc.vector.tensor_tensor(out=out_ap, in0=ps_ap, in1=e, op=ALU.mult),
                 reads=[psb, ebb], writes=[outb])
        else:
            K.op(dve, lambda: nc.vector.tensor_tensor(out=e, in0=ps_ap, in1=e, op=ALU.mult),
                 reads=[psb], writes=[ebb])
            K.op(pool, lambda: nc.gpsimd.tensor_tensor(out=out_ap, in0=e, in1=mul_ap, op=ALU.mult),
                 reads=[ebb, Lhg, Lps], writes=[outb])

    def rms_rstd(ss_ap, n, out_ap, b):
        K.op(dve, lambda: nc.vector.tensor_scalar(out=out_ap, in0=ss_ap, scalar1=1.0 / n, scalar2=EPS,
                                                  op0=ALU.mult, op1=ALU.add), writes=[b])
        K.op(act, lambda: nc.scalar.activation(out=out_ap, in_=out_ap, func=AF.Ln), writes=[b])
        K.op(act, lambda: nc.scalar.activation(out=out_ap, in_=out_ap, func=AF.Exp, scale=-0.5), writes=[b])

    def _cls(n):
        return 32 if n <= 32 else (64 if n <= 64 else 128)

    def mm(out, lhsT, rhs, start, stop, reads, writes, inc=None):
        kc = _cls(lhsT.shape[0])
        mode = ("mm", str(lhsT.dtype), kc, _cls(lhsT.shape[-1]), lhsT.base_partition() if kc < 128 else 0)
        return K.op(pe, lambda: nc.tensor.matmul(out, lhsT, rhs, start=start, stop=stop), reads=reads, writes=writes,
                    inc=(bool(stop) or kc < 128) if inc is None else inc, mode=mode)

    def tr(out, in_, reads, writes):
        return K.op(pe, lambda: nc.tensor.transpose(out, in_, ident[:]), reads=list(reads) + [*CbL], writes=writes,
                    mode=("tr",))

    def front(s, i, p):
        xb = x_seq[:, i, :]
        n1 = smb["n1"]
        K.op(act, lambda: nc.scalar.activation(out=hh[:], in_=xb, func=AF.Square, accum_out=small[:, 0:1]),
             reads=[Xb[i]], writes=[hhb, n1])
        rms_rstd(small[:, 0:1], float(D), small[:, 2:3], n1)
        K.op(dve, lambda: nc.vector.scalar_tensor_tensor(out=hh[:], in0=xb, scalar=small[:, 2:3], in1=preg[:],
                                                         op0=ALU.mult, op1=ALU.mult),
             reads=[Xb[i], n1, Lpre], writes=[hhb])

    def front_b(tp):
        tb, tbb = G()
        for dc in range(8):
            tr(bf(tb)[:, dc * P:(dc + 1) * P], hh[:, dc * P:(dc + 1) * P], [hhb], [tbb])
        K.op(act, lambda: nc.scalar.activation(out=XYf[tp], in_=bf(tb)[:, 0:D], func=AF.Copy),
             reads=[tbb], writes=[XYb[tp]])


    def block(s, i, p, last_layer, do_front=True, next_front=None, prefetch=None):
        xb = x_seq[:, i, :]
        par = i % 2

        def finish():
            if last_layer:
                K.dma(K.q_sp, lambda: nc.sync.dma_start(out=out_d[s, i * P:(i + 1) * P, :], in_=x_seq[:, i, :]),
                      reads=[Xb[i]])
        if do_front:
            front(s, i, p)
            front_b(i % 2)
        def proj(col0, width):
            b_, bb_ = G()
            for dc in range(8):
                mm(b_[:, 0:width], XYc[par][:, dc, :], Win[:, dc, col0:col0 + width], dc == 0, dc == 7,
                   [XYb[par], Wib], [bb_])
            return b_, bb_

        b7, b7b = proj(3584, 8)
        fcb = smb["fc"]
        K.op(dve, lambda: nc.vector.tensor_tensor(out=small[:, 12:20], in0=b7[:, 0:8], in1=fbias[:], op=ALU.add),
             reads=[b7b, Lfb], writes=[fcb])
        K.op(act, lambda: nc.scalar.activation(out=small[:, 12:20], in_=small[:, 12:20], func=AF.Exp, scale=-1.0),
             writes=[fcb])
        K.op(pool, lambda: nc.gpsimd.tensor_scalar(out=small[:, 12:20], in0=small[:, 12:20], scalar1=1.0,
                                                   scalar2=1.0, op0=ALU.mult, op1=ALU.add), writes=[fcb])
        K.op(act, lambda: nc.scalar.activation(out=small[:, 20:28], in_=small[:, 12:20], func=AF.Ln), writes=[fcb])
        sp_ap = small[:, 20:28]
        acc_ap = small[:, 28:36]
        accb = smb["acc"]
        b3, b3b = proj(1536, 512)
        K.op(act, lambda: nc.scalar.activation(out=q_sb, in_=b3[:, 0:512], func=AF.Copy), reads=[b3b], writes=[qsb_b])
        b4, b4b = proj(2048, 512)
        K.op(dve, lambda: nc.vector.tensor_copy(out=k_sb, in_=b4[:, 0:512]), reads=[b4b], writes=[ksb_b])
        tq, tqb = G()
        for hp in range(4):
            tr(bf(tq)[:, hp * P:(hp + 1) * P], q_sb[:, hp * P:(hp + 1) * P], [qsb_b], [tqb])
        for hp in range(4):
            tr(bf(tq)[:, (4 + hp) * P:(5 + hp) * P], k_sb[:, hp * P:(hp + 1) * P], [ksb_b], [tqb])
        K.op(act, lambda: nc.scalar.activation(out=QTa[0:64].rearrange("p a b -> p (a b)"), in_=bf(tq)[0:64, 0:512], func=AF.Copy),
             reads=[tqb], writes=[QTb])
        K.op(act, lambda: nc.scalar.activation(out=QTz[64:128].rearrange("p a b -> p (a b)"), in_=bf(tq)[64:128, 0:512], func=AF.Copy),
             reads=[tqb], writes=[QTb])
        K.op(act, lambda: nc.scalar.activation(out=KT[:, :, i * P:(i + 1) * P],
                                               in_=bf(tq)[:, 512:1024].rearrange("p (a b) -> p a b", b=P),
                                               func=AF.Copy), reads=[tqb], writes=[KTb[i]])

        bc_, bcb = G()
        mm(bc_[:, 0:8], ufull[:], sp_ap, True, False, [*CbL, fcb], [bcb])
        mm(bc_[:, 0:8], ones[:], acc_ap, False, True, [*CbL, accb], [bcb])
        K.op(pool, lambda: nc.gpsimd.tensor_tensor(out=acc_ap, in0=acc_ap, in1=sp_ap, op=ALU.add),
             reads=[fcb], writes=[accb])
        mm(bc_[:, 8:16], ones[:], acc_ap, True, True, [*CbL, accb], [bcb])
        totb = smb["tot"]
        wb = smb["w"]
        K.op(dve, lambda: nc.vector.tensor_copy(out=totall[:, i, :], in_=bc_[:, 8:16]), reads=[bcb], writes=[totb])
        K.op(dve, lambda: nc.vector.tensor_tensor(out=small[:, 52:60], in0=bc_[:, 0:8], in1=totall[:, i, :],
                                                  op=ALU.subtract), reads=[bcb, totb], writes=[wb])
        K.op(act, lambda: nc.scalar.activation(out=small[:, 52:60], in_=small[:, 52:60], func=AF.Exp), writes=[wb])
        K.op(pool, lambda: nc.gpsimd.tensor_tensor(
            out=fij[par][:, 0:i + 1, :], in0=totall[:, 0:i + 1, :],
            in1=totall[:, i:i + 1, :].to_broadcast([P, i + 1, 8]), op=ALU.subtract),
            reads=[totb], writes=[biasb[par]])
        K.op(act, lambda: nc.scalar.activation(out=fij[par][:, 0:i + 1, :], in_=fij[par][:, 0:i + 1, :], func=AF.Exp),
             writes=[biasb[par]])
        b5, b5b = proj(2560, 512)
        K.op(dve, lambda: nc.vector.tensor_tensor(
            out=Vv[:, i, :, 0:64], in0=b5[:, 0:512].rearrange("p (h d) -> p h d", d=64),
            in1=small[:, 52:60].unsqueeze(2).to_broadcast([P, 8, 64]), op=ALU.mult),
            reads=[b5b, smb["w"]], writes=[Vb[i]])
        K.op(pool, lambda: nc.gpsimd.tensor_copy(out=Vv[:, i, :, 64], in_=small[:, 52:60]),
             reads=[smb["w"]], writes=[Vb[i]])
        b6, b6b = proj(3072, 512)
        silu_gate(b6[:, 0:256], 256, gate_c[:, 0:256], gcb, psb=b6b)
        silu_gate(b6[:, 256:512], 256, gate_c[:, 256:512], gcb, psb=b6b)
        gstate[1] = [0, 1, 2]
        b0, b0b = proj(0, 512)
        Fq, Fqb = Fs[0], Fb[0]
        Fk, Fkb = Fs[1], Fb[1]
        Fl, Flb = Fs[2], Fb[2]
        Fx, Fxb = Fs[3], Fb[3]
        Fy, Fyb = Fs[4], Fb[4]
        b1, b1b = proj(512, 512)
        b2, b2b = proj(1024, 512)
        if prefetch is not None:
            load_win(prefetch)
        def ab_thread():
            K.op(act, lambda: nc.scalar.activation(out=Fy[:], in_=b0[:, 256:512], func=AF.Exp, scale=-1.0),
                 reads=[b0b], writes=[Fyb])
            yield
            K.op(act, lambda: nc.scalar.activation(out=Fy[:], in_=Fy[:], func=AF.Ln, bias=1.0), writes=[Fyb])
            yield
            K.op(act, lambda: nc.scalar.activation(out=Fy[:], in_=Fy[:], func=AF.Exp, scale=-1.0), writes=[Fyb])
            yield
            if lb_modes[p] == 0:
                K.op(pool, lambda: nc.gpsimd.tensor_scalar(out=Fk[:], in0=Fy[:], scalar1=-1.0, scalar2=1.0,
                                                           op0=ALU.mult, op1=ALU.add), reads=[Fyb], writes=[Fkb])
                yield
                K.op(dve, lambda: nc.vector.tensor_scalar(out=Fy[:], in0=Fy[:], scalar1=TINY, scalar2=None,
                                                          op0=ALU.max), writes=[Fyb])
                yield
            else:
                K.op(pool, lambda: nc.gpsimd.tensor_tensor(out=Fy[:], in0=Fy[:], in1=oml_bc[:], op=ALU.mult),
                     reads=[Llb], writes=[Fyb])
                yield
                K.op(pool, lambda: nc.gpsimd.tensor_tensor(out=Fk[:], in0=oml_bc[:], in1=Fy[:], op=ALU.subtract),
                     reads=[Llb, Fyb], writes=[Fkb])
                yield
                K.op(dve, lambda: nc.vector.scalar_tensor_tensor(out=Fy[:], in0=Fy[:], scalar=TINY, in1=lb_bc[:],
                                                                 op0=ALU.max, op1=ALU.add), reads=[Llb], writes=[Fyb])
                yield
            K.op(act, lambda: nc.scalar.activation(out=Fl[:], in_=Fy[:], func=AF.Ln), reads=[Fyb], writes=[Flb])

            yield
            K.op(act, lambda: nc.scalar.activation(out=Fx[:], in_=b0[:, 0:256], func=AF.Exp, scale=-1.0),
                 reads=[b0b], writes=[Fxb])
            yield
            K.op(act, lambda: nc.scalar.activation(out=Fx[:], in_=Fx[:], func=AF.Ln, bias=1.0), writes=[Fxb])
            yield
            K.op(act, lambda: nc.scalar.activation(out=Fx[:], in_=Fx[:], func=AF.Exp, scale=-1.0), writes=[Fxb])
            yield
            K.op(dve, lambda: nc.vector.tensor_tensor(out=Fq[:], in0=b0[:, 0:256], in1=Fx[:], op=ALU.mult),
                 reads=[b0b, Fxb], writes=[Fqb])
            yield
            K.op(act, lambda: nc.scalar.activation(out=va[:], in_=b1[:, 0:256], func=AF.Copy), reads=[b1b], writes=[vab])
            yield
            silu_gate(b1[:, 256:512], 256, gate_a[:], gab, mul_ap=hgrng[:], psb=b1b)
            yield
            K.op(act, lambda: nc.scalar.activation(out=u_sb[par][:], in_=b2[:, 0:256], func=AF.Copy),
                 reads=[b2b], writes=[ub[par]])
            yield
            silu_gate(b2[:, 256:512], 256, gate_b[:], gbb, mul_ap=pscale[:], psb=b2b)

            yield
            be, beb = G()
            mm(be[:, 0:256], ucum[:], Fl[:], True, True, [*CbL, Flb], [beb])
            yield
            mm(be[:, 256:512], lstr[:], Fl[:], True, True, [*CbL, Flb], [beb])
            yield
            bd, bdb = G()
            for hp in range(2):
                mm(bd[:, hp * 2:hp * 2 + 2], Fl[:, hp * P:(hp + 1) * P], ind[:], True, True, [*CbL, Flb], [bdb])
            yield
            K.op(act, lambda: nc.scalar.activation(out=dec[:], in_=bd[:, 0:4], func=AF.Exp), reads=[bdb], writes=[decb])
            yield
            K.op(act, lambda: nc.scalar.activation(out=Fx[:], in_=be[:, 0:256], func=AF.Exp), reads=[beb], writes=[Fxb])
            yield
            K.op(dve, lambda: nc.vector.tensor_tensor(out=qp[:], in0=Fq[:], in1=Fx[:], op=ALU.mult),
                 reads=[Fqb, Fxb], writes=[qpb])
            yield
            K.op(act, lambda: nc.scalar.activation(out=Fy[:], in_=be[:, 256:512], func=AF.Exp, scale=-1.0),
                 reads=[beb], writes=[Fyb])
            yield
            K.op(dve, lambda: nc.vector.tensor_tensor(out=qpp[:], in0=Fq[:], in1=Fy[:], op=ALU.mult),
                 reads=[Fqb, Fyb], writes=[qppb])
            yield
            K.op(act, lambda: nc.scalar.activation(out=Fx[:], in_=be[:, 256:512], func=AF.Exp), reads=[beb], writes=[Fxb])
            yield
            K.op(dve, lambda: nc.vector.tensor_tensor(out=kp[:], in0=Fk[:], in1=Fx[:], op=ALU.mult),
                 reads=[Fkb, Fxb], writes=[kpb])
            yield
            tt_, ttb = G()
            for n_, (src, srcb) in enumerate(((qp, qpb), (qpp, qppb), (kp, kpb))):
                for hp in range(2):
                    tr(bf(tt_)[:, (2 * n_ + hp) * P:(2 * n_ + hp + 1) * P], src[:, hp * P:(hp + 1) * P], [srcb], [ttb])
            yield
            K.op(act, lambda: nc.scalar.activation(out=TT[:].rearrange("p a b -> p (a b)"), in_=bf(tt_)[:, 0:768], func=AF.Copy),
                 reads=[ttb], writes=[TTb])
            yield
            bp, bpb = G()
            for g in range(4):
                ro = (g % 2) * 64
                o_ap = bp[ro:ro + 64, (g // 2) * P:(g // 2 + 1) * P]
                if i == 0:
                    mm(o_ap, u_sb[par][:, g * 64:(g + 1) * 64], pm_cur[:, g * P:(g + 1) * P], True, True,
                       [ub[par], *CbL], [bpb])
                    mm(o_ap[:, 0:16], u_sb[par][:, g * 64:(g + 1) * 64], pm_cur0[:, g * 16:(g + 1) * 16], True, True,
                       [ub[par], *CbL], [bpb])
                else:
                    mm(o_ap, u_sb[par][:, g * 64:(g + 1) * 64], pm_cur[:, g * P:(g + 1) * P], True, False,
                       [ub[par], *CbL], [bpb])
                    mm(o_ap[:, 0:16], u_sb[1 - par][:, g * 64:(g + 1) * 64], pm_prev[:, g * 16:(g + 1) * 16], False, True,
                       [ub[1 - par], *CbL], [bpb])
            yield
            K.op(act, lambda: nc.scalar.activation(out=qpp[:], in_=bp[:, 0:256], func=AF.Copy),
                 reads=[bpb], writes=[plTb])
            yield
            bo, bob = G()
            for g in (0, 2, 1, 3):
                ro = (g % 2) * 64
                mm(bo[:, 256 + g * 64:256 + (g + 1) * 64], plT[ro:ro + 64, g // 2, :], pw[ro:ro + 64, (g // 2) * 64:(g // 2 + 1) * 64],
                   True, True, [plTb, Lpw], [bob])
            yield
            ba, bab = G()
            for h in (0, 2, 1, 3):
                hp, r = h // 2, (h % 2) * 64
                for c in range(2):
                    cs = slice(c * 64, (c + 1) * 64)
                    mm(ba[cs, h * 64:(h + 1) * 64], TT[r:r + 64, 4 + hp, cs], TT[r:r + 64, 2 + hp, cs], True, True,
                       [TTb], [bab])
            yield
            for c in range(2):
                cs = slice(c * 64, (c + 1) * 64)
                for h in range(4):
                    hp, r = h // 2, (h % 2) * 64
                    col = 256 + (c * 2 + hp) * 64
                    mm(ba[r:r + 64, col:col + 64], kp[cs, h * 64:(h + 1) * 64], va[cs, h * 64:(h + 1) * 64], True, True,
                       [kpb, vab], [bab])
            yield
            K.op(dve, lambda: nc.vector.tensor_tensor(out=attn_sb[:], in0=ba[:, 0:256], in1=hmask[:], op=ALU.mult),
                 reads=[bab, *CbL], writes=[attnb])
            yield
            for c in range(2):
                for hp in range(2):
                    col = 256 + (c * 2 + hp) * 64
                    K.op(dve, lambda hp=hp, col=col, c=c: nc.vector.scalar_tensor_tensor(
                        out=S[:, hp, :], in0=S[:, hp, :], scalar=dec[:, hp * 2 + c:hp * 2 + c + 1], in1=ba[:, col:col + 64],
                        op0=ALU.mult, op1=ALU.add), reads=[bab, decb], writes=[Sb])
                tgt = 1 if c == 0 else 0
                if c == 0:
                    K.op(dve, lambda: nc.vector.tensor_copy(out=S_bf[1][:], in_=S[:]), reads=[Sb], writes=[S_bfb[1]])
            yield
            for c in range(2):
                cs = slice(c * 64, (c + 1) * 64)
                for h in range(4):
                    mm(bo[cs, h * 64:(h + 1) * 64], attn_sb[cs, h * 64:(h + 1) * 64], va[cs, h * 64:(h + 1) * 64],
                       True, True, [attnb, vab], [bob])
            yield
            bi, bib = G()
            for h in (0, 2, 1, 3):
                hp, r = h // 2, (h % 2) * 64
                for c in range(2):
                    cs = slice(c * 64, (c + 1) * 64)
                    mm(bi[cs, h * 64:(h + 1) * 64], TT[r:r + 64, hp, cs], S_bf[c][r:r + 64, hp, :],
                       True, True, [TTb, S_bfb[c]], [bib])
            yield
            K.op(dve, lambda: nc.vector.tensor_copy(out=S_bf[0][:], in_=S[:]), reads=[Sb], writes=[S_bfb[0]])
            yield
            hgb = smb["hg"]
            K.op(act, lambda: nc.scalar.activation(out=Fx[:], in_=bo[:, 0:256], func=AF.Copy), reads=[bob], writes=[Fxb])
            yield
            K.op(dve, lambda: nc.vector.tensor_tensor(out=Fx[:], in0=Fx[:], in1=bi[:, 0:256], op=ALU.add),
                 reads=[bib], writes=[Fxb])
            yield
            K.op(dve, lambda: nc.vector.tensor_tensor(out=Fy[:], in0=Fx[:], in1=Fx[:], op=ALU.mult), reads=[Fxb], writes=[Fyb])
            yield
            K.op(dve, lambda: nc.vector.reduce_sum(out=small[:, 4:8], in_=Fy[:].rearrange("p (h d) -> p h d", d=64), axis=AX.X),
                 reads=[Fyb], writes=[hgb])
            yield
            rms_rstd(small[:, 4:8], 64.0, small[:, 8:12], hgb)
            yield
            for h in range(4):
                K.op(dve, lambda h=h: nc.vector.scalar_tensor_tensor(
                    out=XYf[1 - par][:, h * 64:(h + 1) * 64], in0=Fx[:, h * 64:(h + 1) * 64], scalar=small[:, 8 + h:9 + h],
                    in1=gate_a[:, h * 64:(h + 1) * 64], op0=ALU.mult, op1=ALU.mult), reads=[Fxb, hgb, gab], writes=[XYb[1 - par]])
            yield
            K.op(dve, lambda: nc.vector.tensor_tensor(out=XYf[1 - par][:, 256:512], in0=bo[:, 256:512], in1=gate_b[:], op=ALU.mult),
                 reads=[bob, gbb], writes=[XYb[1 - par]])


            yield

        groups = []
        for h in range(8):
            for j0 in range(0, i + 1, 4):
                groups.append([(h, j) for j in range(j0, min(j0 + 4, i + 1))])
        rdb = smb["rden"]

        def emit_S(gi):
            stt, stb = ST[gi % NST]
            ptb = PTb[gi % NPT]
            pt = PT[gi % NPT]
            for sl, (h, j) in enumerate(groups[gi]):
                hp, r = h // 2, (h % 2) * 64
                mm(stt[:, sl * P:(sl + 1) * P], KT[:, hp, j * P:(j + 1) * P], (QTa if r == 0 else QTz)[:, hp, :], True, True,
                   [KTb[j], QTb], [stb])
            ng = len(groups[gi])
            K.op(act, lambda: nc.scalar.activation(out=pt[:, 0:ng * P], in_=stt[:, 0:ng * P], func=AF.Exp, scale=0.125),
                 reads=[stb], writes=[ptb])
            gh, gj0 = groups[gi][0]
            vt, vtb = Vt[gi % NVT], Vtb[gi % NVT]
            K.op(dve, lambda: nc.vector.tensor_tensor(
                out=vt[:, 0:ng * 65].rearrange("p (a b) -> p a b", b=65), in0=V[:, gj0:gj0 + ng, gh * 65:(gh + 1) * 65],
                in1=fij[par][:, gj0:gj0 + ng, gh:gh + 1].to_broadcast([P, ng, 65]), op=ALU.mult),
                reads=[biasb[par]] + [Vb[j] for j in range(gj0, gj0 + ng)], writes=[vtb])
            for sl, (h, j) in enumerate(groups[gi]):
                if j == i:
                    K.op(dve, lambda sl=sl: nc.vector.tensor_tensor(
                        out=pt[:, sl * P:(sl + 1) * P], in0=pt[:, sl * P:(sl + 1) * P], in1=fmask[:], op=ALU.mult),
                        reads=[*CbL], writes=[ptb])

        def emit_PV(gi):
            pt, ptb = PT[gi % NPT], PTb[gi % NPT]
            for sl, (h, j) in enumerate(groups[gi]):
                oa, oab = OA[h // 4]
                c0 = (h % 4) * 65
                mm(oa[:, c0:c0 + 65], pt[:, sl * P:(sl + 1) * P], Vt[gi % NVT][:, sl * 65:(sl + 1) * 65], j == 0, j == i,
                   [ptb, Vtb[gi % NVT]], [oab], inc=True)
                if j == i and h % 4 == 3:
                    hb = h - 3
                    K.op(dve, lambda oa=oa: nc.vector.reciprocal(
                        out=small[:, 44:48], in_=oa[:, 0:260].rearrange("p (h e) -> p h e", e=65)[:, :, 64]),
                        reads=[oab], writes=[rdb])
                    for hh in range(4):
                        K.op(dve, lambda hh=hh, oa=oa, hb=hb: nc.vector.scalar_tensor_tensor(
                            out=XYf[1 - par][:, 512 + (hb + hh) * 64:512 + (hb + hh + 1) * 64], in0=oa[:, hh * 65:hh * 65 + 64],
                            scalar=small[:, 44 + hh:45 + hh], in1=gate_c[:, (hb + hh) * 64:(hb + hh + 1) * 64],
                            op0=ALU.mult, op1=ALU.mult), reads=[oab, rdb, gcb], writes=[XYb[1 - par]])

        gen = ab_thread()
        steps_per_group = max(1, -(-12 // len(groups)))
        if next_front is not None:
            if next_front[2] != p:
                load_cast(preg[:], ld[next_front[2]]["preg_bc"][:, :], Lpre)
            front(*next_front)
        gstate[1] = [0, 1, 2]
        emit_S(0)
        if len(groups) > 1:
            emit_S(1)
        for gi in range(len(groups)):
            if gi + 2 < len(groups):
                emit_S(gi + 2)
            emit_PV(gi)
            for _ in range(steps_per_group):
                next(gen, None)
        for _ in gen:
            pass
        gstate[1] = [0, 1, 2, 3, 4, 5]

        tm, tmb = G()
        for ec in range(8):
            tr(bf(tm)[:, ec * P:(ec + 1) * P], XYf[1 - par][:, ec * P:(ec + 1) * P], [XYb[1 - par]], [tmb])
        K.op(act, lambda: nc.scalar.activation(out=XYf[par], in_=bf(tm)[:, 0:D], func=AF.Copy),
             reads=[tmb], writes=[XYb[par]])
        if next_front is not None:
            front_b(1 - par)
        ys = []
        for half in range(2):
            y_, yb_ = G()
            for ec in range(8):
                mm(y_[:, 0:512], XYc[par][:, ec, :], Wout[:, ec, half * 512:(half + 1) * 512], ec == 0, ec == 7,
                   [XYb[par], Wob], [yb_])
            ys.append((y_, yb_))
        if prefetch is not None:
            load_wout(prefetch)
        n2 = smb["n2"]
        for half in range(2):
            y_, yb_ = ys[half]
            K.op(act, lambda y_=y_, half=half: nc.scalar.activation(
                out=hh[:, half * 512:(half + 1) * 512], in_=y_[:, 0:512], func=AF.Square,
                accum_out=small[:, 48 + half:49 + half]), reads=[yb_], writes=[hhb, n2])
        K.op(dve, lambda: nc.vector.tensor_tensor(out=small[:, 50:51], in0=small[:, 48:49], in1=small[:, 49:50], op=ALU.add),
             writes=[n2])
        rms_rstd(small[:, 50:51], float(D), small[:, 51:52], n2)
        for q4 in range(4):
            y_, yb_ = ys[q4 // 2]
            c0 = (q4 % 2) * 256
            K.op(dve, lambda y_=y_, q4=q4, c0=c0: nc.vector.scalar_tensor_tensor(
                out=Fs[q4][:], in0=y_[:, c0:c0 + 256], scalar=small[:, 51:52], in1=postg[:, q4 * 256:(q4 + 1) * 256],
                op0=ALU.mult, op1=ALU.mult), reads=[yb_, n2, Lpost], writes=[Fb[q4]])
            K.op(pool, lambda q4=q4: nc.gpsimd.tensor_tensor(
                out=x_seq[:, i, q4 * 256:(q4 + 1) * 256], in0=x_seq[:, i, q4 * 256:(q4 + 1) * 256],
                in1=Fs[q4][:], op=ALU.add), reads=[Fb[q4]], writes=[Xb[i]])
        finish()

    if 1 in lb_modes:
        compute_lb1(list(lb_modes).index(1))

    x_preloaded = set()
    for s in range(NSEQ):
        for i in range(NB):
            if (s, i) not in x_preloaded:
                load_f32(x_seq[:, i, :], x_d[s, i * P:(i + 1) * P, :], Xb[i])
        for p in range(NL):
            first = (s == 0 and p == 0)
            load_layer(p, with_weights=first, with_preg=first)
            nxt = None
            if p + 1 < NL:
                nxt = p + 1
            elif s + 1 < NSEQ:
                nxt = 0
            K.op(pool, lambda: nc.gpsimd.memset(S[:], 0.0), writes=[Sb])
            K.op(pool, lambda: nc.gpsimd.memset(S_bf[0][:], 0.0), writes=[S_bfb[0]])
            K.op(pool, lambda: nc.gpsimd.memset(small[:, 28:36], 0.0), writes=[smb["acc"]])
            for i in range(NB):
                if i == NB - 1 and p == NL - 1 and s + 1 < NSEQ:
                    for i2 in range(NB - 1):
                        load_f32(x_seq[:, i2, :], x_d[s + 1, i2 * P:(i2 + 1) * P, :], Xb[i2])
                        x_preloaded.add((s + 1, i2))
                bg = (s == 0 and p + 1 < NL and NB >= 2)
                if bg and i == NB - 1:
                    wsc_ready[p + 1] = True
                if i + 1 < NB:
                    nf = (s, i + 1, p)
                elif nxt is not None:
                    nf = (s, 0, p + 1) if p + 1 < NL else (s + 1, 0, 0)
                else:
                    nf = None
                block(s, i, p, p == NL - 1, do_front=(first and i == 0), next_front=nf,
                      prefetch=(nxt if i == NB - 1 else None))
                if bg and i < NB - 1:
                    q = p + 1
                    per = -(-16 // (NB - 1))
                    for pc in range(i * per, min(16, (i + 1) * per)):
                        if pc < 8:
                            K.dma(K.q_pool, lambda pc=pc, q=q: nc.gpsimd.dma_start(
                                out=wsc_in[q][:, pc, :], in_=ld[q]["w_in"][pc * P:(pc + 1) * P, :]), writes=[wsc_ib[q][pc]])
                        else:
                            K.dma(K.q_pool, lambda pc=pc, q=q: nc.gpsimd.dma_start(
                                out=wsc_out[q][:, pc - 8, :], in_=ld[q]["w_out"][(pc - 8) * P:(pc - 7) * P, :]),
                                writes=[wsc_ob[q][pc - 8]])
    sp.wait_all([(l[0], l[1]) for l in K.q_sp.lanes if l[1] > 0])
    return nc


def _layer_inputs(l, lower_bounds, pre_norm_g, w_in, hgrn_norm_g, fox_f_bias, pool_w, pool_scale, w_out, post_norm_g):
    f = np.float32
    bc = lambda v: np.ascontiguousarray(np.broadcast_to(np.asarray(v, f)[None, :], (P, v.shape[0])))
    pwl = np.zeros((P, 128), f)
    for g in range(4):
        pwl[(g % 2) * 64:(g % 2) * 64 + 64, (g // 2) * 64:(g // 2 + 1) * 64] = pool_w[l, g]
    return {
        "w_in": np.ascontiguousarray(w_in[l], f), "w_out": np.ascontiguousarray(w_out[l], f),
        "preg_bc": bc(pre_norm_g[l]), "postg_bc": bc(post_norm_g[l]),
        "hgrng_bc": bc(hgrn_norm_g[l]), "pscale_bc": bc(pool_scale[l]),
        "lbraw_bc": bc(np.concatenate([lower_bounds[0], lower_bounds[1]])),
        "fbias_bc": bc(fox_f_bias[l]), "pw": pwl,
    }


FUSED = True
_cache = {}


def run(x, params, T, NSEQ, ncores, layer_groups):
    consts = _consts()
    cur = np.ascontiguousarray(x, np.float32)
    for grp in layer_groups:
        key = (T, NSEQ, tuple(grp))
        if key not in _cache:
            _cache[key] = build_program(T, NSEQ, [0 if l == 0 else 1 for l in grp])
        nc = _cache[key]
        base = {"c_" + k: v for k, v in consts.items()}
        for p, l in enumerate(grp):
            for k, v in _layer_inputs(l, **params).items():
                base["l%d_%s" % (p, k)] = v
        in_maps = []
        for c in range(ncores):
            m = dict(base)
            m["x"] = np.ascontiguousarray(cur[c * NSEQ:(c + 1) * NSEQ])
            in_maps.append(m)
        res = run_bass_kernel_spmd(nc, in_maps, core_ids=list(range(ncores)))
        cur = np.concatenate([np.asarray(r["out"], np.float32) for r in res.results], axis=0)
    return cur


def kernel(x, lower_bounds, pre_norm_g, w_in, hgrn_norm_g, fox_f_bias, pool_w, pool_scale, w_out, post_norm_g):
    params = dict(lower_bounds=np.asarray(lower_bounds), pre_norm_g=np.asarray(pre_norm_g), w_in=np.asarray(w_in),
                  hgrn_norm_g=np.asarray(hgrn_norm_g), fox_f_bias=np.asarray(fox_f_bias), pool_w=np.asarray(pool_w),
                  pool_scale=np.asarray(pool_scale), w_out=np.asarray(w_out), post_norm_g=np.asarray(post_norm_g))
    x = np.asarray(x)
    B, T, _ = x.shape
    groups = [[0, 1]] if FUSED else [[0], [1]]
    return run(x, params, T, B // 8, 8, groups).astype(np.float32)
```

```python
import numpy as np
import concourse.bass as bass
import concourse.mybir as mybir
from concourse.bass_utils import run_bass_kernel_spmd

F32 = mybir.dt.float32
BF16 = mybir.dt.bfloat16
AF = mybir.ActivationFunctionType
ALU = mybir.AluOpType
AX = mybir.AxisListType

D = 1024
INW = 3592
EPS = 1e-6
TINY = 1e-30
P = 128
POOL_WINDOWS = (2, 4, 8, 16)
EPOCH = 24000


CLOCKS = {}


class Buf:
    __slots__ = ("w", "r", "excl")

    def __init__(self, excl=False):
        self.w = None
        self.r = {}
        self.excl = excl


class Eng:
    def __init__(self, nc, eng, name):
        self.nc, self.eng, self.name = nc, eng, name
        self.sem = nc.alloc_semaphore(name + "_s0")
        self.own = {id(self.sem)}
        self.n = 0
        self.ep = 0
        self.seen = {}
        self.pending = 0

    def wait_all(self, deps):
        deps = [d for d in deps if self.seen.get(id(d[0]), 0) < d[1]]
        if len(deps) > 1:
            keep = []
            for d in deps:
                k = id(d[0])
                implied = False
                for d2 in deps:
                    if d2 is not d:
                        c2 = CLOCKS.get((id(d2[0]), d2[1]))
                        if c2 is not None and c2.get(k, 0) >= d[1]:
                            implied = True
                            break
                if not implied:
                    keep.append(d)
            deps = keep
        for sem, val in deps:
            key = id(sem)
            if self.seen.get(key, 0) < val:
                self.eng.wait_ge(sem, val)
                self.seen[key] = val
            c = CLOCKS.get((key, val))
            if c is not None:
                seen = self.seen
                for k2, v2 in c.items():
                    if seen.get(k2, 0) < v2:
                        seen[k2] = v2

    def snapshot(self, tok):
        c = dict(self.seen)
        c[id(tok[0])] = max(c.get(id(tok[0]), 0), tok[1])
        CLOCKS[(id(tok[0]), tok[1])] = c

    def issue(self, ins):
        if self.n >= EPOCH and self.pending == 0:
            self.ep += 1
            self.sem = self.nc.alloc_semaphore("%s_s%d" % (self.name, self.ep))
            self.own.add(id(self.sem))
            self.n = 0
        self.n += 1
        ins.then_inc(self.sem, 1)
        self.pending = 0
        return (self.sem, self.n)

    def issue_noinc(self, ins):
        if self.n >= EPOCH:
            pass
        self.pending += 1
        return (self.sem, self.n + 1)


class DmaQ:
    def __init__(self, nc, E, name, nlanes):
        self.E = E
        self.lanes = [[nc.alloc_semaphore("%s_l%d" % (name, i)), 0] for i in range(nlanes)]
        self.k = 0

    def issue(self, fn, deps):
        lane = self.lanes[self.k]
        self.k = (self.k + 1) % len(self.lanes)
        d = set(deps)
        if lane[1] > 0:
            d.add((lane[0], lane[1]))
        self.E.wait_all(d)
        lane[1] += 16
        fn().then_inc(lane[0], 16)
        tok = (lane[0], lane[1])
        self.E.snapshot(tok)
        return tok


class Ctx:
    def __init__(self, nc):
        self.nc = nc
        self.pe = Eng(nc, nc.tensor, "pe")
        self.act = Eng(nc, nc.scalar, "act")
        self.dve = Eng(nc, nc.vector, "dve")
        self.pool = Eng(nc, nc.gpsimd, "pool")
        self.sp = Eng(nc, nc.sync, "sp")
        self.q_sp = DmaQ(nc, self.sp, "qsp", 8)
        self.q_pool = DmaQ(nc, self.pool, "qpl", 16)

    def _deps(self, reads, writes, E=None):
        deps = set()
        own = E.own if E is not None else ()
        is_pe = E is self.pe
        for b in reads:
            if b.w is not None and not (is_pe and id(b.w[0]) in own):
                deps.add(b.w)
            if b.excl:
                for tok in b.r.values():
                    if id(tok[0]) not in own:
                        deps.add(tok)
        for b in writes:
            if b.w is not None and not (is_pe and id(b.w[0]) in own):
                deps.add(b.w)
            for tok in b.r.values():
                if not (is_pe and id(tok[0]) in own):
                    deps.add(tok)
        return deps

    @staticmethod
    def _commit(tok, reads, writes):
        for b in reads:
            b.r[id(tok[0])] = tok
        for b in writes:
            b.w = tok
            b.r = {}

    def op(self, E, fn, reads=(), writes=(), inc=True, mode=None):
        if mode is not None and mode != getattr(self, "pe_mode", None):
            if E.n > 0:
                assert E.pending == 0
                E.eng.wait_ge(E.sem, E.n)
                E.seen[id(E.sem)] = E.n
            self.pe_mode = mode
        E.wait_all(self._deps(reads, writes, E))
        tok = E.issue(fn()) if inc else E.issue_noinc(fn())
        if inc:
            E.snapshot(tok)
        elif (id(tok[0]), tok[1]) not in CLOCKS:
            E.snapshot(tok)
        self._commit(tok, reads, writes)
        return tok

    def dma(self, Q, fn, reads=(), writes=()):
        tok = Q.issue(fn, self._deps(reads, writes))
        self._commit(tok, reads, writes)
        return tok


def _consts():
    s = np.arange(P)[:, None]
    t = np.arange(P)[None, :]
    same = (s // 64) == (t // 64)
    c = {}
    c["ident"] = np.eye(P, dtype=np.float32)
    c["ucum"] = (same & (s <= t)).astype(np.float32)
    c["lstr"] = (same & (s > t)).astype(np.float32)
    c["ufull"] = (s <= t).astype(np.float32)
    c["ones"] = np.ones((P, P), np.float32)
    ind = np.zeros((P, 2), np.float32)
    ind[:64, 0] = 1
    ind[64:, 1] = 1
    c["ind"] = ind
    c["fmask"] = (s <= t).astype(np.float32)
    hm = np.zeros((P, 256), np.float32)
    for h in range(4):
        hm[:, h * 64:(h + 1) * 64] = ((np.arange(P)[:, None] % 64) <= np.arange(64)[None, :])
    c["hmask"] = hm
    pm_cur = np.zeros((P, 4, P), np.float32)
    pm_prev = np.zeros((P, 4, 16), np.float32)
    pm_cur0 = np.zeros((P, 4, 16), np.float32)
    for g, w in enumerate(POOL_WINDOWS):
        for tt in range(P):
            for ss in range(tt - w + 1, tt + 1):
                if ss >= 0:
                    pm_cur[ss, g, tt] += 1.0 / w
                else:
                    if tt < 16:
                        pm_prev[ss + P, g, tt] += 1.0 / w
            pm_cur[tt, g, tt] -= 1.0
        for tt in range(16):
            cnt = min(tt + 1, w)
            for ss in range(max(0, tt - w + 1), tt + 1):
                pm_cur0[ss, g, tt] += 1.0 / cnt
            pm_cur0[tt, g, tt] -= 1.0
    c["pm_cur"] = pm_cur.reshape(P, 4 * P)
    c["pm_prev"] = pm_prev.reshape(P, 64)
    c["pm_cur0"] = pm_cur0.reshape(P, 64)
    return c


CONST_SHAPES = {"ident": (P, P), "ucum": (P, P), "lstr": (P, P), "ufull": (P, P), "ones": (P, P),
                "ind": (P, 2), "fmask": (P, P), "hmask": (P, 256), "pm_cur": (P, 512),
                "pm_prev": (P, 64), "pm_cur0": (P, 64)}
LAYER_SHAPES = {"w_in": (D, INW), "w_out": (D, D), "preg_bc": (P, D), "postg_bc": (P, D),
                "hgrng_bc": (P, 256), "pscale_bc": (P, 256), "lbraw_bc": (P, 512),
                "fbias_bc": (P, 8), "pw": (P, 128)}


def build_program(T, NSEQ, lb_modes, debug_out=False):
    NL = len(lb_modes)
    NB = T // P
    nc = bass.Bass("TRN2", target_bir_lowering=False)
    CLOCKS.clear()
    K = Ctx(nc)
    pe, act, dve, pool, sp = K.pe, K.act, K.dve, K.pool, K.sp

    x_d = nc.dram_tensor("x", [NSEQ, T, D], F32, kind="ExternalInput").ap()
    out_d = nc.dram_tensor("out", [NSEQ, T, D], F32, kind="ExternalOutput").ap()
    cd = {k: nc.dram_tensor("c_" + k, list(v), F32, kind="ExternalInput").ap() for k, v in CONST_SHAPES.items()}
    ld = [{k: nc.dram_tensor("l%d_%s" % (p, k), list(v), F32, kind="ExternalInput").ap()
           for k, v in LAYER_SHAPES.items()} for p in range(NL)]

    wsc_in = [nc.dram_tensor("wsc_in%d" % p, [P, 8, INW], BF16, kind="Internal").ap() for p in range(NL)]
    wsc_out = [nc.dram_tensor("wsc_out%d" % p, [P, 8, D], BF16, kind="Internal").ap() for p in range(NL)]
    wsc_ib = [[Buf() for _ in range(8)] for _ in range(NL)]
    wsc_ob = [[Buf() for _ in range(8)] for _ in range(NL)]
    wsc_ready = [False] * NL

    def sb(name, shape, dt):
        return nc.alloc_sbuf_tensor(name, list(shape), dt)

    x_seq = sb("x_seq", [P, NB, D], F32)
    Xb = [Buf() for _ in range(NB)]
    Win = sb("Win", [P, 8, INW], BF16)
    Wout = sb("Wout", [P, 8, D], BF16)
    Wib = Buf()
    Wob = Buf()
    KT = sb("KT", [P, 4, T], BF16)
    KTb = [Buf() for _ in range(NB)]
    V = sb("V", [P, NB, 8 * 65], BF16)
    Vb = [Buf() for _ in range(NB)]
    Vones = Buf()
    ident = sb("ident", [P, P], BF16)
    ucum = sb("ucum", [P, P], F32)
    lstr = sb("lstr", [P, P], F32)
    ufull = sb("ufull", [P, P], F32)
    ones = sb("ones", [P, P], F32)
    ind = sb("ind", [P, 2], F32)
    fmask = sb("fmask", [P, P], BF16)
    hmask = sb("hmask", [P, 256], BF16)
    pm_cur = sb("pm_cur", [P, 512], BF16)
    pm_prev = sb("pm_prev", [P, 64], BF16)
    pm_cur0 = sb("pm_cur0", [P, 64], BF16)
    CbL = [Buf() for _ in range(len(CONST_SHAPES))]
    preg = sb("preg", [P, D], BF16)
    postg = sb("postg", [P, D], BF16)
    hgrng = sb("hgrng", [P, 256], BF16)
    pscale = sb("pscale", [P, 256], BF16)
    lb_bc = sb("lb_bc", [P, 256], F32)
    oml_bc = sb("oml_bc", [P, 256], F32)
    lbraw = None
    fbias = sb("fbias", [P, 8], F32)
    pw = sb("pw", [P, 128], BF16)
    Lpre, Lpost, Lpw, Lhg, Lps, Lfb, Llb = (Buf() for _ in range(7))
    hm = sb("hm", [P, D], BF16)
    hmb = Buf()
    hh = sb("hh", [P, D], BF16)
    hhb = Buf()
    hT = sb("hT", [P, 8, P], BF16)
    hTb = Buf()
    XYf = [hT[:].rearrange("p a b -> p (a b)"), hm[:]]
    XYc = [hT[:], hm[:].rearrange("p (a b) -> p a b", b=P)]
    XYb = [hTb, hmb]
    NF = 5
    Fs = [sb("F%d" % i, [P, 256], F32) for i in range(NF)]
    Fb = [Buf() for _ in range(NF)]
    ebig = sb("ebig", [P, 256], F32)
    ebb = Buf()
    gate_a = sb("gate_a", [P, 256], BF16)
    gate_b = sb("gate_b", [P, 256], BF16)
    gate_c = sb("gate_c", [P, 512], BF16)
    gab, gbb, gcb = Buf(), Buf(), Buf()
    qp = sb("qp", [P, 256], BF16)
    qpp = sb("qpp", [P, 256], BF16)
    kp = sb("kp", [P, 256], BF16)
    va = sb("va", [P, 256], BF16)
    attn_sb = qp
    qpb, qppb, kpb, vab = Buf(), Buf(), Buf(), Buf()
    attnb = qpb
    TT = sb("TT", [P, 6, P], BF16)
    TTb = Buf()
    S = sb("S", [P, 2, 64], F32)
    Sb = Buf()
    S_bf = [sb("S_bf%d" % i, [P, 2, 64], BF16) for i in range(2)]
    S_bfb = [Buf(), Buf()]
    dec = sb("dec", [P, 4], F32)
    decb = Buf()
    u_sb = [sb("u_sb%d" % i, [P, 256], BF16) for i in range(2)]
    ub = [Buf(), Buf()]
    plT = qpp[:].rearrange("p (a b) -> p a b", b=P)
    plTb = qppb
    q_sb = Fs[3][:].bitcast(BF16)
    k_sb = Fs[4][:].bitcast(BF16)
    qsb_b, ksb_b = Fb[3], Fb[4]
    QTa = sb("QTa", [P, 4, P], BF16)
    QTz = sb("QTz", [P, 4, P], BF16)
    QTb = Buf()
    NPT = 3
    NVT = 3
    Vt = [sb("Vt%d" % i, [P, 4 * 65], BF16) for i in range(NVT)]
    Vtb = [Buf() for _ in range(NVT)]
    PT = [sb("PT%d" % i, [P, 512], BF16) for i in range(NPT)]
    PTb = [Buf() for _ in range(NPT)]
    fij = [sb("fij%d" % i, [P, NB, 8], F32) for i in range(2)]
    biasb = [Buf(), Buf()]
    totall = sb("totall", [P, NB, 8], F32)
    small = sb("small", [P, 64], F32)
    smb = {k: Buf() for k in ("n1", "hg", "fc", "acc", "tot", "rden", "n2", "w")}

    banks = [nc.alloc_psum_tensor("bank%d" % i, [P, 512], F32) for i in range(8)]
    bankb = [Buf(excl=True) for _ in range(8)]
    gstate = [0, [0, 1, 2, 3, 4, 5]]

    def G():
        lst = gstate[1]
        i = lst[gstate[0] % len(lst)]
        gstate[0] += 1
        return banks[i], bankb[i]

    NST = 3
    ST = [(banks[3], bankb[3]), (banks[4], bankb[4]), (banks[5], bankb[5])]
    OA = [(banks[6], bankb[6]), (banks[7], bankb[7])]

    def bf(t):
        return t[:].bitcast(BF16)

    def load_cast(dst_ap, src_ap, buf):
        K.dma(K.q_pool, lambda: nc.gpsimd.dma_start(out=dst_ap, in_=src_ap), writes=[buf])

    def load_f32(dst_ap, src_ap, buf):
        K.dma(K.q_sp, lambda: nc.sync.dma_start(out=dst_ap, in_=src_ap), writes=[buf])

    _cseen = []
    for name, t_, cast in (("ident", ident, 1), ("ucum", ucum, 0), ("lstr", lstr, 0), ("ufull", ufull, 0),
                           ("ones", ones, 0), ("ind", ind, 0), ("fmask", fmask, 1), ("hmask", hmask, 1),
                           ("pm_cur", pm_cur, 1), ("pm_prev", pm_prev, 1), ("pm_cur0", pm_cur0, 1)):
        (load_cast if cast else load_f32)(t_[:], cd[name][:, :], CbL[len(_cseen)])
        _cseen.append(name)
    Vv = V[:].rearrange("p b (h e) -> p b h e", e=65)
    K.op(pool, lambda: nc.gpsimd.memset(QTa[:], 0.0), writes=[QTb])
    K.op(pool, lambda: nc.gpsimd.memset(QTz[:], 0.0), writes=[QTb])

    def load_win(p):
        if wsc_ready[p]:
            for dc in range(8):
                K.dma(K.q_sp, lambda dc=dc: nc.sync.dma_start(out=Win[:, dc, :], in_=wsc_in[p][:, dc, :]),
                      reads=[wsc_ib[p][dc]], writes=[Wib])
        else:
            for dc in range(8):
                load_cast(Win[:, dc, :], ld[p]["w_in"][dc * P:(dc + 1) * P, :], Wib)
            for dc in range(8):
                K.dma(K.q_sp, lambda dc=dc: nc.sync.dma_start(out=wsc_in[p][:, dc, :], in_=Win[:, dc, :]),
                      reads=[Wib], writes=[wsc_ib[p][dc]])

    def load_wout(p):
        if wsc_ready[p]:
            for ec in range(8):
                K.dma(K.q_sp, lambda ec=ec: nc.sync.dma_start(out=Wout[:, ec, :], in_=wsc_out[p][:, ec, :]),
                      reads=[wsc_ob[p][ec]], writes=[Wob])
        else:
            for ec in range(8):
                load_cast(Wout[:, ec, :], ld[p]["w_out"][ec * P:(ec + 1) * P, :], Wob)
            for ec in range(8):
                K.dma(K.q_sp, lambda ec=ec: nc.sync.dma_start(out=wsc_out[p][:, ec, :], in_=Wout[:, ec, :]),
                      reads=[Wob], writes=[wsc_ob[p][ec]])
            wsc_ready[p] = True

    def load_layer(p, with_weights=True, with_preg=True):
        L = ld[p]
        if with_weights:
            load_win(p)
            load_wout(p)
        if with_preg:
            load_cast(preg[:], L["preg_bc"][:, :], Lpre)
        load_cast(postg[:], L["postg_bc"][:, :], Lpost)
        load_cast(pw[:], L["pw"][:, :], Lpw)
        load_cast(hgrng[:], L["hgrng_bc"][:, :], Lhg)
        load_cast(pscale[:], L["pscale_bc"][:, :], Lps)
        load_f32(fbias[:], L["fbias_bc"][:, :], Lfb)

    def compute_lb1(p):
        L = ld[p]
        load_f32(Fs[1][:], L["lbraw_bc"][:, 0:256], Fb[1])
        load_f32(Fs[2][:], L["lbraw_bc"][:, 256:512], Fb[2])
        K.op(pool, lambda: nc.gpsimd.tensor_tensor(out=Fs[0][:], in0=Fs[1][:], in1=Fs[2][:],
                                                   op=ALU.subtract), reads=[Fb[1], Fb[2]], writes=[Fb[0]])
        K.op(act, lambda: nc.scalar.activation(out=Fs[0][:], in_=Fs[0][:], func=AF.Exp),
             reads=[], writes=[Fb[0]])
        K.op(pool, lambda: nc.gpsimd.tensor_scalar(out=Fs[0][:], in0=Fs[0][:], scalar1=1.0, scalar2=1.0,
                                                   op0=ALU.mult, op1=ALU.add), writes=[Fb[0]])
        K.op(dve, lambda: nc.vector.reciprocal(out=lb_bc[:], in_=Fs[0][:]), reads=[Fb[0]], writes=[Llb])
        K.op(pool, lambda: nc.gpsimd.tensor_scalar(out=oml_bc[:], in0=lb_bc[:], scalar1=-1.0, scalar2=1.0,
                                                   op0=ALU.mult, op1=ALU.add), writes=[Llb])

    def silu_gate(ps_ap, width, out_ap, outb, mul_ap=None, psb=None):
        e = ebig[:, 0:width]
        K.op(act, lambda: nc.scalar.activation(out=e, in_=ps_ap, func=AF.Exp, scale=-1.0),
             reads=[psb], writes=[ebb])
        K.op(act, lambda: nc.scalar.activation(out=e, in_=e, func=AF.Ln, bias=1.0), writes=[ebb])
        K.op(act, lambda: nc.scalar.activation(out=e, in_=e, func=AF.Exp, scale=-1.0), writes=[ebb])
        if mul_ap is None:
            K.op(dve, lambda: nc.vector.tensor_tensor(out=out_ap, in0=ps_ap, in1=e, op=ALU.mult),
                 reads=[psb, ebb], writes=[outb])
        else:
            K.op(dve, lambda: nc.vector.tensor_tensor(out=e, in0=ps_ap, in1=e, op=ALU.mult),
                 reads=[psb], writes=[ebb])
            K.op(pool, lambda: nc.gpsimd.tensor_tensor(out=out_ap, in0=e, in1=mul_ap, op=ALU.mult),
                 reads=[ebb, Lhg, Lps], writes=[outb])

    def rms_rstd(ss_ap, n, out_ap, b):
        K.op(dve, lambda: nc.vector.tensor_scalar(out=out_ap, in0=ss_ap, scalar1=1.0 / n, scalar2=EPS,
                                                  op0=ALU.mult, op1=ALU.add), writes=[b])
        K.op(act, lambda: nc.scalar.activation(out=out_ap, in_=out_ap, func=AF.Ln), writes=[b])
        K.op(act, lambda: nc.scalar.activation(out=out_ap, in_=out_ap, func=AF.Exp, scale=-0.5), writes=[b])

    def _cls(n):
        return 32 if n <= 32 else (64 if n <= 64 else 128)

    def mm(out, lhsT, rhs, start, stop, reads, writes, inc=None):
        kc = _cls(lhsT.shape[0])
        mode = ("mm", str(lhsT.dtype), kc, _cls(lhsT.shape[-1]), lhsT.base_partition() if kc < 128 else 0)
        return K.op(pe, lambda: nc.tensor.matmul(out, lhsT, rhs, start=start, stop=stop), reads=reads, writes=writes,
                    inc=(bool(stop) or kc < 128) if inc is None else inc, mode=mode)

    def tr(out, in_, reads, writes):
        return K.op(pe, lambda: nc.tensor.transpose(out, in_, ident[:]), reads=list(reads) + [*CbL], writes=writes,
                    mode=("tr",))

    def front(s, i, p):
        xb = x_seq[:, i, :]
        n1 = smb["n1"]
        K.op(act, lambda: nc.scalar.activation(out=hh[:], in_=xb, func=AF.Square, accum_out=small[:, 0:1]),
             reads=[Xb[i]], writes=[hhb, n1])
        rms_rstd(small[:, 0:1], float(D), small[:, 2:3], n1)
        K.op(dve, lambda: nc.vector.scalar_tensor_tensor(out=hh[:], in0=xb, scalar=small[:, 2:3], in1=preg[:],
                                                         op0=ALU.mult, op1=ALU.mult),
             reads=[Xb[i], n1, Lpre], writes=[hhb])

    def front_b(tp):
        tb, tbb = G()
        for dc in range(8):
            tr(bf(tb)[:, dc * P:(dc + 1) * P], hh[:, dc * P:(dc + 1) * P], [hhb], [tbb])
        K.op(act, lambda: nc.scalar.activation(out=XYf[tp], in_=bf(tb)[:, 0:D], func=AF.Copy),
             reads=[tbb], writes=[XYb[tp]])


    def block(s, i, p, last_layer, do_front=True, next_front=None, prefetch=None):
        xb = x_seq[:, i, :]
        par = i % 2

        def finish():
            if last_layer:
                K.dma(K.q_sp, lambda: nc.sync.dma_start(out=out_d[s, i * P:(i + 1) * P, :], in_=x_seq[:, i, :]),
                      reads=[Xb[i]])
        if do_front:
            front(s, i, p)
            front_b(i % 2)
        def proj(col0, width):
            b_, bb_ = G()
            for dc in range(8):
                mm(b_[:, 0:width], XYc[par][:, dc, :], Win[:, dc, col0:col0 + width], dc == 0, dc == 7,
                   [XYb[par], Wib], [bb_])
            return b_, bb_

        b7, b7b = proj(3584, 8)
        fcb = smb["fc"]
        K.op(dve, lambda: nc.vector.tensor_tensor(out=small[:, 12:20], in0=b7[:, 0:8], in1=fbias[:], op=ALU.add),
             reads=[b7b, Lfb], writes=[fcb])
        K.op(act, lambda: nc.scalar.activation(out=small[:, 12:20], in_=small[:, 12:20], func=AF.Exp, scale=-1.0),
             writes=[fcb])
        K.op(act, lambda: nc.scalar.activation(out=small[:, 20:28], in_=small[:, 12:20], func=AF.Ln, bias=1.0), writes=[fcb])
        sp_ap = small[:, 20:28]
        acc_ap = small[:, 28:36]
        accb = smb["acc"]
        b3, b3b = proj(1536, 512)
        K.op(act, lambda: nc.scalar.activation(out=q_sb, in_=b3[:, 0:512], func=AF.Copy), reads=[b3b], writes=[qsb_b])
        b4, b4b = proj(2048, 512)
        K.op(dve, lambda: nc.vector.tensor_copy(out=k_sb, in_=b4[:, 0:512]), reads=[b4b], writes=[ksb_b])
        tq, tqb = G()
        for hp in range(4):
            tr(bf(tq)[:, hp * P:(hp + 1) * P], q_sb[:, hp * P:(hp + 1) * P], [qsb_b], [tqb])
        for hp in range(4):
            tr(bf(tq)[:, (4 + hp) * P:(5 + hp) * P], k_sb[:, hp * P:(hp + 1) * P], [ksb_b], [tqb])
        K.op(act, lambda: nc.scalar.activation(out=QTa[0:64].rearrange("p a b -> p (a b)"), in_=bf(tq)[0:64, 0:512], func=AF.Copy),
             reads=[tqb], writes=[QTb])
        K.op(act, lambda: nc.scalar.activation(out=QTz[64:128].rearrange("p a b -> p (a b)"), in_=bf(tq)[64:128, 0:512], func=AF.Copy),
             reads=[tqb], writes=[QTb])
        K.op(act, lambda: nc.scalar.activation(out=KT[:, :, i * P:(i + 1) * P],
                                               in_=bf(tq)[:, 512:1024].rearrange("p (a b) -> p a b", b=P),
                                               func=AF.Copy), reads=[tqb], writes=[KTb[i]])

        bc_, bcb = G()
        mm(bc_[:, 0:8], ufull[:], sp_ap, True, False, [*CbL, fcb], [bcb])
        mm(bc_[:, 0:8], ones[:], acc_ap, False, True, [*CbL, accb], [bcb])
        K.op(pool, lambda: nc.gpsimd.tensor_tensor(out=acc_ap, in0=acc_ap, in1=sp_ap, op=ALU.add),
             reads=[fcb], writes=[accb])
        mm(bc_[:, 8:16], ones[:], acc_ap, True, True, [*CbL, accb], [bcb])
        totb = smb["tot"]
        wb = smb["w"]
        K.op(dve, lambda: nc.vector.tensor_copy(out=totall[:, i, :], in_=bc_[:, 8:16]), reads=[bcb], writes=[totb])
        K.op(dve, lambda: nc.vector.tensor_tensor(out=small[:, 52:60], in0=bc_[:, 0:8], in1=totall[:, i, :],
                                                  op=ALU.subtract), reads=[bcb, totb], writes=[wb])
        K.op(act, lambda: nc.scalar.activation(out=small[:, 52:60], in_=small[:, 52:60], func=AF.Exp), writes=[wb])
        K.op(pool, lambda: nc.gpsimd.tensor_tensor(
            out=fij[par][:, 0:i + 1, :], in0=totall[:, 0:i + 1, :],
            in1=totall[:, i:i + 1, :].to_broadcast([P, i + 1, 8]), op=ALU.subtract),
            reads=[totb], writes=[biasb[par]])
        K.op(act, lambda: nc.scalar.activation(out=fij[par][:, 0:i + 1, :], in_=fij[par][:, 0:i + 1, :], func=AF.Exp),
             writes=[biasb[par]])
        b5, b5b = proj(2560, 512)
        K.op(dve, lambda: nc.vector.tensor_tensor(
            out=Vv[:, i, :, 0:64], in0=b5[:, 0:512].rearrange("p (h d) -> p h d", d=64),
            in1=small[:, 52:60].unsqueeze(2).to_broadcast([P, 8, 64]), op=ALU.mult),
            reads=[b5b, smb["w"]], writes=[Vb[i]])
        K.op(pool, lambda: nc.gpsimd.tensor_copy(out=Vv[:, i, :, 64], in_=small[:, 52:60]),
             reads=[smb["w"]], writes=[Vb[i]])
        b6, b6b = proj(3072, 512)
        silu_gate(b6[:, 0:256], 256, gate_c[:, 0:256], gcb, psb=b6b)
        silu_gate(b6[:, 256:512], 256, gate_c[:, 256:512], gcb, psb=b6b)
        gstate[1] = [0, 1, 2]
        b0, b0b = proj(0, 512)
        Fq, Fqb = Fs[0], Fb[0]
        Fk, Fkb = Fs[1], Fb[1]
        Fl, Flb = Fs[2], Fb[2]
        Fx, Fxb = Fs[3], Fb[3]
        Fy, Fyb = Fs[4], Fb[4]
        b1, b1b = proj(512, 512)
        b2, b2b = proj(1024, 512)
        if prefetch is not None:
            load_win(prefetch)
        def ab_thread():
            K.op(act, lambda: nc.scalar.activation(out=Fy[:], in_=b0[:, 256:512], func=AF.Exp, scale=-1.0),
                 reads=[b0b], writes=[Fyb])
            yield
            K.op(act, lambda: nc.scalar.activation(out=Fy[:], in_=Fy[:], func=AF.Ln, bias=1.0), writes=[Fyb])
            yield
            K.op(act, lambda: nc.scalar.activation(out=Fy[:], in_=Fy[:], func=AF.Exp, scale=-1.0), writes=[Fyb])
            yield
            if lb_modes[p] == 0:
                K.op(pool, lambda: nc.gpsimd.tensor_scalar(out=Fk[:], in0=Fy[:], scalar1=-1.0, scalar2=1.0,
                                                           op0=ALU.mult, op1=ALU.add), reads=[Fyb], writes=[Fkb])
                yield
                K.op(dve, lambda: nc.vector.tensor_scalar(out=Fy[:], in0=Fy[:], scalar1=TINY, scalar2=None,
                                                          op0=ALU.max), writes=[Fyb])
                yield
            else:
                K.op(pool, lambda: nc.gpsimd.tensor_tensor(out=Fy[:], in0=Fy[:], in1=oml_bc[:], op=ALU.mult),
                     reads=[Llb], writes=[Fyb])
                yield
                K.op(pool, lambda: nc.gpsimd.tensor_tensor(out=Fk[:], in0=oml_bc[:], in1=Fy[:], op=ALU.subtract),
                     reads=[Llb, Fyb], writes=[Fkb])
                yield
                K.op(dve, lambda: nc.vector.scalar_tensor_tensor(out=Fy[:], in0=Fy[:], scalar=TINY, in1=lb_bc[:],
                                                                 op0=ALU.max, op1=ALU.add), reads=[Llb], writes=[Fyb])
                yield
            K.op(act, lambda: nc.scalar.activation(out=Fl[:], in_=Fy[:], func=AF.Ln), reads=[Fyb], writes=[Flb])

            yield
            K.op(act, lambda: nc.scalar.activation(out=Fx[:], in_=b0[:, 0:256], func=AF.Exp, scale=-1.0),
                 reads=[b0b], writes=[Fxb])
            yield
            K.op(act, lambda: nc.scalar.activation(out=Fx[:], in_=Fx[:], func=AF.Ln, bias=1.0), writes=[Fxb])
            yield
            K.op(act, lambda: nc.scalar.activation(out=Fx[:], in_=Fx[:], func=AF.Exp, scale=-1.0), writes=[Fxb])
            yield
            K.op(dve, lambda: nc.vector.tensor_tensor(out=Fq[:], in0=b0[:, 0:256], in1=Fx[:], op=ALU.mult),
                 reads=[b0b, Fxb], writes=[Fqb])
            yield
            K.op(act, lambda: nc.scalar.activation(out=va[:], in_=b1[:, 0:256], func=AF.Copy), reads=[b1b], writes=[vab])
            yield
            silu_gate(b1[:, 256:512], 256, gate_a[:], gab, mul_ap=hgrng[:], psb=b1b)
            yield
            K.op(act, lambda: nc.scalar.activation(out=u_sb[par][:], in_=b2[:, 0:256], func=AF.Copy),
                 reads=[b2b], writes=[ub[par]])
            yield
            silu_gate(b2[:, 256:512], 256, gate_b[:], gbb, mul_ap=pscale[:], psb=b2b)

            yield
            be, beb = G()
            mm(be[:, 0:256], ucum[:], Fl[:], True, True, [*CbL, Flb], [beb])
            yield
            mm(be[:, 256:512], lstr[:], Fl[:], True, True, [*CbL, Flb], [beb])
            yield
            bd, bdb = G()
            for hp in range(2):
                mm(bd[:, hp * 2:hp * 2 + 2], Fl[:, hp * P:(hp + 1) * P], ind[:], True, True, [*CbL, Flb], [bdb])
            yield
            K.op(act, lambda: nc.scalar.activation(out=dec[:], in_=bd[:, 0:4], func=AF.Exp), reads=[bdb], writes=[decb])
            yield
            K.op(act, lambda: nc.scalar.activation(out=Fx[:], in_=be[:, 0:256], func=AF.Exp), reads=[beb], writes=[Fxb])
            yield
            K.op(dve, lambda: nc.vector.tensor_tensor(out=qp[:], in0=Fq[:], in1=Fx[:], op=ALU.mult),
                 reads=[Fqb, Fxb], writes=[qpb])
            yield
            K.op(act, lambda: nc.scalar.activation(out=Fy[:], in_=be[:, 256:512], func=AF.Exp, scale=-1.0),
                 reads=[beb], writes=[Fyb])
            yield
            K.op(dve, lambda: nc.vector.tensor_tensor(out=qpp[:], in0=Fq[:], in1=Fy[:], op=ALU.mult),
                 reads=[Fqb, Fyb], writes=[qppb])
            yield
            K.op(act, lambda: nc.scalar.activation(out=Fx[:], in_=be[:, 256:512], func=AF.Exp), reads=[beb], writes=[Fxb])
            yield
            K.op(dve, lambda: nc.vector.tensor_tensor(out=kp[:], in0=Fk[:], in1=Fx[:], op=ALU.mult),
                 reads=[Fkb, Fxb], writes=[kpb])
            yield
            tt_, ttb = G()
            for n_, (src, srcb) in enumerate(((qp, qpb), (qpp, qppb), (kp, kpb))):
                for hp in range(2):
                    tr(bf(tt_)[:, (2 * n_ + hp) * P:(2 * n_ + hp + 1) * P], src[:, hp * P:(hp + 1) * P], [srcb], [ttb])
            yield
            K.op(act, lambda: nc.scalar.activation(out=TT[:].rearrange("p a b -> p (a b)"), in_=bf(tt_)[:, 0:768], func=AF.Copy),
                 reads=[ttb], writes=[TTb])
            yield
            bp, bpb = G()
            for g in range(4):
                ro = (g % 2) * 64
                o_ap = bp[ro:ro + 64, (g // 2) * P:(g // 2 + 1) * P]
                if i == 0:
                    mm(o_ap, u_sb[par][:, g * 64:(g + 1) * 64], pm_cur[:, g * P:(g + 1) * P], True, True,
                       [ub[par], *CbL], [bpb])
                    mm(o_ap[:, 0:16], u_sb[par][:, g * 64:(g + 1) * 64], pm_cur0[:, g * 16:(g + 1) * 16], True, True,
                       [ub[par], *CbL], [bpb])
                else:
                    mm(o_ap, u_sb[par][:, g * 64:(g + 1) * 64], pm_cur[:, g * P:(g + 1) * P], True, False,
                       [ub[par], *CbL], [bpb])
                    mm(o_ap[:, 0:16], u_sb[1 - par][:, g * 64:(g + 1) * 64], pm_prev[:, g * 16:(g + 1) * 16], False, True,
                       [ub[1 - par], *CbL], [bpb])
            yield
            K.op(act, lambda: nc.scalar.activation(out=qpp[:], in_=bp[:, 0:256], func=AF.Copy),
                 reads=[bpb], writes=[plTb])
            yield
            bo, bob = G()
            for g in (0, 2, 1, 3):
                ro = (g % 2) * 64
                mm(bo[:, 256 + g * 64:256 + (g + 1) * 64], plT[ro:ro + 64, g // 2, :], pw[ro:ro + 64, (g // 2) * 64:(g // 2 + 1) * 64],
                   True, True, [plTb, Lpw], [bob])
            yield
            ba, bab = G()
            for h in (0, 2, 1, 3):
                hp, r = h // 2, (h % 2) * 64
                for c in range(2):
                    cs = slice(c * 64, (c + 1) * 64)
                    mm(ba[cs, h * 64:(h + 1) * 64], TT[r:r + 64, 4 + hp, cs], TT[r:r + 64, 2 + hp, cs], True, True,
                       [TTb], [bab])
            yield
            for c in range(2):
                cs = slice(c * 64, (c + 1) * 64)
                for h in range(4):
                    hp, r = h // 2, (h % 2) * 64
                    col = 256 + (c * 2 + hp) * 64
                    mm(ba[r:r + 64, col:col + 64], kp[cs, h * 64:(h + 1) * 64], va[cs, h * 64:(h + 1) * 64], True, True,
                       [kpb, vab], [bab])
            yield
            K.op(dve, lambda: nc.vector.tensor_tensor(out=attn_sb[:], in0=ba[:, 0:256], in1=hmask[:], op=ALU.mult),
                 reads=[bab, *CbL], writes=[attnb])
            yield
            for c in range(2):
                for hp in range(2):
                    col = 256 + (c * 2 + hp) * 64
                    K.op(dve, lambda hp=hp, col=col, c=c: nc.vector.scalar_tensor_tensor(
                        out=S[:, hp, :], in0=S[:, hp, :], scalar=dec[:, hp * 2 + c:hp * 2 + c + 1], in1=ba[:, col:col + 64],
                        op0=ALU.mult, op1=ALU.add), reads=[bab, decb], writes=[Sb])
                tgt = 1 if c == 0 else 0
                if c == 0:
                    K.op(dve, lambda: nc.vector.tensor_copy(out=S_bf[1][:], in_=S[:]), reads=[Sb], writes=[S_bfb[1]])
            yield
            for c in range(2):
                cs = slice(c * 64, (c + 1) * 64)
                for h in range(4):
                    mm(bo[cs, h * 64:(h + 1) * 64], attn_sb[cs, h * 64:(h + 1) * 64], va[cs, h * 64:(h + 1) * 64],
                       True, True, [attnb, vab], [bob])
            yield
            bi, bib = G()
            for h in (0, 2, 1, 3):
                hp, r = h // 2, (h % 2) * 64
                for c in range(2):
                    cs = slice(c * 64, (c + 1) * 64)
                    mm(bi[cs, h * 64:(h + 1) * 64], TT[r:r + 64, hp, cs], S_bf[c][r:r + 64, hp, :],
                       True, True, [TTb, S_bfb[c]], [bib])
            yield
            K.op(dve, lambda: nc.vector.tensor_copy(out=S_bf[0][:], in_=S[:]), reads=[Sb], writes=[S_bfb[0]])
            yield
            hgb = smb["hg"]
            K.op(act, lambda: nc.scalar.activation(out=Fx[:], in_=bo[:, 0:256], func=AF.Copy), reads=[bob], writes=[Fxb])
            yield
            K.op(dve, lambda: nc.vector.tensor_tensor(out=Fx[:], in0=Fx[:], in1=bi[:, 0:256], op=ALU.add),
                 reads=[bib], writes=[Fxb])
            yield
            K.op(dve, lambda: nc.vector.tensor_tensor(out=Fy[:], in0=Fx[:], in1=Fx[:], op=ALU.mult), reads=[Fxb], writes=[Fyb])
            yield
            K.op(dve, lambda: nc.vector.reduce_sum(out=small[:, 4:8], in_=Fy[:].rearrange("p (h d) -> p h d", d=64), axis=AX.X),
                 reads=[Fyb], writes=[hgb])
            yield
            rms_rstd(small[:, 4:8], 64.0, small[:, 8:12], hgb)
            yield
            for h in range(4):
                K.op(dve, lambda h=h: nc.vector.scalar_tensor_tensor(
                    out=XYf[1 - par][:, h * 64:(h + 1) * 64], in0=Fx[:, h * 64:(h + 1) * 64], scalar=small[:, 8 + h:9 + h],
                    in1=gate_a[:, h * 64:(h + 1) * 64], op0=ALU.mult, op1=ALU.mult), reads=[Fxb, hgb, gab], writes=[XYb[1 - par]])
            yield
            K.op(dve, lambda: nc.vector.tensor_tensor(out=XYf[1 - par][:, 256:512], in0=bo[:, 256:512], in1=gate_b[:], op=ALU.mult),
                 reads=[bob, gbb], writes=[XYb[1 - par]])


            yield

        groups = []
        for h in range(8):
            for j0 in range(0, i + 1, 4):
                groups.append([(h, j) for j in range(j0, min(j0 + 4, i + 1))])
        rdb = smb["rden"]

        def emit_S(gi):
            stt, stb = ST[gi % NST]
            ptb = PTb[gi % NPT]
            pt = PT[gi % NPT]
            for sl, (h, j) in enumerate(groups[gi]):
                hp, r = h // 2, (h % 2) * 64
                mm(stt[:, sl * P:(sl + 1) * P], KT[:, hp, j * P:(j + 1) * P], (QTa if r == 0 else QTz)[:, hp, :], True, True,
                   [KTb[j], QTb], [stb])
            ng = len(groups[gi])
            K.op(act, lambda: nc.scalar.activation(out=pt[:, 0:ng * P], in_=stt[:, 0:ng * P], func=AF.Exp, scale=0.125),
                 reads=[stb], writes=[ptb])
            gh, gj0 = groups[gi][0]
            vt, vtb = Vt[gi % NVT], Vtb[gi % NVT]
            K.op(dve, lambda: nc.vector.tensor_tensor(
                out=vt[:, 0:ng * 65].rearrange("p (a b) -> p a b", b=65), in0=V[:, gj0:gj0 + ng, gh * 65:(gh + 1) * 65],
                in1=fij[par][:, gj0:gj0 + ng, gh:gh + 1].to_broadcast([P, ng, 65]), op=ALU.mult),
                reads=[biasb[par]] + [Vb[j] for j in range(gj0, gj0 + ng)], writes=[vtb])
            for sl, (h, j) in enumerate(groups[gi]):
                if j == i:
                    K.op(dve, lambda sl=sl: nc.vector.tensor_tensor(
                        out=pt[:, sl * P:(sl + 1) * P], in0=pt[:, sl * P:(sl + 1) * P], in1=fmask[:], op=ALU.mult),
                        reads=[*CbL], writes=[ptb])

        def emit_PV(gi):
            pt, ptb = PT[gi % NPT], PTb[gi % NPT]
            for sl, (h, j) in enumerate(groups[gi]):
                oa, oab = OA[h // 4]
                c0 = (h % 4) * 65
                mm(oa[:, c0:c0 + 65], pt[:, sl * P:(sl + 1) * P], Vt[gi % NVT][:, sl * 65:(sl + 1) * 65], j == 0, j == i,
                   [ptb, Vtb[gi % NVT]], [oab], inc=True)
                if j == i and h % 4 == 3:
                    hb = h - 3
                    K.op(dve, lambda oa=oa: nc.vector.reciprocal(
                        out=small[:, 44:48], in_=oa[:, 0:260].rearrange("p (h e) -> p h e", e=65)[:, :, 64]),
                        reads=[oab], writes=[rdb])
                    for hh in range(4):
                        K.op(dve, lambda hh=hh, oa=oa, hb=hb: nc.vector.scalar_tensor_tensor(
                            out=XYf[1 - par][:, 512 + (hb + hh) * 64:512 + (hb + hh + 1) * 64], in0=oa[:, hh * 65:hh * 65 + 64],
                            scalar=small[:, 44 + hh:45 + hh], in1=gate_c[:, (hb + hh) * 64:(hb + hh + 1) * 64],
                            op0=ALU.mult, op1=ALU.mult), reads=[oab, rdb, gcb], writes=[XYb[1 - par]])

        gen = ab_thread()
        steps_per_group = max(1, -(-12 // len(groups)))
        if next_front is not None:
            if next_front[2] != p:
                load_cast(preg[:], ld[next_front[2]]["preg_bc"][:, :], Lpre)
            front(*next_front)
        gstate[1] = [0, 1, 2]
        emit_S(0)
        if len(groups) > 1:
            emit_S(1)
        for gi in range(len(groups)):
            if gi + 2 < len(groups):
                emit_S(gi + 2)
            emit_PV(gi)
            for _ in range(steps_per_group):
                next(gen, None)
        for _ in gen:
            pass
        gstate[1] = [0, 1, 2, 3, 4, 5]

        tm, tmb = G()
        for ec in range(8):
            tr(bf(tm)[:, ec * P:(ec + 1) * P], XYf[1 - par][:, ec * P:(ec + 1) * P], [XYb[1 - par]], [tmb])
        K.op(act, lambda: nc.scalar.activation(out=XYf[par], in_=bf(tm)[:, 0:D], func=AF.Copy),
             reads=[tmb], writes=[XYb[par]])
        if next_front is not None:
            front_b(1 - par)
        ys = []
        for half in range(2):
            y_, yb_ = G()
            for ec in range(8):
                mm(y_[:, 0:512], XYc[par][:, ec, :], Wout[:, ec, half * 512:(half + 1) * 512], ec == 0, ec == 7,
                   [XYb[par], Wob], [yb_])
            ys.append((y_, yb_))
        if prefetch is not None:
            load_wout(prefetch)
        n2 = smb["n2"]
        for half in range(2):
            y_, yb_ = ys[half]
            K.op(act, lambda y_=y_, half=half: nc.scalar.activation(
                out=hh[:, half * 512:(half + 1) * 512], in_=y_[:, 0:512], func=AF.Square,
                accum_out=small[:, 48 + half:49 + half]), reads=[yb_], writes=[hhb, n2])
        K.op(dve, lambda: nc.vector.tensor_tensor(out=small[:, 50:51], in0=small[:, 48:49], in1=small[:, 49:50], op=ALU.add),
             writes=[n2])
        rms_rstd(small[:, 50:51], float(D), small[:, 51:52], n2)
        for q4 in range(4):
            y_, yb_ = ys[q4 // 2]
            c0 = (q4 % 2) * 256
            K.op(dve, lambda y_=y_, q4=q4, c0=c0: nc.vector.scalar_tensor_tensor(
                out=Fs[q4][:], in0=y_[:, c0:c0 + 256], scalar=small[:, 51:52], in1=postg[:, q4 * 256:(q4 + 1) * 256],
                op0=ALU.mult, op1=ALU.mult), reads=[yb_, n2, Lpost], writes=[Fb[q4]])
            K.op(pool, lambda q4=q4: nc.gpsimd.tensor_tensor(
                out=x_seq[:, i, q4 * 256:(q4 + 1) * 256], in0=x_seq[:, i, q4 * 256:(q4 + 1) * 256],
                in1=Fs[q4][:], op=ALU.add), reads=[Fb[q4]], writes=[Xb[i]])
        finish()

    if 1 in lb_modes:
        compute_lb1(list(lb_modes).index(1))

    x_preloaded = set()
    for s in range(NSEQ):
        for i in range(NB):
            if (s, i) not in x_preloaded:
                load_f32(x_seq[:, i, :], x_d[s, i * P:(i + 1) * P, :], Xb[i])
        for p in range(NL):
            first = (s == 0 and p == 0)
            load_layer(p, with_weights=first, with_preg=first)
            nxt = None
            if p + 1 < NL:
                nxt = p + 1
            elif s + 1 < NSEQ:
                nxt = 0
            K.op(pool, lambda: nc.gpsimd.memset(S[:], 0.0), writes=[Sb])
            K.op(pool, lambda: nc.gpsimd.memset(S_bf[0][:], 0.0), writes=[S_bfb[0]])
            K.op(pool, lambda: nc.gpsimd.memset(small[:, 28:36], 0.0), writes=[smb["acc"]])
            for i in range(NB):
                if i == NB - 1 and p == NL - 1 and s + 1 < NSEQ:
                    for i2 in range(NB - 1):
                        load_f32(x_seq[:, i2, :], x_d[s + 1, i2 * P:(i2 + 1) * P, :], Xb[i2])
                        x_preloaded.add((s + 1, i2))
                bg = (s == 0 and p + 1 < NL and NB >= 2)
                if bg and i == NB - 1:
                    wsc_ready[p + 1] = True
                if i + 1 < NB:
                    nf = (s, i + 1, p)
                elif nxt is not None:
                    nf = (s, 0, p + 1) if p + 1 < NL else (s + 1, 0, 0)
                else:
                    nf = None
                block(s, i, p, p == NL - 1, do_front=(first and i == 0), next_front=nf,
                      prefetch=(nxt if i == NB - 1 else None))
                if bg and i < NB - 1:
                    q = p + 1
                    per = -(-16 // (NB - 1))
                    for pc in range(i * per, min(16, (i + 1) * per)):
                        if pc < 8:
                            K.dma(K.q_pool, lambda pc=pc, q=q: nc.gpsimd.dma_start(
                                out=wsc_in[q][:, pc, :], in_=ld[q]["w_in"][pc * P:(pc + 1) * P, :]), writes=[wsc_ib[q][pc]])
                        else:
                            K.dma(K.q_pool, lambda pc=pc, q=q: nc.gpsimd.dma_start(
                                out=wsc_out[q][:, pc - 8, :], in_=ld[q]["w_out"][(pc - 8) * P:(pc - 7) * P, :]),
                                writes=[wsc_ob[q][pc - 8]])
    sp.wait_all([(l[0], l[1]) for l in K.q_sp.lanes if l[1] > 0])
    return nc


def _layer_inputs(l, lower_bounds, pre_norm_g, w_in, hgrn_norm_g, fox_f_bias, pool_w, pool_scale, w_out, post_norm_g):
    f = np.float32
    bc = lambda v: np.ascontiguousarray(np.broadcast_to(np.asarray(v, f)[None, :], (P, v.shape[0])))
    pwl = np.zeros((P, 128), f)
    for g in range(4):
        pwl[(g % 2) * 64:(g % 2) * 64 + 64, (g // 2) * 64:(g // 2 + 1) * 64] = pool_w[l, g]
    return {
        "w_in": np.ascontiguousarray(w_in[l], f), "w_out": np.ascontiguousarray(w_out[l], f),
        "preg_bc": bc(pre_norm_g[l]), "postg_bc": bc(post_norm_g[l]),
        "hgrng_bc": bc(hgrn_norm_g[l]), "pscale_bc": bc(pool_scale[l]),
        "lbraw_bc": bc(np.concatenate([lower_bounds[0], lower_bounds[1]])),
        "fbias_bc": bc(fox_f_bias[l]), "pw": pwl,
    }


FUSED = True
_cache = {}


def run(x, params, T, NSEQ, ncores, layer_groups):
    consts = _consts()
    cur = np.ascontiguousarray(x, np.float32)
    for grp in layer_groups:
        key = (T, NSEQ, tuple(grp))
        if key not in _cache:
            _cache[key] = build_program(T, NSEQ, [0 if l == 0 else 1 for l in grp])
        nc = _cache[key]
        base = {"c_" + k: v for k, v in consts.items()}
        for p, l in enumerate(grp):
            for k, v in _layer_inputs(l, **params).items():
                base["l%d_%s" % (p, k)] = v
        in_maps = []
        for c in range(ncores):
            m = dict(base)
            m["x"] = np.ascontiguousarray(cur[c * NSEQ:(c + 1) * NSEQ])
            in_maps.append(m)
        res = run_bass_kernel_spmd(nc, in_maps, core_ids=list(range(ncores)))
        cur = np.concatenate([np.asarray(r["out"], np.float32) for r in res.results], axis=0)
    return cur


def kernel(x, lower_bounds, pre_norm_g, w_in, hgrn_norm_g, fox_f_bias, pool_w, pool_scale, w_out, post_norm_g):
    params = dict(lower_bounds=np.asarray(lower_bounds), pre_norm_g=np.asarray(pre_norm_g), w_in=np.asarray(w_in),
                  hgrn_norm_g=np.asarray(hgrn_norm_g), fox_f_bias=np.asarray(fox_f_bias), pool_w=np.asarray(pool_w),
                  pool_scale=np.asarray(pool_scale), w_out=np.asarray(w_out), post_norm_g=np.asarray(post_norm_g))
    x = np.asarray(x)
    B, T, _ = x.shape
    groups = [[0, 1]] if FUSED else [[0], [1]]
    return run(x, params, T, B // 8, 8, groups).astype(np.float32)
```

```python
import numpy as np
import concourse.bass as bass
import concourse.mybir as mybir
from concourse.bass_utils import run_bass_kernel_spmd

F32 = mybir.dt.float32
BF16 = mybir.dt.bfloat16
AF = mybir.ActivationFunctionType
ALU = mybir.AluOpType
AX = mybir.AxisListType

D = 1024
INW = 3592
EPS = 1e-6
TINY = 1e-30
P = 128
POOL_WINDOWS = (2, 4, 8, 16)
EPOCH = 24000


CLOCKS = {}


class Buf:
    __slots__ = ("w", "r", "excl")

    def __init__(self, excl=False):
        self.w = None
        self.r = {}
        self.excl = excl


class Eng:
    def __init__(self, nc, eng, name):
        self.nc, self.eng, self.name = nc, eng, name
        self.sem = nc.alloc_semaphore(name + "_s0")
        self.own = {id(self.sem)}
        self.n = 0
        self.ep = 0
        self.seen = {}
        self.pending = 0

    def wait_all(self, deps):
        deps = [d for d in deps if self.seen.get(id(d[0]), 0) < d[1]]
        if len(deps) > 1:
            keep = []
            for d in deps:
                k = id(d[0])
                implied = False
                for d2 in deps:
                    if d2 is not d:
                        c2 = CLOCKS.get((id(d2[0]), d2[1]))
                        if c2 is not None and c2.get(k, 0) >= d[1]:
                            implied = True
                            break
                if not implied:
                    keep.append(d)
            deps = keep
        for sem, val in deps:
            key = id(sem)
            if self.seen.get(key, 0) < val:
                self.eng.wait_ge(sem, val)
                self.seen[key] = val
            c = CLOCKS.get((key, val))
            if c is not None:
                seen = self.seen
                for k2, v2 in c.items():
                    if seen.get(k2, 0) < v2:
                        seen[k2] = v2

    def snapshot(self, tok):
        c = dict(self.seen)
        c[id(tok[0])] = max(c.get(id(tok[0]), 0), tok[1])
        CLOCKS[(id(tok[0]), tok[1])] = c

    def issue(self, ins):
        if self.n >= EPOCH and self.pending == 0:
            self.ep += 1
            self.sem = self.nc.alloc_semaphore("%s_s%d" % (self.name, self.ep))
            self.own.add(id(self.sem))
            self.n = 0
        self.n += 1
        ins.then_inc(self.sem, 1)
        self.pending = 0
        return (self.sem, self.n)

    def issue_noinc(self, ins):
        if self.n >= EPOCH:
            pass
        self.pending += 1
        return (self.sem, self.n + 1)


class DmaQ:
    def __init__(self, nc, E, name, nlanes):
        self.E = E
        self.lanes = [[nc.alloc_semaphore("%s_l%d" % (name, i)), 0] for i in range(nlanes)]
        self.k = 0

    def issue(self, fn, deps):
        lane = self.lanes[self.k]
        self.k = (self.k + 1) % len(self.lanes)
        d = set(deps)
        if lane[1] > 0:
            d.add((lane[0], lane[1]))
        self.E.wait_all(d)
        lane[1] += 16
        fn().then_inc(lane[0], 16)
        tok = (lane[0], lane[1])
        self.E.snapshot(tok)
        return tok


class Ctx:
    def __init__(self, nc):
        self.nc = nc
        self.pe = Eng(nc, nc.tensor, "pe")
        self.act = Eng(nc, nc.scalar, "act")
        self.dve = Eng(nc, nc.vector, "dve")
        self.pool = Eng(nc, nc.gpsimd, "pool")
        self.sp = Eng(nc, nc.sync, "sp")
        self.q_sp = DmaQ(nc, self.sp, "qsp", 8)
        self.q_pool = DmaQ(nc, self.pool, "qpl", 16)

    def _deps(self, reads, writes, E=None):
        deps = set()
        own = E.own if E is not None else ()
        is_pe = E is self.pe
        for b in reads:
            if b.w is not None and not (is_pe and id(b.w[0]) in own):
                deps.add(b.w)
            if b.excl:
                for tok in b.r.values():
                    if id(tok[0]) not in own:
                        deps.add(tok)
        for b in writes:
            if b.w is not None and not (is_pe and id(b.w[0]) in own):
                deps.add(b.w)
            for tok in b.r.values():
                if not (is_pe and id(tok[0]) in own):
                    deps.add(tok)
        return deps

    @staticmethod
    def _commit(tok, reads, writes):
        for b in reads:
            b.r[id(tok[0])] = tok
        for b in writes:
            b.w = tok
            b.r = {}

    def op(self, E, fn, reads=(), writes=(), inc=True, mode=None):
        if mode is not None and mode != getattr(self, "pe_mode", None):
            if E.n > 0:
                assert E.pending == 0
                E.eng.wait_ge(E.sem, E.n)
                E.seen[id(E.sem)] = E.n
            self.pe_mode = mode
        E.wait_all(self._deps(reads, writes, E))
        tok = E.issue(fn()) if inc else E.issue_noinc(fn())
        if inc:
            E.snapshot(tok)
        elif (id(tok[0]), tok[1]) not in CLOCKS:
            E.snapshot(tok)
        self._commit(tok, reads, writes)
        return tok

    def dma(self, Q, fn, reads=(), writes=()):
        tok = Q.issue(fn, self._deps(reads, writes))
        self._commit(tok, reads, writes)
        return tok


def _consts():
    s = np.arange(P)[:, None]
    t = np.arange(P)[None, :]
    same = (s // 64) == (t // 64)
    c = {}
    c["ident"] = np.eye(P, dtype=np.float32)
    c["ucum"] = (same & (s <= t)).astype(np.float32)
    c["lstr"] = (same & (s > t)).astype(np.float32)
    c["ufull"] = (s <= t).astype(np.float32)
    c["ones"] = np.ones((P, P), np.float32)
    ind = np.zeros((P, 2), np.float32)
    ind[:64, 0] = 1
    ind[64:, 1] = 1
    c["ind"] = ind
    c["fmask"] = (s <= t).astype(np.float32)
    hm = np.zeros((P, 256), np.float32)
    for h in range(4):
        hm[:, h * 64:(h + 1) * 64] = ((np.arange(P)[:, None] % 64) <= np.arange(64)[None, :])
    c["hmask"] = hm
    pm_cur = np.zeros((P, 4, P), np.float32)
    pm_prev = np.zeros((P, 4, 16), np.float32)
    pm_cur0 = np.zeros((P, 4, 16), np.float32)
    for g, w in enumerate(POOL_WINDOWS):
        for tt in range(P):
            for ss in range(tt - w + 1, tt + 1):
                if ss >= 0:
                    pm_cur[ss, g, tt] += 1.0 / w
                else:
                    if tt < 16:
                        pm_prev[ss + P, g, tt] += 1.0 / w
            pm_cur[tt, g, tt] -= 1.0
        for tt in range(16):
            cnt = min(tt + 1, w)
            for ss in range(max(0, tt - w + 1), tt + 1):
                pm_cur0[ss, g, tt] += 1.0 / cnt
            pm_cur0[tt, g, tt] -= 1.0
    c["pm_cur"] = pm_cur.reshape(P, 4 * P)
    c["pm_prev"] = pm_prev.reshape(P, 64)
    c["pm_cur0"] = pm_cur0.reshape(P, 64)
    return c


CONST_SHAPES = {"ident": (P, P), "ucum": (P, P), "lstr": (P, P), "ufull": (P, P), "ones": (P, P),
                "ind": (P, 2), "fmask": (P, P), "hmask": (P, 256), "pm_cur": (P, 512),
                "pm_prev": (P, 64), "pm_cur0": (P, 64)}
LAYER_SHAPES = {"w_in": (D, INW), "w_out": (D, D), "preg_bc": (P, D), "postg_bc": (P, D),
                "hgrng_bc": (P, 256), "pscale_bc": (P, 256), "lbraw_bc": (P, 512),
                "fbias_bc": (P, 8), "pw": (P, 128)}


def build_program(T, NSEQ, lb_modes, debug_out=False):
    NL = len(lb_modes)
    NB = T // P
    nc = bass.Bass("TRN2", target_bir_lowering=False)
    CLOCKS.clear()
    K = Ctx(nc)
    pe, act, dve, pool, sp = K.pe, K.act, K.dve, K.pool, K.sp

    x_d = nc.dram_tensor("x", [NSEQ, T, D], F32, kind="ExternalInput").ap()
    out_d = nc.dram_tensor("out", [NSEQ, T, D], F32, kind="ExternalOutput").ap()
    cd = {k: nc.dram_tensor("c_" + k, list(v), F32, kind="ExternalInput").ap() for k, v in CONST_SHAPES.items()}
    ld = [{k: nc.dram_tensor("l%d_%s" % (p, k), list(v), F32, kind="ExternalInput").ap()
           for k, v in LAYER_SHAPES.items()} for p in range(NL)]

    wsc_in = [nc.dram_tensor("wsc_in%d" % p, [P, 8, INW], BF16, kind="Internal").ap() for p in range(NL)]
    wsc_out = [nc.dram_tensor("wsc_out%d" % p, [P, 8, D], BF16, kind="Internal").ap() for p in range(NL)]
    wsc_ib = [[Buf() for _ in range(8)] for _ in range(NL)]
    wsc_ob = [[Buf() for _ in range(8)] for _ in range(NL)]
    wsc_ready = [False] * NL

    def sb(name, shape, dt):
        return nc.alloc_sbuf_tensor(name, list(shape), dt)

    x_seq = sb("x_seq", [P, NB, D], F32)
    Xb = [Buf() for _ in range(NB)]
    Win = sb("Win", [P, 8, INW], BF16)
    Wout = sb("Wout", [P, 8, D], BF16)
    Wib = Buf()
    Wob = Buf()
    KT = sb("KT", [P, 4, T], BF16)
    KTb = [Buf() for _ in range(NB)]
    V = sb("V", [P, NB, 8 * 65], BF16)
    Vb = [Buf() for _ in range(NB)]
    Vones = Buf()
    ident = sb("ident", [P, P], BF16)
    ucum = sb("ucum", [P, P], F32)
    lstr = sb("lstr", [P, P], F32)
    ufull = sb("ufull", [P, P], F32)
    ones = sb("ones", [P, P], F32)
    ind = sb("ind", [P, 2], F32)
    fmask = sb("fmask", [P, P], BF16)
    hmask = sb("hmask", [P, 256], BF16)
    pm_cur = sb("pm_cur", [P, 512], BF16)
    pm_prev = sb("pm_prev", [P, 64], BF16)
    pm_cur0 = sb("pm_cur0", [P, 64], BF16)
    CbL = [Buf() for _ in range(len(CONST_SHAPES))]
    preg = sb("preg", [P, D], BF16)
    postg = sb("postg", [P, D], BF16)
    hgrng = sb("hgrng", [P, 256], BF16)
    pscale = sb("pscale", [P, 256], BF16)
    lb_bc = sb("lb_bc", [P, 256], F32)
    oml_bc = sb("oml_bc", [P, 256], F32)
    lbraw = None
    fbias = sb("fbias", [P, 8], F32)
    pw = sb("pw", [P, 128], BF16)
    Lpre, Lpost, Lpw, Lhg, Lps, Lfb, Llb = (Buf() for _ in range(7))
    hm = sb("hm", [P, D], BF16)
    hmb = Buf()
    hh = sb("hh", [P, D], BF16)
    hhb = Buf()
    hT = sb("hT", [P, 8, P], BF16)
    hTb = Buf()
    XYf = [hT[:].rearrange("p a b -> p (a b)"), hm[:]]
    XYc = [hT[:], hm[:].rearrange("p (a b) -> p a b", b=P)]
    XYb = [hTb, hmb]
    NF = 5
    Fs = [sb("F%d" % i, [P, 256], F32) for i in range(NF)]
    Fb = [Buf() for _ in range(NF)]
    ebig = sb("ebig", [P, 256], F32)
    ebb = Buf()
    gate_a = sb("gate_a", [P, 256], BF16)
    gate_b = sb("gate_b", [P, 256], BF16)
    gate_c = sb("gate_c", [P, 512], BF16)
    gab, gbb, gcb = Buf(), Buf(), Buf()
    qp = sb("qp", [P, 256], BF16)
    qpp = sb("qpp", [P, 256], BF16)
    kp = sb("kp", [P, 256], BF16)
    va = sb("va", [P, 256], BF16)
    attn_sb = qp
    qpb, qppb, kpb, vab = Buf(), Buf(), Buf(), Buf()
    attnb = qpb
    TT = sb("TT", [P, 6, P], BF16)
    TTb = Buf()
    S = sb("S", [P, 2, 64], F32)
    Sb = Buf()
    S_bf = [sb("S_bf%d" % i, [P, 2, 64], BF16) for i in range(2)]
    S_bfb = [Buf(), Buf()]
    dec = sb("dec", [P, 4], F32)
    decb = Buf()
    u_sb = [sb("u_sb%d" % i, [P, 256], BF16) for i in range(2)]
    ub = [Buf(), Buf()]
    plT = qpp[:].rearrange("p (a b) -> p a b", b=P)
    plTb = qppb
    q_sb = Fs[3][:].bitcast(BF16)
    k_sb = Fs[4][:].bitcast(BF16)
    qsb_b, ksb_b = Fb[3], Fb[4]
    QTa = sb("QTa", [P, 4, P], BF16)
    QTz = sb("QTz", [P, 4, P], BF16)
    QTb = Buf()
    NPT = 3
    NVT = 3
    Vt = [sb("Vt%d" % i, [P, 4 * 65], BF16) for i in range(NVT)]
    Vtb = [Buf() for _ in range(NVT)]
    PT = [sb("PT%d" % i, [P, 512], BF16) for i in range(NPT)]
    PTb = [Buf() for _ in range(NPT)]
    fij = [sb("fij%d" % i, [P, NB, 8], F32) for i in range(2)]
    biasb = [Buf(), Buf()]
    totall = sb("totall", [P, NB, 8], F32)
    small = sb("small", [P, 64], F32)
    smb = {k: Buf() for k in ("n1", "hg", "fc", "acc", "tot", "rden", "n2", "w")}

    banks = [nc.alloc_psum_tensor("bank%d" % i, [P, 512], F32) for i in range(8)]
    bankb = [Buf(excl=True) for _ in range(8)]
    gstate = [0, [0, 1, 2, 3, 4, 5]]

    def G():
        lst = gstate[1]
        i = lst[gstate[0] % len(lst)]
        gstate[0] += 1
        return banks[i], bankb[i]

    NST = 3
    ST = [(banks[3], bankb[3]), (banks[4], bankb[4]), (banks[5], bankb[5])]
    OA = [(banks[6], bankb[6]), (banks[7], bankb[7])]

    def bf(t):
        return t[:].bitcast(BF16)

    def load_cast(dst_ap, src_ap, buf):
        K.dma(K.q_pool, lambda: nc.gpsimd.dma_start(out=dst_ap, in_=src_ap), writes=[buf])

    def load_f32(dst_ap, src_ap, buf):
        K.dma(K.q_sp, lambda: nc.sync.dma_start(out=dst_ap, in_=src_ap), writes=[buf])

    _cseen = []
    for name, t_, cast in (("ident", ident, 1), ("ucum", ucum, 0), ("lstr", lstr, 0), ("ufull", ufull, 0),
                           ("ones", ones, 0), ("ind", ind, 0), ("fmask", fmask, 1), ("hmask", hmask, 1),
                           ("pm_cur", pm_cur, 1), ("pm_prev", pm_prev, 1), ("pm_cur0", pm_cur0, 1)):
        (load_cast if cast else load_f32)(t_[:], cd[name][:, :], CbL[len(_cseen)])
        _cseen.append(name)
    Vv = V[:].rearrange("p b (h e) -> p b h e", e=65)
    K.op(pool, lambda: nc.gpsimd.memset(QTa[:], 0.0), writes=[QTb])
    K.op(pool, lambda: nc.gpsimd.memset(QTz[:], 0.0), writes=[QTb])

    def load_win(p):
        if wsc_ready[p]:
            for dc in range(8):
                K.dma(K.q_sp, lambda dc=dc: nc.sync.dma_start(out=Win[:, dc, :], in_=wsc_in[p][:, dc, :]),
                      reads=[wsc_ib[p][dc]], writes=[Wib])
        else:
            for dc in range(8):
                load_cast(Win[:, dc, :], ld[p]["w_in"][dc * P:(dc + 1) * P, :], Wib)
            for dc in range(8):
                K.dma(K.q_sp, lambda dc=dc: nc.sync.dma_start(out=wsc_in[p][:, dc, :], in_=Win[:, dc, :]),
                      reads=[Wib], writes=[wsc_ib[p][dc]])

    def load_wout(p):
        if wsc_ready[p]:
            for ec in range(8):
                K.dma(K.q_sp, lambda ec=ec: nc.sync.dma_start(out=Wout[:, ec, :], in_=wsc_out[p][:, ec, :]),
                      reads=[wsc_ob[p][ec]], writes=[Wob])
        else:
            for ec in range(8):
                load_cast(Wout[:, ec, :], ld[p]["w_out"][ec * P:(ec + 1) * P, :], Wob)
            for ec in range(8):
                K.dma(K.q_sp, lambda ec=ec: nc.sync.dma_start(out=wsc_out[p][:, ec, :], in_=Wout[:, ec, :]),
                      reads=[Wob], writes=[wsc_ob[p][ec]])
            wsc_ready[p] = True

    def load_layer(p, with_weights=True, with_preg=True):
        L = ld[p]
        if with_weights:
            load_win(p)
            load_wout(p)
        if with_preg:
            load_cast(preg[:], L["preg_bc"][:, :], Lpre)
        load_cast(postg[:], L["postg_bc"][:, :], Lpost)
        load_cast(pw[:], L["pw"][:, :], Lpw)
        load_cast(hgrng[:], L["hgrng_bc"][:, :], Lhg)
        load_cast(pscale[:], L["pscale_bc"][:, :], Lps)
        load_f32(fbias[:], L["fbias_bc"][:, :], Lfb)

    def compute_lb1(p):
        L = ld[p]
        load_f32(Fs[1][:], L["lbraw_bc"][:, 0:256], Fb[1])
        load_f32(Fs[2][:], L["lbraw_bc"][:, 256:512], Fb[2])
        K.op(pool, lambda: nc.gpsimd.tensor_tensor(out=Fs[0][:], in0=Fs[1][:], in1=Fs[2][:],
                                                   op=ALU.subtract), reads=[Fb[1], Fb[2]], writes=[Fb[0]])
        K.op(act, lambda: nc.scalar.activation(out=Fs[0][:], in_=Fs[0][:], func=AF.Exp),
             reads=[], writes=[Fb[0]])
        K.op(pool, lambda: nc.gpsimd.tensor_scalar(out=Fs[0][:], in0=Fs[0][:], scalar1=1.0, scalar2=1.0,
                                                   op0=ALU.mult, op1=ALU.add), writes=[Fb[0]])
        K.op(dve, lambda: nc.vector.reciprocal(out=lb_bc[:], in_=Fs[0][:]), reads=[Fb[0]], writes=[Llb])
        K.op(pool, lambda: nc.gpsimd.tensor_scalar(out=oml_bc[:], in0=lb_bc[:], scalar1=-1.0, scalar2=1.0,
                                                   op0=ALU.mult, op1=ALU.add), writes=[Llb])

    def silu_gate(ps_ap, width, out_ap, outb, mul_ap=None, psb=None):
        e = ebig[:, 0:width]
        K.op(act, lambda: nc.scalar.activation(out=e, in_=ps_ap, func=AF.Exp, scale=-1.0),
             reads=[psb], writes=[ebb])
        K.op(act, lambda: nc.scalar.activation(out=e, in_=e, func=AF.Ln, bias=1.0), writes=[ebb])
        K.op(act, lambda: nc.scalar.activation(out=e, in_=e, func=AF.Exp, scale=-1.0), writes=[ebb])
        if mul_ap is None:
            K.op(dve, lambda: nc.vector.tensor_tensor(out=out_ap, in0=ps_ap, in1=e, op=ALU.mult),
                 reads=[psb, ebb], writes=[outb])
        else:
            K.op(dve, lambda: nc.vector.tensor_tensor(out=e, in0=ps_ap, in1=e, op=ALU.mult),
                 reads=[psb], writes=[ebb])
            K.op(pool, lambda: nc.gpsimd.tensor_tensor(out=out_ap, in0=e, in1=mul_ap, op=ALU.mult),
                 reads=[ebb, Lhg, Lps], writes=[outb])

    def rms_rstd(ss_ap, n, out_ap, b):
        K.op(act, lambda: nc.scalar.activation(out=out_ap, in_=ss_ap, func=AF.Ln, scale=1.0 / n, bias=EPS), writes=[b])
        K.op(act, lambda: nc.scalar.activation(out=out_ap, in_=out_ap, func=AF.Exp, scale=-0.5), writes=[b])

    def _cls(n):
        return 32 if n <= 32 else (64 if n <= 64 else 128)

    def mm(out, lhsT, rhs, start, stop, reads, writes, inc=None):
        kc = _cls(lhsT.shape[0])
        mode = ("mm", str(lhsT.dtype), kc, _cls(lhsT.shape[-1]), lhsT.base_partition() if kc < 128 else 0)
        return K.op(pe, lambda: nc.tensor.matmul(out, lhsT, rhs, start=start, stop=stop), reads=reads, writes=writes,
                    inc=(bool(stop) or kc < 128) if inc is None else inc, mode=mode)

    def tr(out, in_, reads, writes):
        return K.op(pe, lambda: nc.tensor.transpose(out, in_, ident[:]), reads=list(reads) + [*CbL], writes=writes,
                    mode=("tr",))

    def front(s, i, p):
        xb = x_seq[:, i, :]
        n1 = smb["n1"]
        K.op(act, lambda: nc.scalar.activation(out=hh[:], in_=xb, func=AF.Square, accum_out=small[:, 0:1]),
             reads=[Xb[i]], writes=[hhb, n1])
        rms_rstd(small[:, 0:1], float(D), small[:, 2:3], n1)
        K.op(dve, lambda: nc.vector.scalar_tensor_tensor(out=hh[:], in0=xb, scalar=small[:, 2:3], in1=preg[:],
                                                         op0=ALU.mult, op1=ALU.mult),
             reads=[Xb[i], n1, Lpre], writes=[hhb])

    def front_b(tp):
        tb, tbb = G()
        for dc in range(8):
            tr(bf(tb)[:, dc * P:(dc + 1) * P], hh[:, dc * P:(dc + 1) * P], [hhb], [tbb])
        K.op(act, lambda: nc.scalar.activation(out=XYf[tp], in_=bf(tb)[:, 0:D], func=AF.Copy),
             reads=[tbb], writes=[XYb[tp]])


    def block(s, i, p, last_layer, do_front=True, next_front=None, prefetch=None):
        xb = x_seq[:, i, :]
        par = i % 2

        def finish():
            if last_layer:
                K.dma(K.q_sp, lambda: nc.sync.dma_start(out=out_d[s, i * P:(i + 1) * P, :], in_=x_seq[:, i, :]),
                      reads=[Xb[i]])
        if do_front:
            front(s, i, p)
            front_b(i % 2)
        def proj(col0, width):
            b_, bb_ = G()
            for dc in range(8):
                mm(b_[:, 0:width], XYc[par][:, dc, :], Win[:, dc, col0:col0 + width], dc == 0, dc == 7,
                   [XYb[par], Wib], [bb_])
            return b_, bb_

        b7, b7b = proj(3584, 8)
        fcb = smb["fc"]
        K.op(dve, lambda: nc.vector.tensor_tensor(out=small[:, 12:20], in0=b7[:, 0:8], in1=fbias[:], op=ALU.add),
             reads=[b7b, Lfb], writes=[fcb])
        K.op(act, lambda: nc.scalar.activation(out=small[:, 12:20], in_=small[:, 12:20], func=AF.Exp, scale=-1.0),
             writes=[fcb])
        K.op(act, lambda: nc.scalar.activation(out=small[:, 20:28], in_=small[:, 12:20], func=AF.Ln, bias=1.0), writes=[fcb])
        sp_ap = small[:, 20:28]
        acc_ap = small[:, 28:36]
        accb = smb["acc"]
        b3, b3b = proj(1536, 512)
        K.op(act, lambda: nc.scalar.activation(out=q_sb, in_=b3[:, 0:512], func=AF.Copy), reads=[b3b], writes=[qsb_b])
        b4, b4b = proj(2048, 512)
        K.op(dve, lambda: nc.vector.tensor_copy(out=k_sb, in_=b4[:, 0:512]), reads=[b4b], writes=[ksb_b])
        tq, tqb = G()
        for hp in range(4):
            tr(bf(tq)[:, hp * P:(hp + 1) * P], q_sb[:, hp * P:(hp + 1) * P], [qsb_b], [tqb])
        for hp in range(4):
            tr(bf(tq)[:, (4 + hp) * P:(5 + hp) * P], k_sb[:, hp * P:(hp + 1) * P], [ksb_b], [tqb])
        K.op(act, lambda: nc.scalar.activation(out=QTa[0:64].rearrange("p a b -> p (a b)"), in_=bf(tq)[0:64, 0:512], func=AF.Copy),
             reads=[tqb], writes=[QTb])
        K.op(act, lambda: nc.scalar.activation(out=QTz[64:128].rearrange("p a b -> p (a b)"), in_=bf(tq)[64:128, 0:512], func=AF.Copy),
             reads=[tqb], writes=[QTb])
        K.op(act, lambda: nc.scalar.activation(out=KT[:, :, i * P:(i + 1) * P],
                                               in_=bf(tq)[:, 512:1024].rearrange("p (a b) -> p a b", b=P),
                                               func=AF.Copy), reads=[tqb], writes=[KTb[i]])

        bc_, bcb = G()
        mm(bc_[:, 0:8], ufull[:], sp_ap, True, False, [*CbL, fcb], [bcb])
        mm(bc_[:, 0:8], ones[:], acc_ap, False, True, [*CbL, accb], [bcb])
        K.op(pool, lambda: nc.gpsimd.tensor_tensor(out=acc_ap, in0=acc_ap, in1=sp_ap, op=ALU.add),
             reads=[fcb], writes=[accb])
        mm(bc_[:, 8:16], ones[:], acc_ap, True, True, [*CbL, accb], [bcb])
        totb = smb["tot"]
        wb = smb["w"]
        K.op(dve, lambda: nc.vector.tensor_copy(out=totall[:, i, :], in_=bc_[:, 8:16]), reads=[bcb], writes=[totb])
        K.op(dve, lambda: nc.vector.tensor_tensor(out=small[:, 52:60], in0=bc_[:, 0:8], in1=totall[:, i, :],
                                                  op=ALU.subtract), reads=[bcb, totb], writes=[wb])
        K.op(act, lambda: nc.scalar.activation(out=small[:, 52:60], in_=small[:, 52:60], func=AF.Exp), writes=[wb])
        K.op(pool, lambda: nc.gpsimd.tensor_tensor(
            out=fij[par][:, 0:i + 1, :], in0=totall[:, 0:i + 1, :],
            in1=totall[:, i:i + 1, :].to_broadcast([P, i + 1, 8]), op=ALU.subtract),
            reads=[totb], writes=[biasb[par]])
        K.op(act, lambda: nc.scalar.activation(out=fij[par][:, 0:i + 1, :], in_=fij[par][:, 0:i + 1, :], func=AF.Exp),
             writes=[biasb[par]])
        b5, b5b = proj(2560, 512)
        K.op(dve, lambda: nc.vector.tensor_tensor(
            out=Vv[:, i, :, 0:64], in0=b5[:, 0:512].rearrange("p (h d) -> p h d", d=64),
            in1=small[:, 52:60].unsqueeze(2).to_broadcast([P, 8, 64]), op=ALU.mult),
            reads=[b5b, smb["w"]], writes=[Vb[i]])
        K.op(pool, lambda: nc.gpsimd.tensor_copy(out=Vv[:, i, :, 64], in_=small[:, 52:60]),
             reads=[smb["w"]], writes=[Vb[i]])
        b6, b6b = proj(3072, 512)
        silu_gate(b6[:, 0:256], 256, gate_c[:, 0:256], gcb, psb=b6b)
        silu_gate(b6[:, 256:512], 256, gate_c[:, 256:512], gcb, psb=b6b)
        gstate[1] = [0, 1, 2]
        b0, b0b = proj(0, 512)
        Fq, Fqb = Fs[0], Fb[0]
        Fk, Fkb = Fs[1], Fb[1]
        Fl, Flb = Fs[2], Fb[2]
        Fx, Fxb = Fs[3], Fb[3]
        Fy, Fyb = Fs[4], Fb[4]
        b1, b1b = proj(512, 512)
        b2, b2b = proj(1024, 512)
        if prefetch is not None:
            load_win(prefetch)
        def ab_thread():
            K.op(act, lambda: nc.scalar.activation(out=Fy[:], in_=b0[:, 256:512], func=AF.Exp, scale=-1.0),
                 reads=[b0b], writes=[Fyb])
            yield
            K.op(act, lambda: nc.scalar.activation(out=Fy[:], in_=Fy[:], func=AF.Ln, bias=1.0), writes=[Fyb])
            yield
            K.op(act, lambda: nc.scalar.activation(out=Fy[:], in_=Fy[:], func=AF.Exp, scale=-1.0), writes=[Fyb])
            yield
            if lb_modes[p] == 0:
                K.op(pool, lambda: nc.gpsimd.tensor_scalar(out=Fk[:], in0=Fy[:], scalar1=-1.0, scalar2=1.0,
                                                           op0=ALU.mult, op1=ALU.add), reads=[Fyb], writes=[Fkb])
                yield
                K.op(dve, lambda: nc.vector.tensor_scalar(out=Fy[:], in0=Fy[:], scalar1=TINY, scalar2=None,
                                                          op0=ALU.max), writes=[Fyb])
                yield
            else:
                K.op(pool, lambda: nc.gpsimd.tensor_tensor(out=Fy[:], in0=Fy[:], in1=oml_bc[:], op=ALU.mult),
                     reads=[Llb], writes=[Fyb])
                yield
                K.op(pool, lambda: nc.gpsimd.tensor_tensor(out=Fk[:], in0=oml_bc[:], in1=Fy[:], op=ALU.subtract),
                     reads=[Llb, Fyb], writes=[Fkb])
                yield
                K.op(dve, lambda: nc.vector.scalar_tensor_tensor(out=Fy[:], in0=Fy[:], scalar=TINY, in1=lb_bc[:],
                                                                 op0=ALU.max, op1=ALU.add), reads=[Llb], writes=[Fyb])
                yield
            K.op(act, lambda: nc.scalar.activation(out=Fl[:], in_=Fy[:], func=AF.Ln), reads=[Fyb], writes=[Flb])

            yield
            K.op(act, lambda: nc.scalar.activation(out=Fx[:], in_=b0[:, 0:256], func=AF.Exp, scale=-1.0),
                 reads=[b0b], writes=[Fxb])
            yield
            K.op(act, lambda: nc.scalar.activation(out=Fx[:], in_=Fx[:], func=AF.Ln, bias=1.0), writes=[Fxb])
            yield
            K.op(act, lambda: nc.scalar.activation(out=Fx[:], in_=Fx[:], func=AF.Exp, scale=-1.0), writes=[Fxb])
            yield
            K.op(dve, lambda: nc.vector.tensor_tensor(out=Fq[:], in0=b0[:, 0:256], in1=Fx[:], op=ALU.mult),
                 reads=[b0b, Fxb], writes=[Fqb])
            yield
            K.op(act, lambda: nc.scalar.activation(out=va[:], in_=b1[:, 0:256], func=AF.Copy), reads=[b1b], writes=[vab])
            yield
            silu_gate(b1[:, 256:512], 256, gate_a[:], gab, mul_ap=hgrng[:], psb=b1b)
            yield
            K.op(act, lambda: nc.scalar.activation(out=u_sb[par][:], in_=b2[:, 0:256], func=AF.Copy),
                 reads=[b2b], writes=[ub[par]])
            yield
            silu_gate(b2[:, 256:512], 256, gate_b[:], gbb, mul_ap=pscale[:], psb=b2b)

            yield
            be, beb = G()
            mm(be[:, 0:256], ucum[:], Fl[:], True, True, [*CbL, Flb], [beb])
            yield
            mm(be[:, 256:512], lstr[:], Fl[:], True, True, [*CbL, Flb], [beb])
            yield
            bd, bdb = G()
            for hp in range(2):
                mm(bd[:, hp * 2:hp * 2 + 2], Fl[:, hp * P:(hp + 1) * P], ind[:], True, True, [*CbL, Flb], [bdb])
            yield
            K.op(act, lambda: nc.scalar.activation(out=dec[:], in_=bd[:, 0:4], func=AF.Exp), reads=[bdb], writes=[decb])
            yield
            K.op(act, lambda: nc.scalar.activation(out=Fx[:], in_=be[:, 0:256], func=AF.Exp), reads=[beb], writes=[Fxb])
            yield
            K.op(dve, lambda: nc.vector.tensor_tensor(out=qp[:], in0=Fq[:], in1=Fx[:], op=ALU.mult),
                 reads=[Fqb, Fxb], writes=[qpb])
            yield
            K.op(act, lambda: nc.scalar.activation(out=Fy[:], in_=be[:, 256:512], func=AF.Exp, scale=-1.0),
                 reads=[beb], writes=[Fyb])
            yield
            K.op(dve, lambda: nc.vector.tensor_tensor(out=qpp[:], in0=Fq[:], in1=Fy[:], op=ALU.mult),
                 reads=[Fqb, Fyb], writes=[qppb])
            yield
            K.op(act, lambda: nc.scalar.activation(out=Fx[:], in_=be[:, 256:512], func=AF.Exp), reads=[beb], writes=[Fxb])
            yield
            K.op(dve, lambda: nc.vector.tensor_tensor(out=kp[:], in0=Fk[:], in1=Fx[:], op=ALU.mult),
                 reads=[Fkb, Fxb], writes=[kpb])
            yield
            tt_, ttb = G()
            for n_, (src, srcb) in enumerate(((qp, qpb), (qpp, qppb), (kp, kpb))):
                for hp in range(2):
                    tr(bf(tt_)[:, (2 * n_ + hp) * P:(2 * n_ + hp + 1) * P], src[:, hp * P:(hp + 1) * P], [srcb], [ttb])
            yield
            K.op(act, lambda: nc.scalar.activation(out=TT[:].rearrange("p a b -> p (a b)"), in_=bf(tt_)[:, 0:768], func=AF.Copy),
                 reads=[ttb], writes=[TTb])
            yield
            bp, bpb = G()
            for g in range(4):
                ro = (g % 2) * 64
                o_ap = bp[ro:ro + 64, (g // 2) * P:(g // 2 + 1) * P]
                if i == 0:
                    mm(o_ap, u_sb[par][:, g * 64:(g + 1) * 64], pm_cur[:, g * P:(g + 1) * P], True, True,
                       [ub[par], *CbL], [bpb])
                    mm(o_ap[:, 0:16], u_sb[par][:, g * 64:(g + 1) * 64], pm_cur0[:, g * 16:(g + 1) * 16], True, True,
                       [ub[par], *CbL], [bpb])
                else:
                    mm(o_ap, u_sb[par][:, g * 64:(g + 1) * 64], pm_cur[:, g * P:(g + 1) * P], True, False,
                       [ub[par], *CbL], [bpb])
                    mm(o_ap[:, 0:16], u_sb[1 - par][:, g * 64:(g + 1) * 64], pm_prev[:, g * 16:(g + 1) * 16], False, True,
                       [ub[1 - par], *CbL], [bpb])
            yield
            K.op(act, lambda: nc.scalar.activation(out=qpp[:], in_=bp[:, 0:256], func=AF.Copy),
                 reads=[bpb], writes=[plTb])
            yield
            bo, bob = G()
            for g in (0, 2, 1, 3):
                ro = (g % 2) * 64
                mm(bo[:, 256 + g * 64:256 + (g + 1) * 64], plT[ro:ro + 64, g // 2, :], pw[ro:ro + 64, (g // 2) * 64:(g // 2 + 1) * 64],
                   True, True, [plTb, Lpw], [bob])
            yield
            ba, bab = G()
            for h in (0, 2, 1, 3):
                hp, r = h // 2, (h % 2) * 64
                for c in range(2):
                    cs = slice(c * 64, (c + 1) * 64)
                    mm(ba[cs, h * 64:(h + 1) * 64], TT[r:r + 64, 4 + hp, cs], TT[r:r + 64, 2 + hp, cs], True, True,
                       [TTb], [bab])
            yield
            for c in range(2):
                cs = slice(c * 64, (c + 1) * 64)
                for h in range(4):
                    hp, r = h // 2, (h % 2) * 64
                    col = 256 + (c * 2 + hp) * 64
                    mm(ba[r:r + 64, col:col + 64], kp[cs, h * 64:(h + 1) * 64], va[cs, h * 64:(h + 1) * 64], True, True,
                       [kpb, vab], [bab])
            yield
            K.op(dve, lambda: nc.vector.tensor_tensor(out=attn_sb[:], in0=ba[:, 0:256], in1=hmask[:], op=ALU.mult),
                 reads=[bab, *CbL], writes=[attnb])
            yield
            for c in range(2):
                for hp in range(2):
                    col = 256 + (c * 2 + hp) * 64
                    K.op(dve, lambda hp=hp, col=col, c=c: nc.vector.scalar_tensor_tensor(
                        out=S[:, hp, :], in0=S[:, hp, :], scalar=dec[:, hp * 2 + c:hp * 2 + c + 1], in1=ba[:, col:col + 64],
                        op0=ALU.mult, op1=ALU.add), reads=[bab, decb], writes=[Sb])
                tgt = 1 if c == 0 else 0
                if c == 0:
                    K.op(dve, lambda: nc.vector.tensor_copy(out=S_bf[1][:], in_=S[:]), reads=[Sb], writes=[S_bfb[1]])
            yield
            for c in range(2):
                cs = slice(c * 64, (c + 1) * 64)
                for h in range(4):
                    mm(bo[cs, h * 64:(h + 1) * 64], attn_sb[cs, h * 64:(h + 1) * 64], va[cs, h * 64:(h + 1) * 64],
                       True, True, [attnb, vab], [bob])
            yield
            bi, bib = G()
            for h in (0, 2, 1, 3):
                hp, r = h // 2, (h % 2) * 64
                for c in range(2):
                    cs = slice(c * 64, (c + 1) * 64)
                    mm(bi[cs, h * 64:(h + 1) * 64], TT[r:r + 64, hp, cs], S_bf[c][r:r + 64, hp, :],
                       True, True, [TTb, S_bfb[c]], [bib])
            yield
            K.op(dve, lambda: nc.vector.tensor_copy(out=S_bf[0][:], in_=S[:]), reads=[Sb], writes=[S_bfb[0]])
            yield
            hgb = smb["hg"]
            K.op(act, lambda: nc.scalar.activation(out=Fx[:], in_=bo[:, 0:256], func=AF.Copy), reads=[bob], writes=[Fxb])
            yield
            K.op(dve, lambda: nc.vector.tensor_tensor(out=Fx[:], in0=Fx[:], in1=bi[:, 0:256], op=ALU.add),
                 reads=[bib], writes=[Fxb])
            yield
            K.op(dve, lambda: nc.vector.tensor_tensor(out=Fy[:], in0=Fx[:], in1=Fx[:], op=ALU.mult), reads=[Fxb], writes=[Fyb])
            yield
            K.op(dve, lambda: nc.vector.reduce_sum(out=small[:, 4:8], in_=Fy[:].rearrange("p (h d) -> p h d", d=64), axis=AX.X),
                 reads=[Fyb], writes=[hgb])
            yield
            rms_rstd(small[:, 4:8], 64.0, small[:, 8:12], hgb)
            yield
            for h in range(4):
                K.op(dve, lambda h=h: nc.vector.scalar_tensor_tensor(
                    out=XYf[1 - par][:, h * 64:(h + 1) * 64], in0=Fx[:, h * 64:(h + 1) * 64], scalar=small[:, 8 + h:9 + h],
                    in1=gate_a[:, h * 64:(h + 1) * 64], op0=ALU.mult, op1=ALU.mult), reads=[Fxb, hgb, gab], writes=[XYb[1 - par]])
            yield
            K.op(dve, lambda: nc.vector.tensor_tensor(out=XYf[1 - par][:, 256:512], in0=bo[:, 256:512], in1=gate_b[:], op=ALU.mult),
                 reads=[bob, gbb], writes=[XYb[1 - par]])


            yield

        groups = []
        for h in range(8):
            for j0 in range(0, i + 1, 4):
                groups.append([(h, j) for j in range(j0, min(j0 + 4, i + 1))])
        rdb = smb["rden"]

        def emit_S(gi):
            stt, stb = ST[gi % NST]
            ptb = PTb[gi % NPT]
            pt = PT[gi % NPT]
            for sl, (h, j) in enumerate(groups[gi]):
                hp, r = h // 2, (h % 2) * 64
                mm(stt[:, sl * P:(sl + 1) * P], KT[:, hp, j * P:(j + 1) * P], (QTa if r == 0 else QTz)[:, hp, :], True, True,
                   [KTb[j], QTb], [stb])
            ng = len(groups[gi])
            K.op(act, lambda: nc.scalar.activation(out=pt[:, 0:ng * P], in_=stt[:, 0:ng * P], func=AF.Exp, scale=0.125),
                 reads=[stb], writes=[ptb])
            gh, gj0 = groups[gi][0]
            vt, vtb = Vt[gi % NVT], Vtb[gi % NVT]
            K.op(dve, lambda: nc.vector.tensor_tensor(
                out=vt[:, 0:ng * 65].rearrange("p (a b) -> p a b", b=65), in0=V[:, gj0:gj0 + ng, gh * 65:(gh + 1) * 65],
                in1=fij[par][:, gj0:gj0 + ng, gh:gh + 1].to_broadcast([P, ng, 65]), op=ALU.mult),
                reads=[biasb[par]] + [Vb[j] for j in range(gj0, gj0 + ng)], writes=[vtb])
            for sl, (h, j) in enumerate(groups[gi]):
                if j == i:
                    K.op(dve, lambda sl=sl: nc.vector.tensor_tensor(
                        out=pt[:, sl * P:(sl + 1) * P], in0=pt[:, sl * P:(sl + 1) * P], in1=fmask[:], op=ALU.mult),
                        reads=[*CbL], writes=[ptb])

        def emit_PV(gi):
            pt, ptb = PT[gi % NPT], PTb[gi % NPT]
            for sl, (h, j) in enumerate(groups[gi]):
                oa, oab = OA[h // 4]
                c0 = (h % 4) * 65
                mm(oa[:, c0:c0 + 65], pt[:, sl * P:(sl + 1) * P], Vt[gi % NVT][:, sl * 65:(sl + 1) * 65], j == 0, j == i,
                   [ptb, Vtb[gi % NVT]], [oab], inc=True)
                if j == i and h % 4 == 3:
                    hb = h - 3
                    K.op(dve, lambda oa=oa: nc.vector.reciprocal(
                        out=small[:, 44:48], in_=oa[:, 0:260].rearrange("p (h e) -> p h e", e=65)[:, :, 64]),
                        reads=[oab], writes=[rdb])
                    for hh in range(4):
                        K.op(dve, lambda hh=hh, oa=oa, hb=hb: nc.vector.scalar_tensor_tensor(
                            out=XYf[1 - par][:, 512 + (hb + hh) * 64:512 + (hb + hh + 1) * 64], in0=oa[:, hh * 65:hh * 65 + 64],
                            scalar=small[:, 44 + hh:45 + hh], in1=gate_c[:, (hb + hh) * 64:(hb + hh + 1) * 64],
                            op0=ALU.mult, op1=ALU.mult), reads=[oab, rdb, gcb], writes=[XYb[1 - par]])

        gen = ab_thread()
        steps_per_group = max(1, -(-12 // len(groups)))
        if next_front is not None:
            if next_front[2] != p:
                load_cast(preg[:], ld[next_front[2]]["preg_bc"][:, :], Lpre)
            front(*next_front)
        gstate[1] = [0, 1, 2]
        emit_S(0)
        if len(groups) > 1:
            emit_S(1)
        for gi in range(len(groups)):
            if gi + 2 < len(groups):
                emit_S(gi + 2)
            emit_PV(gi)
            for _ in range(steps_per_group):
                next(gen, None)
        for _ in gen:
            pass
        gstate[1] = [0, 1, 2, 3, 4, 5]

        tm, tmb = G()
        for ec in range(8):
            tr(bf(tm)[:, ec * P:(ec + 1) * P], XYf[1 - par][:, ec * P:(ec + 1) * P], [XYb[1 - par]], [tmb])
        K.op(act, lambda: nc.scalar.activation(out=XYf[par], in_=bf(tm)[:, 0:D], func=AF.Copy),
             reads=[tmb], writes=[XYb[par]])
        if next_front is not None:
            front_b(1 - par)
        ys = []
        for half in range(2):
            y_, yb_ = G()
            for ec in range(8):
                mm(y_[:, 0:512], XYc[par][:, ec, :], Wout[:, ec, half * 512:(half + 1) * 512], ec == 0, ec == 7,
                   [XYb[par], Wob], [yb_])
            ys.append((y_, yb_))
        if prefetch is not None:
            load_wout(prefetch)
        n2 = smb["n2"]
        for half in range(2):
            y_, yb_ = ys[half]
            K.op(act, lambda y_=y_, half=half: nc.scalar.activation(
                out=hh[:, half * 512:(half + 1) * 512], in_=y_[:, 0:512], func=AF.Square,
                accum_out=small[:, 48 + half:49 + half]), reads=[yb_], writes=[hhb, n2])
        K.op(dve, lambda: nc.vector.tensor_tensor(out=small[:, 50:51], in0=small[:, 48:49], in1=small[:, 49:50], op=ALU.add),
             writes=[n2])
        rms_rstd(small[:, 50:51], float(D), small[:, 51:52], n2)
        for q4 in range(4):
            y_, yb_ = ys[q4 // 2]
            c0 = (q4 % 2) * 256
            K.op(dve, lambda y_=y_, q4=q4, c0=c0: nc.vector.scalar_tensor_tensor(
                out=Fs[q4][:], in0=y_[:, c0:c0 + 256], scalar=small[:, 51:52], in1=postg[:, q4 * 256:(q4 + 1) * 256],
                op0=ALU.mult, op1=ALU.mult), reads=[yb_, n2, Lpost], writes=[Fb[q4]])
            K.op(pool, lambda q4=q4: nc.gpsimd.tensor_tensor(
                out=x_seq[:, i, q4 * 256:(q4 + 1) * 256], in0=x_seq[:, i, q4 * 256:(q4 + 1) * 256],
                in1=Fs[q4][:], op=ALU.add), reads=[Fb[q4]], writes=[Xb[i]])
        finish()

    if 1 in lb_modes:
        compute_lb1(list(lb_modes).index(1))

    x_preloaded = set()
    for s in range(NSEQ):
        for i in range(NB):
            if (s, i) not in x_preloaded:
                load_f32(x_seq[:, i, :], x_d[s, i * P:(i + 1) * P, :], Xb[i])
        for p in range(NL):
            first = (s == 0 and p == 0)
            load_layer(p, with_weights=first, with_preg=first)
            nxt = None
            if p + 1 < NL:
                nxt = p + 1
            elif s + 1 < NSEQ:
                nxt = 0
            K.op(pool, lambda: nc.gpsimd.memset(S[:], 0.0), writes=[Sb])
            K.op(pool, lambda: nc.gpsimd.memset(S_bf[0][:], 0.0), writes=[S_bfb[0]])
            K.op(pool, lambda: nc.gpsimd.memset(small[:, 28:36], 0.0), writes=[smb["acc"]])
            for i in range(NB):
                if i == NB - 1 and p == NL - 1 and s + 1 < NSEQ:
                    for i2 in range(NB - 1):
                        load_f32(x_seq[:, i2, :], x_d[s + 1, i2 * P:(i2 + 1) * P, :], Xb[i2])
                        x_preloaded.add((s + 1, i2))
                bg = (s == 0 and p + 1 < NL and NB >= 2)
                if bg and i == NB - 1:
                    wsc_ready[p + 1] = True
                if i + 1 < NB:
                    nf = (s, i + 1, p)
                elif nxt is not None:
                    nf = (s, 0, p + 1) if p + 1 < NL else (s + 1, 0, 0)
                else:
                    nf = None
                block(s, i, p, p == NL - 1, do_front=(first and i == 0), next_front=nf,
                      prefetch=(nxt if i == NB - 1 else None))
                if bg and i < NB - 1:
                    q = p + 1
                    per = -(-16 // (NB - 1))
                    for pc in range(i * per, min(16, (i + 1) * per)):
                        if pc < 8:
                            K.dma(K.q_pool, lambda pc=pc, q=q: nc.gpsimd.dma_start(
                                out=wsc_in[q][:, pc, :], in_=ld[q]["w_in"][pc * P:(pc + 1) * P, :]), writes=[wsc_ib[q][pc]])
                        else:
                            K.dma(K.q_pool, lambda pc=pc, q=q: nc.gpsimd.dma_start(
                                out=wsc_out[q][:, pc - 8, :], in_=ld[q]["w_out"][(pc - 8) * P:(pc - 7) * P, :]),
                                writes=[wsc_ob[q][pc - 8]])
    sp.wait_all([(l[0], l[1]) for l in K.q_sp.lanes if l[1] > 0])
    return nc


def _layer_inputs(l, lower_bounds, pre_norm_g, w_in, hgrn_norm_g, fox_f_bias, pool_w, pool_scale, w_out, post_norm_g):
    f = np.float32
    bc = lambda v: np.ascontiguousarray(np.broadcast_to(np.asarray(v, f)[None, :], (P, v.shape[0])))
    pwl = np.zeros((P, 128), f)
    for g in range(4):
        pwl[(g % 2) * 64:(g % 2) * 64 + 64, (g // 2) * 64:(g // 2 + 1) * 64] = pool_w[l, g]
    return {
        "w_in": np.ascontiguousarray(w_in[l], f), "w_out": np.ascontiguousarray(w_out[l], f),
        "preg_bc": bc(pre_norm_g[l]), "postg_bc": bc(post_norm_g[l]),
        "hgrng_bc": bc(hgrn_norm_g[l]), "pscale_bc": bc(pool_scale[l]),
        "lbraw_bc": bc(np.concatenate([lower_bounds[0], lower_bounds[1]])),
        "fbias_bc": bc(fox_f_bias[l]), "pw": pwl,
    }


FUSED = True
_cache = {}


def run(x, params, T, NSEQ, ncores, layer_groups):
    consts = _consts()
    cur = np.ascontiguousarray(x, np.float32)
    for grp in layer_groups:
        key = (T, NSEQ, tuple(grp))
        if key not in _cache:
            _cache[key] = build_program(T, NSEQ, [0 if l == 0 else 1 for l in grp])
        nc = _cache[key]
        base = {"c_" + k: v for k, v in consts.items()}
        for p, l in enumerate(grp):
            for k, v in _layer_inputs(l, **params).items():
                base["l%d_%s" % (p, k)] = v
        in_maps = []
        for c in range(ncores):
            m = dict(base)
            m["x"] = np.ascontiguousarray(cur[c * NSEQ:(c + 1) * NSEQ])
            in_maps.append(m)
        res = run_bass_kernel_spmd(nc, in_maps, core_ids=list(range(ncores)))
        cur = np.concatenate([np.asarray(r["out"], np.float32) for r in res.results], axis=0)
    return cur


def kernel(x, lower_bounds, pre_norm_g, w_in, hgrn_norm_g, fox_f_bias, pool_w, pool_scale, w_out, post_norm_g):
    params = dict(lower_bounds=np.asarray(lower_bounds), pre_norm_g=np.asarray(pre_norm_g), w_in=np.asarray(w_in),
                  hgrn_norm_g=np.asarray(hgrn_norm_g), fox_f_bias=np.asarray(fox_f_bias), pool_w=np.asarray(pool_w),
                  pool_scale=np.asarray(pool_scale), w_out=np.asarray(w_out), post_norm_g=np.asarray(post_norm_g))
    x = np.asarray(x)
    B, T, _ = x.shape
    groups = [[0, 1]] if FUSED else [[0], [1]]
    return run(x, params, T, B // 8, 8, groups).astype(np.float32)
```

```python
import numpy as np
import concourse.bass as bass
import concourse.mybir as mybir
from concourse.bass_utils import run_bass_kernel_spmd

F32 = mybir.dt.float32
BF16 = mybir.dt.bfloat16
AF = mybir.ActivationFunctionType
ALU = mybir.AluOpType
AX = mybir.AxisListType

D = 1024
INW = 3592
EPS = 1e-6
TINY = 1e-30
P = 128
POOL_WINDOWS = (2, 4, 8, 16)
EPOCH = 24000


CLOCKS = {}


class Buf:
    __slots__ = ("w", "r", "excl")

    def __init__(self, excl=False):
        self.w = None
        self.r = {}
        self.excl = excl


class Eng:
    def __init__(self, nc, eng, name):
        self.nc, self.eng, self.name = nc, eng, name
        self.sem = nc.alloc_semaphore(name + "_s0")
        self.own = {id(self.sem)}
        self.n = 0
        self.ep = 0
        self.seen = {}
        self.pending = 0

    def wait_all(self, deps):
        deps = [d for d in deps if self.seen.get(id(d[0]), 0) < d[1]]
        if len(deps) > 1:
            keep = []
            for d in deps:
                k = id(d[0])
                implied = False
                for d2 in deps:
                    if d2 is not d:
                        c2 = CLOCKS.get((id(d2[0]), d2[1]))
                        if c2 is not None and c2.get(k, 0) >= d[1]:
                            implied = True
                            break
                if not implied:
                    keep.append(d)
            deps = keep
        for sem, val in deps:
            key = id(sem)
            if self.seen.get(key, 0) < val:
                self.eng.wait_ge(sem, val)
                self.seen[key] = val
            c = CLOCKS.get((key, val))
            if c is not None:
                seen = self.seen
                for k2, v2 in c.items():
                    if seen.get(k2, 0) < v2:
                        seen[k2] = v2

    def snapshot(self, tok):
        c = dict(self.seen)
        c[id(tok[0])] = max(c.get(id(tok[0]), 0), tok[1])
        CLOCKS[(id(tok[0]), tok[1])] = c

    def issue(self, ins):
        if self.n >= EPOCH and self.pending == 0:
            self.ep += 1
            self.sem = self.nc.alloc_semaphore("%s_s%d" % (self.name, self.ep))
            self.own.add(id(self.sem))
            self.n = 0
        self.n += 1
        ins.then_inc(self.sem, 1)
        self.pending = 0
        return (self.sem, self.n)

    def issue_noinc(self, ins):
        if self.n >= EPOCH:
            pass
        self.pending += 1
        return (self.sem, self.n + 1)


class DmaQ:
    def __init__(self, nc, E, name, nlanes):
        self.E = E
        self.lanes = [[nc.alloc_semaphore("%s_l%d" % (name, i)), 0] for i in range(nlanes)]
        self.k = 0

    def issue(self, fn, deps):
        lane = self.lanes[self.k]
        self.k = (self.k + 1) % len(self.lanes)
        d = set(deps)
        if lane[1] > 0:
            d.add((lane[0], lane[1]))
        self.E.wait_all(d)
        lane[1] += 16
        fn().then_inc(lane[0], 16)
        tok = (lane[0], lane[1])
        self.E.snapshot(tok)
        return tok


class Ctx:
    def __init__(self, nc):
        self.nc = nc
        self.pe = Eng(nc, nc.tensor, "pe")
        self.act = Eng(nc, nc.scalar, "act")
        self.dve = Eng(nc, nc.vector, "dve")
        self.pool = Eng(nc, nc.gpsimd, "pool")
        self.sp = Eng(nc, nc.sync, "sp")
        self.q_sp = DmaQ(nc, self.sp, "qsp", 8)
        self.q_pool = DmaQ(nc, self.pool, "qpl", 16)

    def _deps(self, reads, writes, E=None):
        deps = set()
        own = E.own if E is not None else ()
        is_pe = E is self.pe
        for b in reads:
            if b.w is not None and not (is_pe and id(b.w[0]) in own):
                deps.add(b.w)
            if b.excl:
                for tok in b.r.values():
                    if id(tok[0]) not in own:
                        deps.add(tok)
        for b in writes:
            if b.w is not None and not (is_pe and id(b.w[0]) in own):
                deps.add(b.w)
            for tok in b.r.values():
                if not (is_pe and id(tok[0]) in own):
                    deps.add(tok)
        return deps

    @staticmethod
    def _commit(tok, reads, writes):
        for b in reads:
            b.r[id(tok[0])] = tok
        for b in writes:
            b.w = tok
            b.r = {}

    def op(self, E, fn, reads=(), writes=(), inc=True, mode=None):
        if mode is not None and mode != getattr(self, "pe_mode", None):
            if E.n > 0:
                assert E.pending == 0
                E.eng.wait_ge(E.sem, E.n)
                E.seen[id(E.sem)] = E.n
            self.pe_mode = mode
        E.wait_all(self._deps(reads, writes, E))
        tok = E.issue(fn()) if inc else E.issue_noinc(fn())
        if inc:
            E.snapshot(tok)
        elif (id(tok[0]), tok[1]) not in CLOCKS:
            E.snapshot(tok)
        self._commit(tok, reads, writes)
        return tok

    def dma(self, Q, fn, reads=(), writes=()):
        tok = Q.issue(fn, self._deps(reads, writes))
        self._commit(tok, reads, writes)
        return tok


def _consts():
    s = np.arange(P)[:, None]
    t = np.arange(P)[None, :]
    same = (s // 64) == (t // 64)
    c = {}
    c["ident"] = np.eye(P, dtype=np.float32)
    c["ucum"] = (same & (s <= t)).astype(np.float32)
    c["lstr"] = (same & (s > t)).astype(np.float32)
    c["ufull"] = (s <= t).astype(np.float32)
    c["ones"] = np.ones((P, P), np.float32)
    ind = np.zeros((P, 2), np.float32)
    ind[:64, 0] = 1
    ind[64:, 1] = 1
    c["ind"] = ind
    c["fmask"] = (s <= t).astype(np.float32)
    hm = np.zeros((P, 256), np.float32)
    for h in range(4):
        hm[:, h * 64:(h + 1) * 64] = ((np.arange(P)[:, None] % 64) <= np.arange(64)[None, :])
    c["hmask"] = hm
    pm_cur = np.zeros((P, 4, P), np.float32)
    pm_prev = np.zeros((P, 4, 16), np.float32)
    pm_cur0 = np.zeros((P, 4, 16), np.float32)
    for g, w in enumerate(POOL_WINDOWS):
        for tt in range(P):
            for ss in range(tt - w + 1, tt + 1):
                if ss >= 0:
                    pm_cur[ss, g, tt] += 1.0 / w
                else:
                    if tt < 16:
                        pm_prev[ss + P, g, tt] += 1.0 / w
            pm_cur[tt, g, tt] -= 1.0
        for tt in range(16):
            cnt = min(tt + 1, w)
            for ss in range(max(0, tt - w + 1), tt + 1):
                pm_cur0[ss, g, tt] += 1.0 / cnt
            pm_cur0[tt, g, tt] -= 1.0
    c["pm_cur"] = pm_cur.reshape(P, 4 * P)
    c["pm_prev"] = pm_prev.reshape(P, 64)
    c["pm_cur0"] = pm_cur0.reshape(P, 64)
    return c


CONST_SHAPES = {"ident": (P, P), "ucum": (P, P), "lstr": (P, P), "ufull": (P, P), "ones": (P, P),
                "ind": (P, 2), "fmask": (P, P), "hmask": (P, 256), "pm_cur": (P, 512),
                "pm_prev": (P, 64), "pm_cur0": (P, 64)}
LAYER_SHAPES = {"w_in": (D, INW), "w_out": (D, D), "preg_bc": (P, D), "postg_bc": (P, D),
                "hgrng_bc": (P, 256), "pscale_bc": (P, 256), "lbraw_bc": (P, 512),
                "fbias_bc": (P, 8), "pw": (P, 128)}


def build_program(T, NSEQ, lb_modes, debug_out=False):
    NL = len(lb_modes)
    NB = T // P
    nc = bass.Bass("TRN2", target_bir_lowering=False)
    CLOCKS.clear()
    K = Ctx(nc)
    pe, act, dve, pool, sp = K.pe, K.act, K.dve, K.pool, K.sp

    x_d = nc.dram_tensor("x", [NSEQ, T, D], F32, kind="ExternalInput").ap()
    out_d = nc.dram_tensor("out", [NSEQ, T, D], F32, kind="ExternalOutput").ap()
    cd = {k: nc.dram_tensor("c_" + k, list(v), F32, kind="ExternalInput").ap() for k, v in CONST_SHAPES.items()}
    ld = [{k: nc.dram_tensor("l%d_%s" % (p, k), list(v), F32, kind="ExternalInput").ap()
           for k, v in LAYER_SHAPES.items()} for p in range(NL)]

    wsc_in = [nc.dram_tensor("wsc_in%d" % p, [P, 8, INW], BF16, kind="Internal").ap() for p in range(NL)]
    wsc_out = [nc.dram_tensor("wsc_out%d" % p, [P, 8, D], BF16, kind="Internal").ap() for p in range(NL)]
    wsc_ib = [[Buf() for _ in range(8)] for _ in range(NL)]
    wsc_ob = [[Buf() for _ in range(8)] for _ in range(NL)]
    wsc_ready = [False] * NL

    def sb(name, shape, dt):
        return nc.alloc_sbuf_tensor(name, list(shape), dt)

    x_seq = sb("x_seq", [P, NB, D], F32)
    Xb = [Buf() for _ in range(NB)]
    Win = sb("Win", [P, 8, INW], BF16)
    Wout = sb("Wout", [P, 8, D], BF16)
    Wib = Buf()
    Wob = Buf()
    KT = sb("KT", [P, 4, T], BF16)
    KTb = [Buf() for _ in range(NB)]
    V = sb("V", [P, NB, 8 * 65], BF16)
    Vb = [Buf() for _ in range(NB)]
    Vones = Buf()
    ident = sb("ident", [P, P], BF16)
    ucum = sb("ucum", [P, P], F32)
    lstr = sb("lstr", [P, P], F32)
    ufull = sb("ufull", [P, P], F32)
    ones = sb("ones", [P, P], F32)
    ind = sb("ind", [P, 2], F32)
    fmask = sb("fmask", [P, P], BF16)
    hmask = sb("hmask", [P, 256], BF16)
    pm_cur = sb("pm_cur", [P, 512], BF16)
    pm_prev = sb("pm_prev", [P, 64], BF16)
    pm_cur0 = sb("pm_cur0", [P, 64], BF16)
    CbL = [Buf() for _ in range(len(CONST_SHAPES))]
    preg = sb("preg", [P, D], BF16)
    postg = sb("postg", [P, D], BF16)
    hgrng = sb("hgrng", [P, 256], BF16)
    pscale = sb("pscale", [P, 256], BF16)
    lb_bc = sb("lb_bc", [P, 256], F32)
    oml_bc = sb("oml_bc", [P, 256], F32)
    lbraw = None
    fbias = sb("fbias", [P, 8], F32)
    pw = sb("pw", [P, 128], BF16)
    Lpre, Lpost, Lpw, Lhg, Lps, Lfb, Llb = (Buf() for _ in range(7))
    hm = sb("hm", [P, D], BF16)
    hmb = Buf()
    hh = sb("hh", [P, D], BF16)
    hhb = Buf()
    hT = sb("hT", [P, 8, P], BF16)
    hTb = Buf()
    XYf = [hT[:].rearrange("p a b -> p (a b)"), hm[:]]
    XYc = [hT[:], hm[:].rearrange("p (a b) -> p a b", b=P)]
    XYb = [hTb, hmb]
    NF = 5
    Fs = [sb("F%d" % i, [P, 256], F32) for i in range(NF)]
    Fb = [Buf() for _ in range(NF)]
    ebig = sb("ebig", [P, 256], F32)
    ebb = Buf()
    gate_a = sb("gate_a", [P, 256], BF16)
    gate_b = sb("gate_b", [P, 256], BF16)
    gate_c = sb("gate_c", [P, 512], BF16)
    gab, gbb, gcb = Buf(), Buf(), Buf()
    qp = sb("qp", [P, 256], BF16)
    qpp = sb("qpp", [P, 256], BF16)
    kp = sb("kp", [P, 256], BF16)
    va = sb("va", [P, 256], BF16)
    attn_sb = qp
    qpb, qppb, kpb, vab = Buf(), Buf(), Buf(), Buf()
    attnb = qpb
    TT = sb("TT", [P, 6, P], BF16)
    TTb = Buf()
    S = sb("S", [P, 2, 64], F32)
    Sb = Buf()
    S_bf = [sb("S_bf%d" % i, [P, 2, 64], BF16) for i in range(2)]
    S_bfb = [Buf(), Buf()]
    dec = sb("dec", [P, 4], F32)
    decb = Buf()
    u_sb = [sb("u_sb%d" % i, [P, 256], BF16) for i in range(2)]
    ub = [Buf(), Buf()]
    plT = qpp[:].rearrange("p (a b) -> p a b", b=P)
    plTb = qppb
    q_sb = Fs[3][:].bitcast(BF16)
    k_sb = Fs[4][:].bitcast(BF16)
    qsb_b, ksb_b = Fb[3], Fb[4]
    QTa = sb("QTa", [P, 4, P], BF16)
    QTz = sb("QTz", [P, 4, P], BF16)
    QTb = Buf()
    NPT = 3
    NVT = 3
    Vt = [sb("Vt%d" % i, [P, 4 * 65], BF16) for i in range(NVT)]
    Vtb = [Buf() for _ in range(NVT)]
    PT = [sb("PT%d" % i, [P, 512], BF16) for i in range(NPT)]
    PTb = [Buf() for _ in range(NPT)]
    fij = [sb("fij%d" % i, [P, NB, 8], F32) for i in range(2)]
    biasb = [Buf(), Buf()]
    totall = sb("totall", [P, NB, 8], F32)
    small = sb("small", [P, 64], F32)
    smb = {k: Buf() for k in ("n1", "hg", "fc", "acc", "tot", "rden", "n2", "w")}

    banks = [nc.alloc_psum_tensor("bank%d" % i, [P, 512], F32) for i in range(8)]
    bankb = [Buf(excl=True) for _ in range(8)]
    gstate = [0, [0, 1, 2, 3, 4, 5]]

    def G():
        lst = gstate[1]
        i = lst[gstate[0] % len(lst)]
        gstate[0] += 1
        return banks[i], bankb[i]

    NST = 3
    ST = [(banks[3], bankb[3]), (banks[4], bankb[4]), (banks[5], bankb[5])]
    OA = [(banks[6], bankb[6]), (banks[7], bankb[7])]

    def bf(t):
        return t[:].bitcast(BF16)

    def load_cast(dst_ap, src_ap, buf):
        K.dma(K.q_pool, lambda: nc.gpsimd.dma_start(out=dst_ap, in_=src_ap), writes=[buf])

    def load_f32(dst_ap, src_ap, buf):
        K.dma(K.q_sp, lambda: nc.sync.dma_start(out=dst_ap, in_=src_ap), writes=[buf])

    _cseen = []
    for name, t_, cast in (("ident", ident, 1), ("ucum", ucum, 0), ("lstr", lstr, 0), ("ufull", ufull, 0),
                           ("ones", ones, 0), ("ind", ind, 0), ("fmask", fmask, 1), ("hmask", hmask, 1),
                           ("pm_cur", pm_cur, 1), ("pm_prev", pm_prev, 1), ("pm_cur0", pm_cur0, 1)):
        (load_cast if cast else load_f32)(t_[:], cd[name][:, :], CbL[len(_cseen)])
        _cseen.append(name)
    Vv = V[:].rearrange("p b (h e) -> p b h e", e=65)
    K.op(pool, lambda: nc.gpsimd.memset(QTa[:], 0.0), writes=[QTb])
    K.op(pool, lambda: nc.gpsimd.memset(QTz[:], 0.0), writes=[QTb])

    def load_win(p):
        if wsc_ready[p]:
            for dc in range(8):
                K.dma(K.q_sp, lambda dc=dc: nc.sync.dma_start(out=Win[:, dc, :], in_=wsc_in[p][:, dc, :]),
                      reads=[wsc_ib[p][dc]], writes=[Wib])
        else:
            for dc in range(8):
                load_cast(Win[:, dc, :], ld[p]["w_in"][dc * P:(dc + 1) * P, :], Wib)
            for dc in range(8):
                K.dma(K.q_sp, lambda dc=dc: nc.sync.dma_start(out=wsc_in[p][:, dc, :], in_=Win[:, dc, :]),
                      reads=[Wib], writes=[wsc_ib[p][dc]])

    def load_wout(p):
        if wsc_ready[p]:
            for ec in range(8):
                K.dma(K.q_sp, lambda ec=ec: nc.sync.dma_start(out=Wout[:, ec, :], in_=wsc_out[p][:, ec, :]),
                      reads=[wsc_ob[p][ec]], writes=[Wob])
        else:
            for ec in range(8):
                load_cast(Wout[:, ec, :], ld[p]["w_out"][ec * P:(ec + 1) * P, :], Wob)
            for ec in range(8):
                K.dma(K.q_sp, lambda ec=ec: nc.sync.dma_start(out=wsc_out[p][:, ec, :], in_=Wout[:, ec, :]),
                      reads=[Wob], writes=[wsc_ob[p][ec]])
            wsc_ready[p] = True

    def load_layer(p, with_weights=True, with_preg=True):
        L = ld[p]
        if with_weights:
            load_win(p)
            load_wout(p)
        if with_preg:
            load_cast(preg[:], L["preg_bc"][:, :], Lpre)
        load_cast(postg[:], L["postg_bc"][:, :], Lpost)
        load_cast(pw[:], L["pw"][:, :], Lpw)
        load_cast(hgrng[:], L["hgrng_bc"][:, :], Lhg)
        load_cast(pscale[:], L["pscale_bc"][:, :], Lps)
        load_f32(fbias[:], L["fbias_bc"][:, :], Lfb)

    def compute_lb1(p):
        L = ld[p]
        load_f32(Fs[1][:], L["lbraw_bc"][:, 0:256], Fb[1])
        load_f32(Fs[2][:], L["lbraw_bc"][:, 256:512], Fb[2])
        K.op(pool, lambda: nc.gpsimd.tensor_tensor(out=Fs[0][:], in0=Fs[1][:], in1=Fs[2][:],
                                                   op=ALU.subtract), reads=[Fb[1], Fb[2]], writes=[Fb[0]])
        K.op(act, lambda: nc.scalar.activation(out=Fs[0][:], in_=Fs[0][:], func=AF.Exp),
             reads=[], writes=[Fb[0]])
        K.op(pool, lambda: nc.gpsimd.tensor_scalar(out=Fs[0][:], in0=Fs[0][:], scalar1=1.0, scalar2=1.0,
                                                   op0=ALU.mult, op1=ALU.add), writes=[Fb[0]])
        K.op(dve, lambda: nc.vector.reciprocal(out=lb_bc[:], in_=Fs[0][:]), reads=[Fb[0]], writes=[Llb])
        K.op(pool, lambda: nc.gpsimd.tensor_scalar(out=oml_bc[:], in0=lb_bc[:], scalar1=-1.0, scalar2=1.0,
                                                   op0=ALU.mult, op1=ALU.add), writes=[Llb])

    def silu_gate(ps_ap, width, out_ap, outb, mul_ap=None, psb=None):
        e = ebig[:, 0:width]
        K.op(act, lambda: nc.scalar.activation(out=e, in_=ps_ap, func=AF.Exp, scale=-1.0),
             reads=[psb], writes=[ebb])
        K.op(act, lambda: nc.scalar.activation(out=e, in_=e, func=AF.Ln, bias=1.0), writes=[ebb])
        K.op(act, lambda: nc.scalar.activation(out=e, in_=e, func=AF.Exp, scale=-1.0), writes=[ebb])
        if mul_ap is None:
            K.op(dve, lambda: nc.vector.tensor_tensor(out=out_ap, in0=ps_ap, in1=e, op=ALU.mult),
                 reads=[psb, ebb], writes=[outb])
        else:
            K.op(dve, lambda: nc.vector.tensor_tensor(out=e, in0=ps_ap, in1=e, op=ALU.mult),
                 reads=[psb], writes=[ebb])
            K.op(pool, lambda: nc.gpsimd.tensor_tensor(out=out_ap, in0=e, in1=mul_ap, op=ALU.mult),
                 reads=[ebb, Lhg, Lps], writes=[outb])

    def rms_rstd(ss_ap, n, out_ap, b):
        K.op(act, lambda: nc.scalar.activation(out=out_ap, in_=ss_ap, func=AF.Ln, scale=1.0 / n, bias=EPS), writes=[b])
        K.op(act, lambda: nc.scalar.activation(out=out_ap, in_=out_ap, func=AF.Exp, scale=-0.5), writes=[b])

    def _cls(n):
        return 32 if n <= 32 else (64 if n <= 64 else 128)

    def mm(out, lhsT, rhs, start, stop, reads, writes, inc=None):
        kc = _cls(lhsT.shape[0])
        mode = ("mm", str(lhsT.dtype), kc, _cls(lhsT.shape[-1]), lhsT.base_partition() if kc < 128 else 0)
        return K.op(pe, lambda: nc.tensor.matmul(out, lhsT, rhs, start=start, stop=stop), reads=reads, writes=writes,
                    inc=(bool(stop) or kc < 128) if inc is None else inc, mode=mode)

    def tr(out, in_, reads, writes):
        return K.op(pe, lambda: nc.tensor.transpose(out, in_, ident[:]), reads=list(reads) + [*CbL], writes=writes,
                    mode=("tr",))

    def front(s, i, p):
        xb = x_seq[:, i, :]
        n1 = smb["n1"]
        K.op(act, lambda: nc.scalar.activation(out=hh[:], in_=xb, func=AF.Square, accum_out=small[:, 0:1]),
             reads=[Xb[i]], writes=[hhb, n1])
        rms_rstd(small[:, 0:1], float(D), small[:, 2:3], n1)
        K.op(dve, lambda: nc.vector.scalar_tensor_tensor(out=hh[:], in0=xb, scalar=small[:, 2:3], in1=preg[:],
                                                         op0=ALU.mult, op1=ALU.mult),
             reads=[Xb[i], n1, Lpre], writes=[hhb])

    def front_b(tp):
        tb, tbb = G()
        for dc in range(8):
            tr(bf(tb)[:, dc * P:(dc + 1) * P], hh[:, dc * P:(dc + 1) * P], [hhb], [tbb])
        K.op(act, lambda: nc.scalar.activation(out=XYf[tp], in_=bf(tb)[:, 0:D], func=AF.Copy),
             reads=[tbb], writes=[XYb[tp]])


    def block(s, i, p, last_layer, do_front=True, next_front=None, prefetch=None):
        xb = x_seq[:, i, :]
        par = i % 2

        def finish():
            if last_layer:
                K.dma(K.q_sp, lambda: nc.sync.dma_start(out=out_d[s, i * P:(i + 1) * P, :], in_=x_seq[:, i, :]),
                      reads=[Xb[i]])
        if do_front:
            front(s, i, p)
            front_b(i % 2)
        def proj(col0, width):
            b_, bb_ = G()
            for dc in range(8):
                mm(b_[:, 0:width], XYc[par][:, dc, :], Win[:, dc, col0:col0 + width], dc == 0, dc == 7,
                   [XYb[par], Wib], [bb_])
            return b_, bb_

        b7, b7b = proj(3584, 8)
        fcb = smb["fc"]
        K.op(dve, lambda: nc.vector.tensor_tensor(out=small[:, 12:20], in0=b7[:, 0:8], in1=fbias[:], op=ALU.add),
             reads=[b7b, Lfb], writes=[fcb])
        K.op(act, lambda: nc.scalar.activation(out=small[:, 12:20], in_=small[:, 12:20], func=AF.Exp, scale=-1.0),
             writes=[fcb])
        K.op(act, lambda: nc.scalar.activation(out=small[:, 20:28], in_=small[:, 12:20], func=AF.Ln, bias=1.0), writes=[fcb])
        sp_ap = small[:, 20:28]
        acc_ap = small[:, 28:36]
        accb = smb["acc"]
        b3, b3b = proj(1536, 512)
        K.op(act, lambda: nc.scalar.activation(out=q_sb, in_=b3[:, 0:512], func=AF.Copy), reads=[b3b], writes=[qsb_b])
        b4, b4b = proj(2048, 512)
        K.op(dve, lambda: nc.vector.tensor_copy(out=k_sb, in_=b4[:, 0:512]), reads=[b4b], writes=[ksb_b])
        tq, tqb = G()
        for hp in range(4):
            tr(bf(tq)[:, hp * P:(hp + 1) * P], q_sb[:, hp * P:(hp + 1) * P], [qsb_b], [tqb])
        for hp in range(4):
            tr(bf(tq)[:, (4 + hp) * P:(5 + hp) * P], k_sb[:, hp * P:(hp + 1) * P], [ksb_b], [tqb])
        K.op(act, lambda: nc.scalar.activation(out=QTa[0:64].rearrange("p a b -> p (a b)"), in_=bf(tq)[0:64, 0:512], func=AF.Copy),
             reads=[tqb], writes=[QTb])
        K.op(act, lambda: nc.scalar.activation(out=QTz[64:128].rearrange("p a b -> p (a b)"), in_=bf(tq)[64:128, 0:512], func=AF.Copy),
             reads=[tqb], writes=[QTb])
        K.op(act, lambda: nc.scalar.activation(out=KT[:, :, i * P:(i + 1) * P],
                                               in_=bf(tq)[:, 512:1024].rearrange("p (a b) -> p a b", b=P),
                                               func=AF.Copy), reads=[tqb], writes=[KTb[i]])

        bc_, bcb = G()
        mm(bc_[:, 0:8], ufull[:], sp_ap, True, False, [*CbL, fcb], [bcb])
        mm(bc_[:, 0:8], ones[:], acc_ap, False, True, [*CbL, accb], [bcb])
        K.op(pool, lambda: nc.gpsimd.tensor_tensor(out=acc_ap, in0=acc_ap, in1=sp_ap, op=ALU.add),
             reads=[fcb], writes=[accb])
        mm(bc_[:, 8:16], ones[:], acc_ap, True, True, [*CbL, accb], [bcb])
        totb = smb["tot"]
        wb = smb["w"]
        K.op(dve, lambda: nc.vector.tensor_copy(out=totall[:, i, :], in_=bc_[:, 8:16]), reads=[bcb], writes=[totb])
        K.op(dve, lambda: nc.vector.tensor_tensor(out=small[:, 52:60], in0=bc_[:, 0:8], in1=totall[:, i, :],
                                                  op=ALU.subtract), reads=[bcb, totb], writes=[wb])
        K.op(act, lambda: nc.scalar.activation(out=small[:, 52:60], in_=small[:, 52:60], func=AF.Exp), writes=[wb])
        K.op(pool, lambda: nc.gpsimd.tensor_tensor(
            out=fij[par][:, 0:i + 1, :], in0=totall[:, 0:i + 1, :],
            in1=totall[:, i:i + 1, :].to_broadcast([P, i + 1, 8]), op=ALU.subtract),
            reads=[totb], writes=[biasb[par]])
        K.op(act, lambda: nc.scalar.activation(out=fij[par][:, 0:i + 1, :], in_=fij[par][:, 0:i + 1, :], func=AF.Exp),
             writes=[biasb[par]])
        b5, b5b = proj(2560, 512)
        K.op(dve, lambda: nc.vector.tensor_tensor(
            out=Vv[:, i, :, 0:64], in0=b5[:, 0:512].rearrange("p (h d) -> p h d", d=64),
            in1=small[:, 52:60].unsqueeze(2).to_broadcast([P, 8, 64]), op=ALU.mult),
            reads=[b5b, smb["w"]], writes=[Vb[i]])
        K.op(pool, lambda: nc.gpsimd.tensor_copy(out=Vv[:, i, :, 64], in_=small[:, 52:60]),
             reads=[smb["w"]], writes=[Vb[i]])
        b6, b6b = proj(3072, 512)
        silu_gate(b6[:, 0:256], 256, gate_c[:, 0:256], gcb, psb=b6b)
        silu_gate(b6[:, 256:512], 256, gate_c[:, 256:512], gcb, psb=b6b)
        gstate[1] = [0, 1, 2]
        b0, b0b = proj(0, 512)
        Fq, Fqb = Fs[0], Fb[0]
        Fk, Fkb = Fs[1], Fb[1]
        Fl, Flb = Fs[2], Fb[2]
        Fx, Fxb = Fs[3], Fb[3]
        Fy, Fyb = Fs[4], Fb[4]
        b1, b1b = proj(512, 512)
        b2, b2b = proj(1024, 512)
        if prefetch is not None:
            load_win(prefetch)
        def ab_thread():
            K.op(act, lambda: nc.scalar.activation(out=Fy[:], in_=b0[:, 256:512], func=AF.Exp, scale=-1.0),
                 reads=[b0b], writes=[Fyb])
            yield
            K.op(act, lambda: nc.scalar.activation(out=Fy[:], in_=Fy[:], func=AF.Ln, bias=1.0), writes=[Fyb])
            yield
            if lb_modes[p] == 0:
                K.op(dve, lambda: nc.vector.tensor_scalar(out=Fl[:], in0=Fy[:], scalar1=-1.0, scalar2=float(np.log(TINY)),
                                                          op0=ALU.mult, op1=ALU.max), reads=[Fyb], writes=[Flb])
                yield
                K.op(act, lambda: nc.scalar.activation(out=Fy[:], in_=Fy[:], func=AF.Exp, scale=-1.0), writes=[Fyb])
                yield
                K.op(pool, lambda: nc.gpsimd.tensor_scalar(out=Fk[:], in0=Fy[:], scalar1=-1.0, scalar2=1.0,
                                                           op0=ALU.mult, op1=ALU.add), reads=[Fyb], writes=[Fkb])
                yield
            else:
                K.op(act, lambda: nc.scalar.activation(out=Fy[:], in_=Fy[:], func=AF.Exp, scale=-1.0), writes=[Fyb])
                yield
                K.op(pool, lambda: nc.gpsimd.tensor_tensor(out=Fy[:], in0=Fy[:], in1=oml_bc[:], op=ALU.mult),
                     reads=[Llb], writes=[Fyb])
                yield
                K.op(pool, lambda: nc.gpsimd.tensor_tensor(out=Fk[:], in0=oml_bc[:], in1=Fy[:], op=ALU.subtract),
                     reads=[Llb, Fyb], writes=[Fkb])
                yield
                K.op(dve, lambda: nc.vector.scalar_tensor_tensor(out=Fy[:], in0=Fy[:], scalar=TINY, in1=lb_bc[:],
                                                                 op0=ALU.max, op1=ALU.add), reads=[Llb], writes=[Fyb])
                yield
            if lb_modes[p] != 0:
                K.op(act, lambda: nc.scalar.activation(out=Fl[:], in_=Fy[:], func=AF.Ln), reads=[Fyb], writes=[Flb])

            yield
            K.op(act, lambda: nc.scalar.activation(out=Fx[:], in_=b0[:, 0:256], func=AF.Exp, scale=-1.0),
                 reads=[b0b], writes=[Fxb])
            yield
            K.op(act, lambda: nc.scalar.activation(out=Fx[:], in_=Fx[:], func=AF.Ln, bias=1.0), writes=[Fxb])
            yield
            K.op(act, lambda: nc.scalar.activation(out=Fx[:], in_=Fx[:], func=AF.Exp, scale=-1.0), writes=[Fxb])
            yield
            K.op(dve, lambda: nc.vector.tensor_tensor(out=Fq[:], in0=b0[:, 0:256], in1=Fx[:], op=ALU.mult),
                 reads=[b0b, Fxb], writes=[Fqb])
            yield
            K.op(act, lambda: nc.scalar.activation(out=va[:], in_=b1[:, 0:256], func=AF.Copy), reads=[b1b], writes=[vab])
            yield
            silu_gate(b1[:, 256:512], 256, gate_a[:], gab, mul_ap=hgrng[:], psb=b1b)
            yield
            K.op(act, lambda: nc.scalar.activation(out=u_sb[par][:], in_=b2[:, 0:256], func=AF.Copy),
                 reads=[b2b], writes=[ub[par]])
            yield
            silu_gate(b2[:, 256:512], 256, gate_b[:], gbb, mul_ap=pscale[:], psb=b2b)

            yield
            be, beb = G()
            mm(be[:, 0:256], ucum[:], Fl[:], True, True, [*CbL, Flb], [beb])
            yield
            mm(be[:, 256:512], lstr[:], Fl[:], True, True, [*CbL, Flb], [beb])
            yield
            bd, bdb = G()
            for hp in range(2):
                mm(bd[:, hp * 2:hp * 2 + 2], Fl[:, hp * P:(hp + 1) * P], ind[:], True, True, [*CbL, Flb], [bdb])
            yield
            K.op(act, lambda: nc.scalar.activation(out=dec[:], in_=bd[:, 0:4], func=AF.Exp), reads=[bdb], writes=[decb])
            yield
            K.op(act, lambda: nc.scalar.activation(out=Fx[:], in_=be[:, 0:256], func=AF.Exp), reads=[beb], writes=[Fxb])
            yield
            K.op(dve, lambda: nc.vector.tensor_tensor(out=qp[:], in0=Fq[:], in1=Fx[:], op=ALU.mult),
                 reads=[Fqb, Fxb], writes=[qpb])
            yield
            K.op(act, lambda: nc.scalar.activation(out=Fy[:], in_=be[:, 256:512], func=AF.Exp, scale=-1.0),
                 reads=[beb], writes=[Fyb])
            yield
            K.op(dve, lambda: nc.vector.tensor_tensor(out=qpp[:], in0=Fq[:], in1=Fy[:], op=ALU.mult),
                 reads=[Fqb, Fyb], writes=[qppb])
            yield
            K.op(act, lambda: nc.scalar.activation(out=Fx[:], in_=be[:, 256:512], func=AF.Exp), reads=[beb], writes=[Fxb])
            yield
            K.op(dve, lambda: nc.vector.tensor_tensor(out=kp[:], in0=Fk[:], in1=Fx[:], op=ALU.mult),
                 reads=[Fkb, Fxb], writes=[kpb])
            yield
            tt_, ttb = G()
            for n_, (src, srcb) in enumerate(((qp, qpb), (qpp, qppb), (kp, kpb))):
                for hp in range(2):
                    tr(bf(tt_)[:, (2 * n_ + hp) * P:(2 * n_ + hp + 1) * P], src[:, hp * P:(hp + 1) * P], [srcb], [ttb])
            yield
            K.op(act, lambda: nc.scalar.activation(out=TT[:].rearrange("p a b -> p (a b)"), in_=bf(tt_)[:, 0:768], func=AF.Copy),
                 reads=[ttb], writes=[TTb])
            yield
            bp, bpb = G()
            for g in range(4):
                ro = (g % 2) * 64
                o_ap = bp[ro:ro + 64, (g // 2) * P:(g // 2 + 1) * P]
                if i == 0:
                    mm(o_ap, u_sb[par][:, g * 64:(g + 1) * 64], pm_cur[:, g * P:(g + 1) * P], True, True,
                       [ub[par], *CbL], [bpb])
                    mm(o_ap[:, 0:16], u_sb[par][:, g * 64:(g + 1) * 64], pm_cur0[:, g * 16:(g + 1) * 16], True, True,
                       [ub[par], *CbL], [bpb])
                else:
                    mm(o_ap, u_sb[par][:, g * 64:(g + 1) * 64], pm_cur[:, g * P:(g + 1) * P], True, False,
                       [ub[par], *CbL], [bpb])
                    mm(o_ap[:, 0:16], u_sb[1 - par][:, g * 64:(g + 1) * 64], pm_prev[:, g * 16:(g + 1) * 16], False, True,
                       [ub[1 - par], *CbL], [bpb])
            yield
            K.op(act, lambda: nc.scalar.activation(out=qpp[:], in_=bp[:, 0:256], func=AF.Copy),
                 reads=[bpb], writes=[plTb])
            yield
            bo, bob = G()
            for g in (0, 2, 1, 3):
                ro = (g % 2) * 64
                mm(bo[:, 256 + g * 64:256 + (g + 1) * 64], plT[ro:ro + 64, g // 2, :], pw[ro:ro + 64, (g // 2) * 64:(g // 2 + 1) * 64],
                   True, True, [plTb, Lpw], [bob])
            yield
            ba, bab = G()
            for h in (0, 2, 1, 3):
                hp, r = h // 2, (h % 2) * 64
                for c in range(2):
                    cs = slice(c * 64, (c + 1) * 64)
                    mm(ba[cs, h * 64:(h + 1) * 64], TT[r:r + 64, 4 + hp, cs], TT[r:r + 64, 2 + hp, cs], True, True,
                       [TTb], [bab])
            yield
            for c in range(2):
                cs = slice(c * 64, (c + 1) * 64)
                for h in range(4):
                    hp, r = h // 2, (h % 2) * 64
                    col = 256 + (c * 2 + hp) * 64
                    mm(ba[r:r + 64, col:col + 64], kp[cs, h * 64:(h + 1) * 64], va[cs, h * 64:(h + 1) * 64], True, True,
                       [kpb, vab], [bab])
            yield
            K.op(dve, lambda: nc.vector.tensor_tensor(out=attn_sb[:], in0=ba[:, 0:256], in1=hmask[:], op=ALU.mult),
                 reads=[bab, *CbL], writes=[attnb])
            yield
            for c in range(2):
                for hp in range(2):
                    col = 256 + (c * 2 + hp) * 64
                    K.op(dve, lambda hp=hp, col=col, c=c: nc.vector.scalar_tensor_tensor(
                        out=S[:, hp, :], in0=S[:, hp, :], scalar=dec[:, hp * 2 + c:hp * 2 + c + 1], in1=ba[:, col:col + 64],
                        op0=ALU.mult, op1=ALU.add), reads=[bab, decb], writes=[Sb])
                tgt = 1 if c == 0 else 0
                if c == 0:
                    K.op(dve, lambda: nc.vector.tensor_copy(out=S_bf[1][:], in_=S[:]), reads=[Sb], writes=[S_bfb[1]])
            yield
            for c in range(2):
                cs = slice(c * 64, (c + 1) * 64)
                for h in range(4):
                    mm(bo[cs, h * 64:(h + 1) * 64], attn_sb[cs, h * 64:(h + 1) * 64], va[cs, h * 64:(h + 1) * 64],
                       True, True, [attnb, vab], [bob])
            yield
            bi, bib = G()
            for h in (0, 2, 1, 3):
                hp, r = h // 2, (h % 2) * 64
                for c in range(2):
                    cs = slice(c * 64, (c + 1) * 64)
                    mm(bi[cs, h * 64:(h + 1) * 64], TT[r:r + 64, hp, cs], S_bf[c][r:r + 64, hp, :],
                       True, True, [TTb, S_bfb[c]], [bib])
            yield
            K.op(dve, lambda: nc.vector.tensor_copy(out=S_bf[0][:], in_=S[:]), reads=[Sb], writes=[S_bfb[0]])
            yield
            hgb = smb["hg"]
            K.op(act, lambda: nc.scalar.activation(out=Fx[:], in_=bo[:, 0:256], func=AF.Copy), reads=[bob], writes=[Fxb])
            yield
            K.op(dve, lambda: nc.vector.tensor_tensor(out=Fx[:], in0=Fx[:], in1=bi[:, 0:256], op=ALU.add),
                 reads=[bib], writes=[Fxb])
            yield
            K.op(dve, lambda: nc.vector.tensor_tensor(out=Fy[:], in0=Fx[:], in1=Fx[:], op=ALU.mult), reads=[Fxb], writes=[Fyb])
            yield
            K.op(dve, lambda: nc.vector.reduce_sum(out=small[:, 4:8], in_=Fy[:].rearrange("p (h d) -> p h d", d=64), axis=AX.X),
                 reads=[Fyb], writes=[hgb])
            yield
            rms_rstd(small[:, 4:8], 64.0, small[:, 8:12], hgb)
            yield
            for h in range(4):
                K.op(dve, lambda h=h: nc.vector.scalar_tensor_tensor(
                    out=XYf[1 - par][:, h * 64:(h + 1) * 64], in0=Fx[:, h * 64:(h + 1) * 64], scalar=small[:, 8 + h:9 + h],
                    in1=gate_a[:, h * 64:(h + 1) * 64], op0=ALU.mult, op1=ALU.mult), reads=[Fxb, hgb, gab], writes=[XYb[1 - par]])
            yield
            K.op(dve, lambda: nc.vector.tensor_tensor(out=XYf[1 - par][:, 256:512], in0=bo[:, 256:512], in1=gate_b[:], op=ALU.mult),
                 reads=[bob, gbb], writes=[XYb[1 - par]])


            yield

        groups = []
        for h in range(8):
            for j0 in range(0, i + 1, 4):
                groups.append([(h, j) for j in range(j0, min(j0 + 4, i + 1))])
        rdb = smb["rden"]

        def emit_S(gi):
            stt, stb = ST[gi % NST]
            ptb = PTb[gi % NPT]
            pt = PT[gi % NPT]
            for sl, (h, j) in enumerate(groups[gi]):
                hp, r = h // 2, (h % 2) * 64
                mm(stt[:, sl * P:(sl + 1) * P], KT[:, hp, j * P:(j + 1) * P], (QTa if r == 0 else QTz)[:, hp, :], True, True,
                   [KTb[j], QTb], [stb])
            ng = len(groups[gi])
            K.op(act, lambda: nc.scalar.activation(out=pt[:, 0:ng * P], in_=stt[:, 0:ng * P], func=AF.Exp, scale=0.125),
                 reads=[stb], writes=[ptb])
            gh, gj0 = groups[gi][0]
            vt, vtb = Vt[gi % NVT], Vtb[gi % NVT]
            K.op(dve, lambda: nc.vector.tensor_tensor(
                out=vt[:, 0:ng * 65].rearrange("p (a b) -> p a b", b=65), in0=V[:, gj0:gj0 + ng, gh * 65:(gh + 1) * 65],
                in1=fij[par][:, gj0:gj0 + ng, gh:gh + 1].to_broadcast([P, ng, 65]), op=ALU.mult),
                reads=[biasb[par]] + [Vb[j] for j in range(gj0, gj0 + ng)], writes=[vtb])
            for sl, (h, j) in enumerate(groups[gi]):
                if j == i:
                    K.op(dve, lambda sl=sl: nc.vector.tensor_tensor(
                        out=pt[:, sl * P:(sl + 1) * P], in0=pt[:, sl * P:(sl + 1) * P], in1=fmask[:], op=ALU.mult),
                        reads=[*CbL], writes=[ptb])

        def emit_PV(gi):
            pt, ptb = PT[gi % NPT], PTb[gi % NPT]
            for sl, (h, j) in enumerate(groups[gi]):
                oa, oab = OA[h // 4]
                c0 = (h % 4) * 65
                mm(oa[:, c0:c0 + 65], pt[:, sl * P:(sl + 1) * P], Vt[gi % NVT][:, sl * 65:(sl + 1) * 65], j == 0, j == i,
                   [ptb, Vtb[gi % NVT]], [oab], inc=True)
                if j == i and h % 4 == 3:
                    hb = h - 3
                    K.op(dve, lambda oa=oa: nc.vector.reciprocal(
                        out=small[:, 44:48], in_=oa[:, 0:260].rearrange("p (h e) -> p h e", e=65)[:, :, 64]),
                        reads=[oab], writes=[rdb])
                    for hh in range(4):
                        K.op(dve, lambda hh=hh, oa=oa, hb=hb: nc.vector.scalar_tensor_tensor(
                            out=XYf[1 - par][:, 512 + (hb + hh) * 64:512 + (hb + hh + 1) * 64], in0=oa[:, hh * 65:hh * 65 + 64],
                            scalar=small[:, 44 + hh:45 + hh], in1=gate_c[:, (hb + hh) * 64:(hb + hh + 1) * 64],
                            op0=ALU.mult, op1=ALU.mult), reads=[oab, rdb, gcb], writes=[XYb[1 - par]])

        gen = ab_thread()
        steps_per_group = max(1, -(-12 // len(groups)))
        if next_front is not None:
            if next_front[2] != p:
                load_cast(preg[:], ld[next_front[2]]["preg_bc"][:, :], Lpre)
            front(*next_front)
        gstate[1] = [0, 1, 2]
        emit_S(0)
        if len(groups) > 1:
            emit_S(1)
        for gi in range(len(groups)):
            if gi + 2 < len(groups):
                emit_S(gi + 2)
            emit_PV(gi)
            for _ in range(steps_per_group):
                next(gen, None)
        for _ in gen:
            pass
        gstate[1] = [0, 1, 2, 3, 4, 5]

        tm, tmb = G()
        for ec in range(8):
            tr(bf(tm)[:, ec * P:(ec + 1) * P], XYf[1 - par][:, ec * P:(ec + 1) * P], [XYb[1 - par]], [tmb])
        K.op(act, lambda: nc.scalar.activation(out=XYf[par], in_=bf(tm)[:, 0:D], func=AF.Copy),
             reads=[tmb], writes=[XYb[par]])
        if next_front is not None:
            front_b(1 - par)
        ys = []
        for half in range(2):
            y_, yb_ = G()
            for ec in range(8):
                mm(y_[:, 0:512], XYc[par][:, ec, :], Wout[:, ec, half * 512:(half + 1) * 512], ec == 0, ec == 7,
                   [XYb[par], Wob], [yb_])
            ys.append((y_, yb_))
        if prefetch is not None:
            load_wout(prefetch)
        n2 = smb["n2"]
        for half in range(2):
            y_, yb_ = ys[half]
            K.op(act, lambda y_=y_, half=half: nc.scalar.activation(
                out=hh[:, half * 512:(half + 1) * 512], in_=y_[:, 0:512], func=AF.Square,
                accum_out=small[:, 48 + half:49 + half]), reads=[yb_], writes=[hhb, n2])
        K.op(dve, lambda: nc.vector.tensor_tensor(out=small[:, 50:51], in0=small[:, 48:49], in1=small[:, 49:50], op=ALU.add),
             writes=[n2])
        rms_rstd(small[:, 50:51], float(D), small[:, 51:52], n2)
        for q4 in range(4):
            y_, yb_ = ys[q4 // 2]
            c0 = (q4 % 2) * 256
            K.op(dve, lambda y_=y_, q4=q4, c0=c0: nc.vector.scalar_tensor_tensor(
                out=Fs[q4][:], in0=y_[:, c0:c0 + 256], scalar=small[:, 51:52], in1=postg[:, q4 * 256:(q4 + 1) * 256],
                op0=ALU.mult, op1=ALU.mult), reads=[yb_, n2, Lpost], writes=[Fb[q4]])
            K.op(pool, lambda q4=q4: nc.gpsimd.tensor_tensor(
                out=x_seq[:, i, q4 * 256:(q4 + 1) * 256], in0=x_seq[:, i, q4 * 256:(q4 + 1) * 256],
                in1=Fs[q4][:], op=ALU.add), reads=[Fb[q4]], writes=[Xb[i]])
        finish()

    if 1 in lb_modes:
        compute_lb1(list(lb_modes).index(1))

    x_preloaded = set()
    for s in range(NSEQ):
        for i in range(NB):
            if (s, i) not in x_preloaded:
                load_f32(x_seq[:, i, :], x_d[s, i * P:(i + 1) * P, :], Xb[i])
        for p in range(NL):
            first = (s == 0 and p == 0)
            load_layer(p, with_weights=first, with_preg=first)
            nxt = None
            if p + 1 < NL:
                nxt = p + 1
            elif s + 1 < NSEQ:
                nxt = 0
            K.op(pool, lambda: nc.gpsimd.memset(S[:], 0.0), writes=[Sb])
            K.op(pool, lambda: nc.gpsimd.memset(S_bf[0][:], 0.0), writes=[S_bfb[0]])
            K.op(pool, lambda: nc.gpsimd.memset(small[:, 28:36], 0.0), writes=[smb["acc"]])
            for i in range(NB):
                if i == NB - 1 and p == NL - 1 and s + 1 < NSEQ:
                    for i2 in range(NB - 1):
                        load_f32(x_seq[:, i2, :], x_d[s + 1, i2 * P:(i2 + 1) * P, :], Xb[i2])
                        x_preloaded.add((s + 1, i2))
                bg = (s == 0 and p + 1 < NL and NB >= 2)
                if bg and i == NB - 1:
                    wsc_ready[p + 1] = True
                if i + 1 < NB:
                    nf = (s, i + 1, p)
                elif nxt is not None:
                    nf = (s, 0, p + 1) if p + 1 < NL else (s + 1, 0, 0)
                else:
                    nf = None
                block(s, i, p, p == NL - 1, do_front=(first and i == 0), next_front=nf,
                      prefetch=(nxt if i == NB - 1 else None))
                if bg and i < NB - 1:
                    q = p + 1
                    per = -(-16 // (NB - 1))
                    for pc in range(i * per, min(16, (i + 1) * per)):
                        if pc < 8:
                            K.dma(K.q_pool, lambda pc=pc, q=q: nc.gpsimd.dma_start(
                                out=wsc_in[q][:, pc, :], in_=ld[q]["w_in"][pc * P:(pc + 1) * P, :]), writes=[wsc_ib[q][pc]])
                        else:
                            K.dma(K.q_pool, lambda pc=pc, q=q: nc.gpsimd.dma_start(
                                out=wsc_out[q][:, pc - 8, :], in_=ld[q]["w_out"][(pc - 8) * P:(pc - 7) * P, :]),
                                writes=[wsc_ob[q][pc - 8]])
    sp.wait_all([(l[0], l[1]) for l in K.q_sp.lanes if l[1] > 0])
    return nc


def _layer_inputs(l, lower_bounds, pre_norm_g, w_in, hgrn_norm_g, fox_f_bias, pool_w, pool_scale, w_out, post_norm_g):
    f = np.float32
    bc = lambda v: np.ascontiguousarray(np.broadcast_to(np.asarray(v, f)[None, :], (P, v.shape[0])))
    pwl = np.zeros((P, 128), f)
    for g in range(4):
        pwl[(g % 2) * 64:(g % 2) * 64 + 64, (g // 2) * 64:(g // 2 + 1) * 64] = pool_w[l, g]
    return {
        "w_in": np.ascontiguousarray(w_in[l], f), "w_out": np.ascontiguousarray(w_out[l], f),
        "preg_bc": bc(pre_norm_g[l]), "postg_bc": bc(post_norm_g[l]),
        "hgrng_bc": bc(hgrn_norm_g[l]), "pscale_bc": bc(pool_scale[l]),
        "lbraw_bc": bc(np.concatenate([lower_bounds[0], lower_bounds[1]])),
        "fbias_bc": bc(fox_f_bias[l]), "pw": pwl,
    }


FUSED = True
_cache = {}


def run(x, params, T, NSEQ, ncores, layer_groups):
    consts = _consts()
    cur = np.ascontiguousarray(x, np.float32)
    for grp in layer_groups:
        key = (T, NSEQ, tuple(grp))
        if key not in _cache:
            _cache[key] = build_program(T, NSEQ, [0 if l == 0 else 1 for l in grp])
        nc = _cache[key]
        base = {"c_" + k: v for k, v in consts.items()}
        for p, l in enumerate(grp):
            for k, v in _layer_inputs(l, **params).items():
                base["l%d_%s" % (p, k)] = v
        in_maps = []
        for c in range(ncores):
            m = dict(base)
            m["x"] = np.ascontiguousarray(cur[c * NSEQ:(c + 1) * NSEQ])
            in_maps.append(m)
        res = run_bass_kernel_spmd(nc, in_maps, core_ids=list(range(ncores)))
        cur = np.concatenate([np.asarray(r["out"], np.float32) for r in res.results], axis=0)
    return cur


def kernel(x, lower_bounds, pre_norm_g, w_in, hgrn_norm_g, fox_f_bias, pool_w, pool_scale, w_out, post_norm_g):
    params = dict(lower_bounds=np.asarray(lower_bounds), pre_norm_g=np.asarray(pre_norm_g), w_in=np.asarray(w_in),
                  hgrn_norm_g=np.asarray(hgrn_norm_g), fox_f_bias=np.asarray(fox_f_bias), pool_w=np.asarray(pool_w),
                  pool_scale=np.asarray(pool_scale), w_out=np.asarray(w_out), post_norm_g=np.asarray(post_norm_g))
    x = np.asarray(x)
    B, T, _ = x.shape
    groups = [[0, 1]] if FUSED else [[0], [1]]
    return run(x, params, T, B // 8, 8, groups).astype(np.float32)
```
